# Optimizing a Trainium2 kernel written in Bass

```python
import math
import jax, jax.numpy as jnp
from jax import lax
import numpy as np

D_MODEL = 2048
BATCH = 4
SEQ = 4096
DEPTH = 2

HEAD_DIM = 128
H_A = 4
H_B = 6
H_C = 6
N_HEADS = H_A + H_B + H_C
H_SOFT = H_B + H_C
MIX_WIDTH = N_HEADS * HEAD_DIM
N_BRANCH = 3
D_FF = -(-(8 * D_MODEL) // (3 * 256)) * 256
Q_BLOCK = 128
MOBA_BLOCK = 256
MOBA_TOPK = 3
MOBA_Q_CHUNK = 32
DILATIONS = ((128, 1), (512, 4), (2048, 16))
BAND_BLOCK = 128
N_BUCKETS = 32
MAX_DISTANCE = 2048
RMS_EPS = 1e-6
NEG_INF = -1e30

kernel_name = 'hybrid_sb_moba_dilated_block'


def rmsnorm(x, g):
    xf = x.astype(jnp.float32)
    y = xf * lax.rsqrt(jnp.mean(xf * xf, axis=-1, keepdims=True) + RMS_EPS)
    return (y * g.astype(jnp.float32)).astype(x.dtype)


def t5_bucket(dist):
    max_exact = N_BUCKETS // 2
    d = jnp.maximum(dist, 0)
    df = jnp.maximum(d, 1).astype(jnp.float32)
    large = max_exact + (jnp.log(df / max_exact) / math.log(MAX_DISTANCE / max_exact)
                         * (N_BUCKETS - max_exact)).astype(jnp.int32)
    large = jnp.minimum(large, N_BUCKETS - 1)
    return jnp.where(d < max_exact, d, large)


def stick_breaking_attention(q, k, v):
    B, H, S, hd = q.shape
    scale = hd ** -0.5
    kpos = jnp.arange(S)

    def block(i):
        t0 = i * Q_BLOCK
        qb = lax.dynamic_slice_in_dim(q, t0, Q_BLOCK, axis=2)
        z = jnp.einsum('bhqd,bhkd->bhqk', qb, k).astype(jnp.float32) * scale
        qpos = t0 + jnp.arange(Q_BLOCK)
        past = kpos[None, :] < qpos[:, None]
        log_beta = jax.nn.log_sigmoid(z)
        log_1mb = jnp.where(past, jax.nn.log_sigmoid(-z), 0.0)
        after = lax.cumsum(log_1mb, axis=3, reverse=True) - log_1mb
        w = jnp.where(past, jnp.exp(log_beta + after), 0.0)
        return jnp.einsum('bhqk,bhkd->bhqd', w.astype(v.dtype), v)

    out = lax.map(block, jnp.arange(S // Q_BLOCK))
    return jnp.moveaxis(out, 0, 2).reshape(B, H, S, hd)


def moba_attention(q, k, v, bias_table):
    B, H, S, hd = q.shape
    nb = -(-S // MOBA_BLOCK)
    s_pad = nb * MOBA_BLOCK
    pad = ((0, 0), (0, 0), (0, s_pad - S), (0, 0))
    kp, vp = jnp.pad(k, pad), jnp.pad(v, pad)
    kblk = kp.reshape(B, H, nb, MOBA_BLOCK, hd)
    vblk = vp.reshape(B, H, nb, MOBA_BLOCK, hd)
    scale = hd ** -0.5
    bias_hb = bias_table.T.astype(jnp.float32)
    n_sel = min(MOBA_TOPK, nb - 1)
    own_blk = jnp.arange(S) // MOBA_BLOCK
    if n_sel > 0:
        kmean = jnp.mean(kblk, axis=3)
        gate = jnp.einsum('bhsd,bhnd->bhsn', q, kmean).astype(jnp.float32)
        fully_past = jnp.arange(nb)[None, :] < own_blk[:, None]
        gate = jnp.where(fully_past, gate, NEG_INF)
        _, sel = lax.top_k(gate, n_sel)
        sel_ok = sel < own_blk[:, None]
    b_idx = jnp.arange(B)[:, None, None, None]
    h_idx = jnp.arange(H)[None, :, None, None]
    in_blk = jnp.arange(MOBA_BLOCK)

    def chunk(i):
        t0 = i * MOBA_Q_CHUNK
        qc = lax.dynamic_slice_in_dim(q, t0, MOBA_Q_CHUNK, axis=2)
        tq = t0 + jnp.arange(MOBA_Q_CHUNK)
        b0 = (t0 // MOBA_BLOCK) * MOBA_BLOCK
        k_own = lax.dynamic_slice_in_dim(kp, b0, MOBA_BLOCK, axis=2)
        v_own = lax.dynamic_slice_in_dim(vp, b0, MOBA_BLOCK, axis=2)
        dist = tq[:, None] - (b0 + in_blk)[None, :]
        l_own = jnp.einsum('bhqd,bhkd->bhqk', qc, k_own).astype(jnp.float32) * scale
        l_own = jnp.where(dist >= 0, l_own + bias_hb[:, t5_bucket(dist)], NEG_INF)
        if n_sel == 0:
            p = jax.nn.softmax(l_own, axis=-1).astype(v.dtype)
            return jnp.einsum('bhqk,bhkd->bhqd', p, v_own)
        sc = lax.dynamic_slice_in_dim(sel, t0, MOBA_Q_CHUNK, axis=2)
        ok = lax.dynamic_slice_in_dim(sel_ok, t0, MOBA_Q_CHUNK, axis=2)
        k_sel = kblk[b_idx, h_idx, sc]
        v_sel = vblk[b_idx, h_idx, sc]
        dist_sel = tq[:, None, None] - (sc[..., None] * MOBA_BLOCK + in_blk)
        l_sel = jnp.einsum('bhqd,bhqnkd->bhqnk', qc, k_sel).astype(jnp.float32) * scale
        l_sel = l_sel + bias_hb[h_idx[..., None], t5_bucket(dist_sel)]
        n_past = n_sel * MOBA_BLOCK
        l_sel = jnp.where(ok[..., None], l_sel, NEG_INF).reshape(B, H, MOBA_Q_CHUNK, n_past)
        p = jax.nn.softmax(jnp.concatenate([l_sel, l_own], axis=-1), axis=-1).astype(v.dtype)
        o = jnp.einsum('bhqk,bhqkd->bhqd', p[..., :n_past],
                       v_sel.reshape(B, H, MOBA_Q_CHUNK, n_past, hd))
        return o + jnp.einsum('bhqk,bhkd->bhqd', p[..., n_past:], v_own)

    out = lax.map(chunk, jnp.arange(S // MOBA_Q_CHUNK))
    return jnp.moveaxis(out, 0, 2).reshape(B, H, S, hd)


def dilated_attention(q, k, v, bias_table):
    B, H, S, hd = q.shape
    scale = hd ** -0.5
    bb = BAND_BLOCK
    qi = jnp.arange(bb)[:, None]
    kj = jnp.arange(2 * bb)[None, :]
    delta = qi + bb - kj
    outs, lses = [], []
    for window, r in DILATIONS:
        span = window // r
        L = S // r
        nb = -(-L // bb)
        Lp = nb * bb

        def to_sub(t):
            t = t.reshape(B, H, L, r, hd).swapaxes(2, 3)
            return jnp.pad(t, ((0, 0), (0, 0), (0, 0), (0, Lp - L), (0, 0)))

        def band(t):
            t = jnp.pad(to_sub(t), ((0, 0), (0, 0), (0, 0), (bb, 0), (0, 0))).reshape(B, H, r, nb + 1, bb, hd)
            return jnp.concatenate([t[:, :, :, :-1], t[:, :, :, 1:]], axis=4)

        qs = to_sub(q).reshape(B, H, r, nb, bb, hd)
        kb, vb = band(k), band(v)
        key_idx = (jnp.arange(nb)[:, None, None] - 1) * bb + kj[None]
        mask = (delta >= 0) & (delta <= span) & (key_idx >= 0)
        bias = jnp.transpose(bias_table[t5_bucket(delta * r)], (2, 0, 1)).astype(jnp.float32)
        logits = jnp.einsum('bhrnqd,bhrnkd->bhrnqk', qs, kb).astype(jnp.float32) * scale
        logits = jnp.where(mask, logits + bias[:, None, None], NEG_INF)
        m = jnp.max(logits, axis=-1, keepdims=True)
        p = jnp.exp(logits - m)
        den = jnp.sum(p, axis=-1, keepdims=True)
        o = jnp.einsum('bhrnqk,bhrnkd->bhrnqd', (p / den).astype(v.dtype), vb)
        lse = (m + jnp.log(den))[..., 0]

        def from_sub(t):
            t = t.reshape((B, H, r, Lp) + t.shape[5:])[:, :, :, :L]
            return jnp.swapaxes(t, 2, 3).reshape((B, H, S) + t.shape[4:])

        outs.append(from_sub(o))
        lses.append(from_sub(lse))
    w = jax.nn.softmax(jnp.stack(lses, axis=0), axis=0)
    return jnp.einsum('gbhs,gbhsd->bhsd', w.astype(v.dtype), jnp.stack(outs, axis=0))


def hybrid_layer(x, g_mix, w_in, q_gain, k_gain, w_branch, w_out, g_ffn, w_gu, w_down, rel_bias):
    B, S, _ = x.shape
    h = rmsnorm(x, g_mix)
    proj = h @ w_in
    qkv = proj[..., :3 * MIX_WIDTH].reshape(B, S, 3, N_HEADS, HEAD_DIM)
    qkv = jnp.transpose(qkv, (2, 0, 3, 1, 4))
    q, k, v = qkv[0], qkv[1], qkv[2]
    gates = jax.nn.sigmoid(proj[..., 3 * MIX_WIDTH:].astype(jnp.float32)).astype(x.dtype)
    gates = gates.reshape(B, S, N_BRANCH, D_MODEL)

    o_a = stick_breaking_attention(q[:, :H_A], k[:, :H_A], v[:, :H_A])
    qn = rmsnorm(q[:, H_A:], q_gain[:, None, :])
    kn = rmsnorm(k[:, H_A:], k_gain[:, None, :])
    o_b = moba_attention(qn[:, :H_B], kn[:, :H_B], v[:, H_A:H_A + H_B], rel_bias[:, :H_B])
    o_c = dilated_attention(qn[:, H_B:], kn[:, H_B:], v[:, H_A + H_B:], rel_bias[:, H_B:])

    def flat(o):
        return jnp.transpose(o, (0, 2, 1, 3)).reshape(B, S, -1)

    ea, eb = H_A * HEAD_DIM, (H_A + H_B) * HEAD_DIM
    merged = (gates[:, :, 0] * (flat(o_a) @ w_branch[:ea])
              + gates[:, :, 1] * (flat(o_b) @ w_branch[ea:eb])
              + gates[:, :, 2] * (flat(o_c) @ w_branch[eb:]))
    x = x + merged @ w_out

    h2 = rmsnorm(x, g_ffn)
    gu = h2 @ w_gu
    x = x + (jax.nn.silu(gu[..., :D_FF]) * gu[..., D_FF:]) @ w_down
    return x


def setup_inputs(seed: int = 0) -> dict:
    key = jax.random.key(seed)
    ks = jax.random.split(key, 11)

    def nrm(k, shape, s):
        return jax.random.normal(k, shape, jnp.float32) * s

    return {
        'x': nrm(ks[0], (BATCH, SEQ, D_MODEL), 1.0),
        'g_mix': 1.0 + nrm(ks[1], (DEPTH, D_MODEL), 0.02),
        'w_in': nrm(ks[2], (DEPTH, D_MODEL, 3 * MIX_WIDTH + N_BRANCH * D_MODEL), D_MODEL ** -0.5),
        'q_gain': 1.0 + nrm(ks[3], (DEPTH, H_SOFT, HEAD_DIM), 0.02),
        'k_gain': 1.0 + nrm(ks[4], (DEPTH, H_SOFT, HEAD_DIM), 0.02),
        'w_branch': nrm(ks[5], (DEPTH, MIX_WIDTH, D_MODEL), (MIX_WIDTH / N_BRANCH) ** -0.5),
        'w_out': nrm(ks[6], (DEPTH, D_MODEL, D_MODEL), D_MODEL ** -0.5),
        'g_ffn': 1.0 + nrm(ks[7], (DEPTH, D_MODEL), 0.02),
        'w_gu': nrm(ks[8], (DEPTH, D_MODEL, 2 * D_FF), D_MODEL ** -0.5),
        'w_down': nrm(ks[9], (DEPTH, D_FF, D_MODEL), D_FF ** -0.5),
        'rel_bias': nrm(ks[10], (N_BUCKETS, H_SOFT), 0.5),
    }


def reference(x, g_mix, w_in, q_gain, k_gain, w_branch, w_out, g_ffn, w_gu, w_down, rel_bias):
    for l in range(DEPTH):
        x = hybrid_layer(x, g_mix[l], w_in[l], q_gain[l], k_gain[l], w_branch[l], w_out[l],
                         g_ffn[l], w_gu[l], w_down[l], rel_bias)
    return x
```

```python
import contextlib
import math
import numpy as np
import concourse.bass as bass
import concourse.mybir as mybir
from concourse.bass_utils import run_bass_kernel_spmd

F32 = mybir.dt.float32
BF16 = mybir.dt.bfloat16
ALU = mybir.AluOpType
AF = mybir.ActivationFunctionType
AX = mybir.AxisListType

D_MODEL = 2048
SEQ = 4096
BATCH = 4
DEPTH = 2
HD = 128
D_FF = 5632
NEG = -30000.0
SCALE = HD ** -0.5
RMS_EPS = 1e-6
N_BUCKETS = 32
MAX_DISTANCE = 2048
DILS = ((128, 1), (512, 4), (2048, 16))

ENGS = ("pe", "act", "dve", "pool", "sp")


class Buf:
    __slots__ = ("name", "writer", "readers", "dreaders", "sem", "semcnt")

    def __init__(self, name):
        self.name = name
        self.writer = None
        self.readers = {}
        self.dreaders = []
        self.sem = None
        self.semcnt = 0


class Op:
    __slots__ = ("eng", "fn", "deps", "is_dma", "sig", "sembuf", "needs_sig", "inc")

    def __init__(self, eng, fn, is_dma):
        self.eng = eng
        self.fn = fn
        self.deps = []
        self.is_dma = is_dma
        self.sig = None
        self.sembuf = None
        self.needs_sig = False
        self.inc = 16


class Sched:
    def __init__(self, nc, sems):
        self.nc = nc
        self.sems = sems
        self.dma_pool = [[s_, 0] for s_ in sems["dma"]]
        self.dma_bufs = []
        self.streams = {e: [] for e in ENGS}
        self.final_waits = []
        self.pending_dma = []
        self.barrier_deps = {e: [] for e in ENGS}

    def buf(self, name=None):
        return Buf(name)

    def add(self, eng, fn, reads=(), writes=(), dma_sem=None, extra_deps=(), cc_sem=None):
        op = Op(eng, fn, dma_sem is not None or cc_sem is not None)
        if cc_sem is not None:
            op.sig = (cc_sem, 1)
            op.inc = 1
        if dma_sem is not None:
            b = dma_sem
            if b.sem is None:
                if not self.dma_pool:
                    raise RuntimeError("out of DMA semaphores")
                b.sem = self.dma_pool.pop()
                self.dma_bufs.append(b)
            b.sem[1] += 16
            op.sig = (b.sem[0], b.sem[1])
            op.sembuf = b
        deps = []
        same = lambda d: (not op.is_dma) and (not d.is_dma) and d.eng == eng
        for b in reads:
            w = b.writer
            if w is not None and not (same(w) and eng == "pe"):
                deps.append(w)
        for b in writes:
            w = b.writer
            if w is not None and not same(w):
                deps.append(w)
            for r in b.readers.values():
                if not same(r):
                    deps.append(r)
            deps.extend(b.dreaders)
        deps.extend(extra_deps)
        if self.barrier_deps[eng]:
            deps.extend(self.barrier_deps[eng])
            self.barrier_deps[eng] = []
        seen = set()
        for d in deps:
            if d is op or id(d) in seen:
                continue
            seen.add(id(d))
            op.deps.append(d)
            d.needs_sig = True
        for b in reads:
            if op.is_dma:
                b.dreaders.append(op)
            else:
                b.readers[eng] = op
        for b in writes:
            b.writer = op
            b.readers = {}
            b.dreaders = []
        self.streams[eng].append(op)
        if op.is_dma:
            self.pending_dma.append(op)
        return op

    def barrier(self):
        lasts = []
        for e in ENGS:
            st = [o for o in self.streams[e] if not o.is_dma]
            if st:
                lasts.append(st[-1])
        lasts.extend(self.pending_dma)
        self.pending_dma = []
        for b in self.dma_bufs:
            self.dma_pool.append(b.sem)
            b.sem = None
        self.dma_bufs = []
        for e in ENGS:
            self.barrier_deps[e] = list(lasts)

    def emit(self):
        nc = self.nc
        sems = self.sems
        cnt = {e: 0 for e in ENGS}
        for e in ENGS:
            for op in self.streams[e]:
                if (not op.is_dma) and op.needs_sig:
                    cnt[e] += 1
                    ep = cnt[e] // 30000
                    op.sig = (sems[e][ep], cnt[e] - ep * 30000 + (1 if ep else 0))
        handles = {"pe": "tensor", "act": "scalar", "dve": "vector", "pool": "gpsimd", "sp": "sync"}
        with nc.Block() as block:
            for e in ENGS:
                ops = self.streams[e]
                if not ops and not (e == "sp" and self.final_waits):
                    continue

                def body(eng, ops=ops, e=e):
                    waited = {}
                    for op in ops:
                        for d in op.deps:
                            sem, val = d.sig
                            k = id(sem)
                            if waited.get(k, 0) >= val:
                                continue
                            eng.wait_ge(sem, val)
                            waited[k] = val
                        ins = op.fn(eng)
                        if op.is_dma:
                            ins.then_inc(op.sig[0], op.inc)
                        elif op.needs_sig:
                            ins.then_inc(op.sig[0], 1)
                    if e == "sp":
                        for d in self.final_waits:
                            sem, val = d.sig
                            if waited.get(id(sem), 0) >= val:
                                continue
                            eng.wait_ge(sem, val)
                            waited[id(sem)] = val

                getattr(block, handles[e])(body)


class Tile:
    __slots__ = ("ap", "buf")

    def __init__(self, ap, buf):
        self.ap = ap
        self.buf = buf

    def __getitem__(self, k):
        return self.ap[k]


class Arena:
    def __init__(self, nc, es, S, nbytes, name="arena"):
        self.t = es.enter_context(nc.sbuf_tensor(name, [128, nbytes // 4], F32))
        self.S = S
        self.off = 0
        self.cap = nbytes

    def alloc(self, ncols, dt, name=None):
        esz = 4 if dt == F32 else 2
        nb = (ncols * esz + 31) // 32 * 32
        if self.off + nb > self.cap:
            raise RuntimeError(f"arena overflow allocating {name}: {self.off}+{nb}>{self.cap}")
        v = self.t[:, self.off // 4:(self.off + nb) // 4]
        if dt != F32:
            v = v.bitcast(dt)
        v = v[:, 0:ncols]
        self.off += nb
        return Tile(v, self.S.buf(name))

    def mark(self):
        return self.off

    def release(self, m):
        self.off = m


def t5_bucket_np(dist):
    max_exact = N_BUCKETS // 2
    d = np.maximum(dist, 0)
    df = np.maximum(d, 1).astype(np.float32)
    large = max_exact + (np.log(df / np.float32(max_exact)) / np.float32(math.log(MAX_DISTANCE / max_exact))
                         * np.float32(N_BUCKETS - max_exact)).astype(np.int32)
    large = np.minimum(large, N_BUCKETS - 1)
    return np.where(d < max_exact, d, large)


class Prog:
    def __init__(self, arena_bytes=207 * 1024):
        self.nc = bass.Bass("TRN2", target_bir_lowering=False)
        self.es = contextlib.ExitStack()
        nc, es = self.nc, self.es
        sems = {e: [es.enter_context(nc.semaphore(f"s_{e}{k}")) for k in range(4)] for e in ("pe", "act", "dve", "pool")}
        self.cc_sems = [es.enter_context(nc.semaphore(f"cc{i}")) for i in range(6)]
        sems["dma"] = [es.enter_context(nc.semaphore(f"d{i}")) for i in range(76)]
        self.S = Sched(nc, sems)
        self.ar = Arena(nc, es, self.S, arena_bytes)
        self.ps = []
        for i in range(8):
            t = es.enter_context(nc.psum_tensor(f"ps{i}", [128, 512], F32))
            self.ps.append(Tile(t[:], self.S.buf(f"ps{i}")))

    def din(self, name, shape, dt=F32):
        return self.nc.dram_tensor(name, list(shape), dt, kind="ExternalInput").ap()

    def dout(self, name, shape, dt=F32):
        return self.nc.dram_tensor(name, list(shape), dt, kind="ExternalOutput").ap()

    def dscratch(self, name, shape, dt, debug=False):
        if debug:
            return self.nc.dram_tensor(name, list(shape), dt, kind="ExternalOutput").ap()
        return self.nc.dram_tensor(name, list(shape), dt).ap()

    def finish(self):
        self.S.emit()
        self.es.close()
        return self.nc


def load_consts(P, cst_in):
    S, ar = P.S, P.ar
    cb = ar.alloc(4 * 128, BF16, "cstb")
    S.add("pool", lambda g: g.dma_start(out=cb.ap, in_=cst_in[:, 0:512]), writes=[cb.buf], dma_sem=cb.buf)
    cf = ar.alloc(128, F32, "identf")
    S.add("sp", lambda g: g.dma_start(out=cf.ap, in_=cst_in[:, 0:128]), writes=[cf.buf], dma_sem=cf.buf)
    eps = ar.alloc(1, F32, "eps")
    S.add("dve", lambda v: v.memset(eps.ap, RMS_EPS), writes=[eps.buf])
    return dict(eps=eps, cb=cb, identb=cb.ap[:, 0:128], onesb=cb.ap[:, 128:256], negones=cb.ap[:, 256:384],
                neguincl=cb.ap[:, 384:512], cf=cf, identf=cf.ap)


def host_consts():
    c = np.zeros((128, 512), np.float32)
    c[:, 0:128] = np.eye(128)
    c[:, 128:256] = 1.0
    c[:, 256:384] = -1.0
    j = np.arange(128)[:, None]
    k = np.arange(128)[None, :]
    c[:, 384:512] = -(j >= k).astype(np.float32)
    return c


def norm_transpose(P, C, x_tiles, gb, hT, ntiles, tmp):
    S = P.S
    T = ntiles * 128
    for i, xt in enumerate(x_tiles):
        xs = tmp["x"][i % 2]
        hb = tmp["hb"][i % 2]
        junk = tmp["junk"]
        ssq = tmp["ssq"][i % 2]
        rstd = tmp["rstd"][i % 2]
        if isinstance(xt, Tile):
            xs = xt
        else:
            S.add("sp", lambda g, xs=xs, xt=xt: g.dma_start(out=xs.ap, in_=xt), writes=[xs.buf], dma_sem=xs.buf)
        S.add("act", lambda a, xs=xs, ssq=ssq: a.activation(out=junk.ap, in_=xs.ap, func=AF.Square, accum_out=ssq.ap),
              reads=[xs.buf], writes=[junk.buf, ssq.buf])
        S.add("act", lambda a, ssq=ssq, rstd=rstd: a.activation(out=rstd.ap, in_=ssq.ap, func=AF.Sqrt,
                                                               scale=1.0 / D_MODEL, bias=C["eps"].ap),
              reads=[ssq.buf, C["eps"].buf], writes=[rstd.buf])
        S.add("dve", lambda v, rstd=rstd: v.reciprocal(out=rstd.ap, in_=rstd.ap), reads=[rstd.buf], writes=[rstd.buf])
        S.add("dve", lambda v, xs=xs, rstd=rstd, hb=hb: v.scalar_tensor_tensor(
            out=hb.ap, in0=xs.ap, scalar=rstd.ap, in1=gb.ap, op0=ALU.mult, op1=ALU.mult),
            reads=[xs.buf, rstd.buf, gb.buf], writes=[hb.buf])
        for half in range(2):
            pb = P.ps[6 + half]
            pv = pb.ap.bitcast(BF16)
            for k in range(8):
                fc = half * 8 + k
                S.add("pe", lambda t, pv=pv, k=k, fc=fc, hb=hb: t.transpose(pv[:, k * 128:(k + 1) * 128],
                                                                           hb.ap[:, fc * 128:(fc + 1) * 128], C["identb"]),
                      reads=[hb.buf, C["cb"].buf], writes=[pb.buf])
            dst = hT.ap.rearrange("p (f t) -> p f t", f=16)[:, half * 8:(half + 1) * 8, i * 128:(i + 1) * 128]
            src = pv.rearrange("p (f t) -> p f t", f=8)
            if half == 0:
                S.add("act", lambda a, dst=dst, src=src: a.copy(out=dst, in_=src), reads=[pb.buf], writes=[hT.buf])
            else:
                S.add("dve", lambda v, dst=dst, src=src: v.tensor_copy(out=dst, in_=src), reads=[pb.buf], writes=[hT.buf])


def norm_tmp(P, need_x=True):
    ar = P.ar
    return dict(x=[ar.alloc(2048, F32, f"xs{i}") for i in range(2)] if need_x else [None, None],
                hb=[ar.alloc(2048, BF16, f"hb{i}") for i in range(2)],
                junk=ar.alloc(2048, BF16, "junk"),
                ssq=[ar.alloc(1, F32, f"ssq{i}") for i in range(2)],
                rstd=[ar.alloc(1, F32, f"rstd{i}") for i in range(2)])


def mm(S, out, lhsT, rhs, start, stop, reads, writes):
    return S.add("pe", lambda t: t.matmul(out, lhsT, rhs, start=start, stop=stop), reads=reads, writes=writes)


def emit_A(P, C, T, phases=("a0", "sb", "moba", "dil")):
    S, ar = P.S, P.ar
    debug = False
    x, gmix, wqkv, gcols = T["x"], T["gmix"], T["wqkv"], T["gcols"]
    mstrip, dstrip, sbmask, esel = T["mstrip"], T["dstrip"], T["sbmask"], T["esel"]
    qTs, kTs, vs = T["qTs"], T["kTs"], T["vs"]
    mA = ar.mark()
    gb = ar.alloc(2048, F32, "gb")
    S.add("sp", lambda g: g.dma_start(out=gb.ap, in_=gmix), writes=[gb.buf], dma_sem=gb.buf)
    gc = ar.alloc(12, F32, "gc")
    S.add("sp", lambda g: g.dma_start(out=gc.ap, in_=gcols), writes=[gc.buf], dma_sem=gc.buf)
    S.add("act", lambda a: a.mul(out=gc.ap[:, 0:6], in_=gc.ap[:, 0:6], mul=SCALE), reads=[gc.buf], writes=[gc.buf])
    stores = []
    m0 = ar.mark()
    if "a0" in phases:
        W = ar.alloc(16 * 3072, BF16, "W")
        for fc in range(16):
            S.add("pool", lambda g, fc=fc: g.dma_start(out=W.ap[:, fc * 3072:(fc + 1) * 3072],
                                                       in_=wqkv[fc * 128:(fc + 1) * 128, :]),
                  writes=[W.buf], dma_sem=W.buf)
        tmp = norm_tmp(P)
        hT = [ar.alloc(16 * 512, BF16, f"hT{i}") for i in range(2)]
        sq = [ar.alloc(512, BF16, f"sq{i}") for i in range(2)]
        rs = [ar.alloc(512, F32, f"rs{i}") for i in range(2)]
        qst = [ar.alloc(512, BF16, f"qst{i}") for i in range(4)]
        vst = [ar.alloc(1024, BF16, f"vst{i}") for i in range(2)]
        for g in range(8):
            h = hT[g % 2]
            if T.get("load_hT") is not None:
                T["load_hT"](S, g, h)
            else:
                norm_transpose(P, C, [x[(g * 4 + i) * 128:(g * 4 + i + 1) * 128, :] for i in range(4)], gb, h, 4, tmp)

            def proj(j):
                pq = P.ps[j % 3]
                for fc in range(16):
                    mm(S, pq.ap, W.ap[:, fc * 3072 + j * 128: fc * 3072 + (j + 1) * 128], h.ap[:, fc * 512:(fc + 1) * 512],
                       fc == 0, fc == 15, [W.buf, h.buf], [pq.buf])

            def post(j):
                hh = j % 8
                isq = j < 8
                pq = P.ps[j % 3]
                st = qst[j % 4]
                if hh < 2:
                    if isq:
                        S.add("act", lambda a: a.mul(out=st.ap, in_=pq.ap, mul=SCALE), reads=[pq.buf], writes=[st.buf])
                    else:
                        S.add("act", lambda a: a.copy(out=st.ap, in_=pq.ap), reads=[pq.buf], writes=[st.buf])
                else:
                    s = hh - 2
                    col = gc.ap[:, s:s + 1] if isq else gc.ap[:, 6 + s:7 + s]
                    sqt = sq[j % 2]
                    rst = rs[j % 2]
                    pss = P.ps[3 + j % 2]
                    S.add("act", lambda a: a.activation(out=sqt.ap, in_=pq.ap, func=AF.Square), reads=[pq.buf], writes=[sqt.buf])
                    mm(S, pss.ap, C["onesb"], sqt.ap, True, True, [C["cb"].buf, sqt.buf], [pss.buf])
                    S.add("act", lambda a: a.activation(out=rst.ap, in_=pss.ap, func=AF.Sqrt, scale=1.0 / HD,
                                                        bias=C["eps"].ap), reads=[pss.buf, C["eps"].buf], writes=[rst.buf])
                    S.add("dve", lambda v: v.reciprocal(out=rst.ap, in_=rst.ap), reads=[rst.buf], writes=[rst.buf])
                    S.add("dve", lambda v: v.scalar_tensor_tensor(out=st.ap, in0=pq.ap, scalar=col, in1=rst.ap,
                                                                  op0=ALU.mult, op1=ALU.mult),
                          reads=[pq.buf, rst.buf, gc.buf], writes=[st.buf])
                dst = (qTs if isq else kTs)[hh, :, g * 512:(g + 1) * 512]
                stores.append(S.add("pool", lambda q: q.dma_start(out=dst, in_=st.ap), reads=[st.buf], dma_sem=st.buf))

            for j in range(16):
                proj(j)
                if j >= 1:
                    post(j - 1)
            post(15)
            for i in range(4):
                vt = vst[i % 2]
                for half in range(2):
                    pv = P.ps[half]
                    for fc in range(16):
                        mm(S, pv.ap, h.ap[:, fc * 512 + i * 128: fc * 512 + (i + 1) * 128],
                           W.ap[:, fc * 3072 + 2048 + half * 512: fc * 3072 + 2048 + (half + 1) * 512],
                           fc == 0, fc == 15, [W.buf, h.buf], [pv.buf])
                    if half == 0:
                        S.add("act", lambda a, vt=vt, pv=pv: a.copy(out=vt.ap[:, 0:512], in_=pv.ap), reads=[pv.buf], writes=[vt.buf])
                    else:
                        S.add("dve", lambda v, vt=vt, pv=pv: v.tensor_copy(out=vt.ap[:, 512:1024], in_=pv.ap), reads=[pv.buf], writes=[vt.buf])
                r0 = (g * 4 + i) * 128
                stores.append(S.add("pool", lambda q, vt=vt, r0=r0: q.dma_start(out=vs[r0:r0 + 128, :], in_=vt.ap),
                                    reads=[vt.buf], dma_sem=vt.buf))
        S.barrier()
    ar.release(m0)
    outs = []
    PS = P.ps

    def load_qk(hh):
        QT = ar.alloc(SEQ, BF16, "QT")
        KT = ar.alloc(SEQ, BF16, "KT")
        S.add("sp", lambda g: g.dma_start(out=QT.ap, in_=qTs[hh]), writes=[QT.buf], dma_sem=QT.buf)
        S.add("sp", lambda g: g.dma_start(out=KT.ap, in_=kTs[hh]), writes=[KT.buf], dma_sem=KT.buf)
        return QT, KT

    def load_v(hh, r):
        V = ar.alloc(SEQ, BF16, f"V{r}")
        nsub = SEQ // r // 128
        for c in range(r):
            src = vs[c::r, hh * 128:(hh + 1) * 128].rearrange("(n k) d -> k n d", k=128)
            dst = V.ap[:, c * nsub * 128:(c + 1) * nsub * 128].rearrange("p (n d) -> p n d", d=128)
            S.add("sp", lambda g, src=src, dst=dst: g.dma_start(out=dst, in_=src), writes=[V.buf], dma_sem=V.buf)
        return V

    def store_o(hh, OT):
        outs.extend(T["store_o"](S, hh, OT))

    def sb_head(hh):
        m = ar.mark()
        QT, KT = load_qk(hh)
        V = load_v(hh, 1)
        maskb = ar.alloc(896, BF16, "sbmaskb")
        S.add("pool", lambda g: g.dma_start(out=maskb.ap, in_=sbmask), writes=[maskb.buf], dma_sem=maskb.buf)
        OT = ar.alloc(SEQ, BF16, "OT")
        e1 = [ar.alloc(512, F32, f"e1{i}") for i in range(2)]
        spb = [ar.alloc(512, BF16, f"spb{i}") for i in range(2)]
        a3 = [ar.alloc(512, F32, f"a3{i}") for i in range(2)]
        PT = [ar.alloc(512, BF16, f"PT{i}") for i in range(2)]
        carry = ar.alloc(512, F32, "carry")
        zA = [PS[0], PS[1]]
        aP = [PS[2], PS[3]]
        Tp = PS[4]
        acc = PS[5]
        cb = C["cb"].buf
        for g in range(8):
            tiles = list(range(4 * g + 3, -1, -1))
            qs = QT.ap[:, g * 512:(g + 1) * 512]
            last = len(tiles) - 1

            def st1(idx):
                mt = tiles[idx]
                par = idx % 2
                ing = mt >= 4 * g
                kt = KT.ap[:, mt * 128:(mt + 1) * 128]
                mm(S, zA[par].ap, kt, qs, True, not ing, [KT.buf, QT.buf], [zA[par].buf])
                if ing:
                    off = 384 - 128 * (mt - 4 * g)
                    mm(S, zA[par].ap, C["identb"], maskb.ap[:, off:off + 512], False, True, [cb, maskb.buf], [zA[par].buf])
                S.add("act", lambda a: a.activation(out=e1[par].ap, in_=zA[par].ap, func=AF.Exp),
                      reads=[zA[par].buf], writes=[e1[par].buf])
                S.add("act", lambda a: a.activation(out=spb[par].ap, in_=e1[par].ap, func=AF.Ln, bias=1.0),
                      reads=[e1[par].buf], writes=[spb[par].buf])

            def st2(idx):
                mt = tiles[idx]
                par = idx % 2
                ing = mt >= 4 * g
                kt = KT.ap[:, mt * 128:(mt + 1) * 128]
                mm(S, aP[par].ap, kt, qs, True, False, [KT.buf, QT.buf], [aP[par].buf])
                if ing:
                    off = 384 - 128 * (mt - 4 * g)
                    mm(S, aP[par].ap, C["identb"], maskb.ap[:, off:off + 512], False, False, [cb, maskb.buf], [aP[par].buf])
                mm(S, aP[par].ap, C["neguincl"], spb[par].ap, False, True, [cb, spb[par].buf], [aP[par].buf])
                mm(S, Tp.ap, C["negones"], spb[par].ap, True, True, [cb, spb[par].buf], [Tp.buf])
                if idx == 0:
                    S.add("act", lambda a: a.activation(out=PT[par].ap, in_=aP[par].ap, func=AF.Exp),
                          reads=[aP[par].buf], writes=[PT[par].buf])
                else:
                    S.add("dve", lambda v: v.tensor_tensor(out=a3[par].ap, in0=aP[par].ap, in1=carry.ap, op=ALU.add),
                          reads=[aP[par].buf, carry.buf], writes=[a3[par].buf])
                    S.add("act", lambda a: a.activation(out=PT[par].ap, in_=a3[par].ap, func=AF.Exp),
                          reads=[a3[par].buf], writes=[PT[par].buf])
                mm(S, acc.ap, V.ap[:, mt * 128:(mt + 1) * 128], PT[par].ap, idx == 0, idx == last,
                   [V.buf, PT[par].buf], [acc.buf])
                if idx == 0:
                    S.add("dve", lambda v: v.tensor_copy(out=carry.ap, in_=Tp.ap), reads=[Tp.buf], writes=[carry.buf])
                elif idx < last:
                    S.add("dve", lambda v: v.tensor_tensor(out=carry.ap, in0=carry.ap, in1=Tp.ap, op=ALU.add),
                          reads=[Tp.buf, carry.buf], writes=[carry.buf])

            for idx in range(len(tiles) + 1):
                if idx < len(tiles):
                    st1(idx)
                if idx >= 1:
                    st2(idx - 1)
            S.add("act", lambda a, g=g: a.copy(out=OT.ap[:, g * 512:(g + 1) * 512], in_=acc.ap),
                  reads=[acc.buf], writes=[OT.buf])
        store_o(hh, OT)
        S.barrier()
        ar.release(m)

    def moba_head(hh, s):
        m = ar.mark()
        QT, KT = load_qk(hh)
        V = load_v(hh, 1)
        strip = ar.alloc(4352, BF16, "mstrip")
        S.add("pool", lambda g: g.dma_start(out=strip.ap, in_=mstrip[s]), writes=[strip.buf], dma_sem=strip.buf)
        eselb = ar.alloc(2048, BF16, "eselb")
        S.add("pool", lambda g: g.dma_start(out=eselb.ap[0:16, :], in_=esel), writes=[eselb.buf], dma_sem=eselb.buf)
        selT = ar.alloc(SEQ, BF16, "selT")
        OT = ar.alloc(SEQ, BF16, "OT")
        ksum = ar.alloc(16, F32, "ksum")
        kmb = ar.alloc(16, BF16, "kmb")
        gm = ar.alloc(16, F32, "gm")
        mx = ar.alloc(8, F32, "mx")
        sel01 = ar.alloc(16, F32, "sel01")
        rd = ar.alloc(256, F32, "rd")
        PT = [ar.alloc(256, BF16, f"PTm{i}") for i in range(2)]
        cb = C["cb"].buf
        S.add("dve", lambda v: v.reduce_sum(out=ksum.ap, in_=KT.ap.rearrange("p (n k) -> p n k", k=256), axis=AX.X),
              reads=[KT.buf], writes=[ksum.buf])
        S.add("act", lambda a: a.mul(out=kmb.ap, in_=ksum.ap, mul=1.0 / 256), reads=[ksum.buf], writes=[kmb.buf])
        S.add("dve", lambda v: v.memset(gm.ap, -1e30), writes=[gm.buf])
        pg = PS[6]
        pt = PS[7]
        for i in range(2, 32):
            ob = i // 2
            mm(S, pg.ap[:, 0:16], QT.ap[:, i * 128:(i + 1) * 128], kmb.ap, True, True, [QT.buf, kmb.buf], [pg.buf])
            S.add("dve", lambda v, ob=ob: v.tensor_copy(out=gm.ap[:, 0:ob], in_=pg.ap[:, 0:ob]), reads=[pg.buf], writes=[gm.buf])
            S.add("dve", lambda v: v.max(out=mx.ap, in_=gm.ap), reads=[gm.buf], writes=[mx.buf])
            S.add("dve", lambda v: v.tensor_scalar(out=sel01.ap, in0=gm.ap, scalar1=mx.ap[:, 2:3], scalar2=-1.0,
                                                   op0=ALU.is_ge, op1=ALU.add),
                  reads=[gm.buf, mx.buf], writes=[sel01.buf])
            S.add("pe", lambda t: t.transpose(pt.ap[0:16, 0:128], sel01.ap, C["identf"]),
                  reads=[sel01.buf, C["cf"].buf], writes=[pt.buf])
            S.add("act", lambda a, i=i: a.mul(out=selT.ap[0:16, i * 128:(i + 1) * 128], in_=pt.ap[0:16, 0:128], mul=-NEG),
                  reads=[pt.buf], writes=[selT.buf])
        for G in range(16):
            kts = list(range(0, 2 * G + 2))
            last = len(kts) - 1
            acc = PS[4 + G % 2]
            den = PS[2 + G % 2]
            qs = QT.ap[:, G * 256:(G + 1) * 256]

            def stS(idx):
                kt = kts[idx]
                n = kt // 2
                par = idx % 2
                Sp = PS[par]
                mm(S, Sp.ap[:, 0:256], KT.ap[:, kt * 128:(kt + 1) * 128], qs, True, False, [KT.buf, QT.buf], [Sp.buf])
                off = 256 * G - 128 * kt + 128
                mm(S, Sp.ap[:, 0:256], C["identb"], strip.ap[:, off:off + 256], False, n == G, [cb, strip.buf], [Sp.buf])
                if n < G:
                    mm(S, Sp.ap[:, 0:256], eselb.ap[0:16, n * 128:(n + 1) * 128], selT.ap[0:16, G * 256:(G + 1) * 256],
                       False, True, [eselb.buf, selT.buf], [Sp.buf])
                S.add("act", lambda a: a.activation(out=PT[par].ap, in_=Sp.ap[:, 0:256], func=AF.Exp),
                      reads=[Sp.buf], writes=[PT[par].buf])

            def stPV(idx):
                kt = kts[idx]
                par = idx % 2
                mm(S, acc.ap[:, 0:256], V.ap[:, kt * 128:(kt + 1) * 128], PT[par].ap, idx == 0, idx == last,
                   [V.buf, PT[par].buf], [acc.buf])
                mm(S, den.ap[:, 0:256], C["onesb"], PT[par].ap, idx == 0, idx == last, [cb, PT[par].buf], [den.buf])

            for idx in range(len(kts) + 1):
                if idx < len(kts):
                    stS(idx)
                if idx >= 1:
                    stPV(idx - 1)
            S.add("dve", lambda v, den=den: v.reciprocal(out=rd.ap, in_=den.ap[:, 0:256]), reads=[den.buf], writes=[rd.buf])
            S.add("dve", lambda v, acc=acc, G=G: v.tensor_tensor(out=OT.ap[:, G * 256:(G + 1) * 256], in0=acc.ap[:, 0:256],
                                                                in1=rd.ap, op=ALU.mult),
                  reads=[acc.buf, rd.buf], writes=[OT.buf])
        store_o(hh, OT)
        S.barrier()
        ar.release(m)

    def dil_head(hh, s):
        m = ar.mark()
        QT, KT = load_qk(hh)
        Vr = [load_v(hh, r) for (_, r) in DILS]
        dsb = ar.alloc(768, BF16, "dsb")
        S.add("pool", lambda g: g.dma_start(out=dsb.ap, in_=dstrip[s]), writes=[dsb.buf], dma_sem=dsb.buf)
        ACC = ar.alloc(2 * SEQ, F32, "ACC")
        OT = ar.alloc(SEQ, BF16, "OT")
        PT = [ar.alloc(256, BF16, f"PTd{i}") for i in range(2)]
        cb = C["cb"].buf
        ACC3 = ACC.ap.rearrange("p (a t) -> p a t", a=2)
        for pi, (w, r) in enumerate(DILS):
            nsub = SEQ // r // 128
            V = Vr[pi]
            items = [(c, n) for c in range(r) for n in range(nsub)]

            def sub(T_, c, n):
                return T_.ap[:, c + r * 128 * n: c + r * 128 * n + r * 127 + 1: r]

            def stS(idx):
                c, n = items[idx]
                par = idx % 2
                Sp = PS[par]
                qsl = sub(QT, c, n)
                mm(S, Sp.ap[:, 128:256], sub(KT, c, n), qsl, True, False, [KT.buf, QT.buf], [Sp.buf])
                mm(S, Sp.ap[:, 128:256], C["identb"], dsb.ap[:, pi * 256:pi * 256 + 128], False, True, [cb, dsb.buf], [Sp.buf])
                if n >= 1:
                    mm(S, Sp.ap[:, 0:128], sub(KT, c, n - 1), qsl, True, False, [KT.buf, QT.buf], [Sp.buf])
                    mm(S, Sp.ap[:, 0:128], C["identb"], dsb.ap[:, pi * 256 + 128:pi * 256 + 256], False, True,
                       [cb, dsb.buf], [Sp.buf])
                    S.add("act", lambda a: a.activation(out=PT[par].ap, in_=Sp.ap[:, 0:256], func=AF.Exp),
                          reads=[Sp.buf], writes=[PT[par].buf])
                else:
                    S.add("act", lambda a: a.activation(out=PT[par].ap[:, 128:256], in_=Sp.ap[:, 128:256], func=AF.Exp),
                          reads=[Sp.buf], writes=[PT[par].buf])

            def stPV(idx):
                c, n = items[idx]
                par = idx % 2
                acc = PS[2 + idx % 2]
                ti = c * nsub + n
                vc = V.ap[:, ti * 128:(ti + 1) * 128]
                rb = [V.buf, PT[par].buf]
                if n >= 1:
                    vp = V.ap[:, (ti - 1) * 128:ti * 128]
                    mm(S, acc.ap[:, 0:128], vp, PT[par].ap[:, 0:128], True, False, rb, [acc.buf])
                    mm(S, acc.ap[:, 0:128], vc, PT[par].ap[:, 128:256], False, True, rb, [acc.buf])
                    mm(S, acc.ap[:, 128:256], C["onesb"], PT[par].ap[:, 0:128], True, False, [cb, PT[par].buf], [acc.buf])
                    mm(S, acc.ap[:, 128:256], C["onesb"], PT[par].ap[:, 128:256], False, True, [cb, PT[par].buf], [acc.buf])
                else:
                    mm(S, acc.ap[:, 0:128], vc, PT[par].ap[:, 128:256], True, True, rb, [acc.buf])
                    mm(S, acc.ap[:, 128:256], C["onesb"], PT[par].ap[:, 128:256], True, True, [cb, PT[par].buf], [acc.buf])
                dst = ACC3[:, :, c + r * 128 * n: c + r * 128 * n + r * 127 + 1: r]
                src = acc.ap[:, 0:256].rearrange("p (a t) -> p a t", a=2)
                if pi == 0:
                    S.add("dve", lambda v: v.tensor_copy(out=dst, in_=src), reads=[acc.buf], writes=[ACC.buf])
                else:
                    S.add("dve", lambda v: v.tensor_tensor(out=dst, in0=dst, in1=src, op=ALU.add),
                          reads=[acc.buf, ACC.buf], writes=[ACC.buf])

            for idx in range(len(items) + 1):
                if idx < len(items):
                    stS(idx)
                if idx >= 1:
                    stPV(idx - 1)
        S.add("dve", lambda v: v.reciprocal(out=ACC.ap[:, SEQ:2 * SEQ], in_=ACC.ap[:, SEQ:2 * SEQ]), reads=[ACC.buf], writes=[ACC.buf])
        S.add("dve", lambda v: v.tensor_tensor(out=OT.ap, in0=ACC.ap[:, 0:SEQ], in1=ACC.ap[:, SEQ:2 * SEQ], op=ALU.mult),
              reads=[ACC.buf], writes=[OT.buf])
        store_o(hh, OT)
        S.barrier()
        ar.release(m)

    if "sb" in phases:
        sb_head(0)
        sb_head(1)
    if "moba" in phases:
        for s_ in range(3):
            moba_head(2 + s_, s_)
    if "dil" in phases:
        for s_ in range(3):
            dil_head(5 + s_, s_)
    ar.release(mA)
    return outs


def host_A_inputs(x_b, g_mix_l, w_in_l, q_gain_l, k_gain_l, rel_bias, hg):
    sb = [0, 1] if hg == 0 else [2, 3]
    mo = [4, 5, 6] if hg == 0 else [7, 8, 9]
    di = [10, 11, 12] if hg == 0 else [13, 14, 15]
    hs = sb + mo + di
    cols = lambda base: [w_in_l[:, base + h * 128: base + (h + 1) * 128] for h in hs]
    wqkv = np.ascontiguousarray(np.concatenate(cols(0) + cols(2048) + cols(4096), axis=1))
    soft = [h - 4 for h in hs[2:]]
    gcols = np.ascontiguousarray(np.concatenate([q_gain_l[soft].T, k_gain_l[soft].T], axis=1).astype(np.float32))
    negf = np.float32(NEG)
    kk = np.arange(128)[:, None]
    u = np.arange(4352)[None, :]
    dist = u - 128 - kk
    bidx = t5_bucket_np(dist)
    mstrip = np.stack([np.where(dist >= 0, rel_bias[bidx, h - 4], negf) for h in mo]).astype(np.float32)
    u2 = np.arange(256)[None, :]
    delta = u2 - kk
    valid = (delta >= 0) & (delta <= 128)
    ds = []
    for h in di:
        per = []
        for (w, r) in DILS:
            per.append(np.where(valid, rel_bias[t5_bucket_np(delta * r), h - 4], negf))
        ds.append(np.concatenate(per, axis=1))
    dstrip = np.stack(ds).astype(np.float32)
    return dict(x=np.ascontiguousarray(x_b), gmix=np.ascontiguousarray(np.broadcast_to(g_mix_l[None, :], (128, D_MODEL))),
                wqkv=wqkv, gcols=gcols, mstrip=np.ascontiguousarray(mstrip), dstrip=np.ascontiguousarray(dstrip),
                sbmask=host_sbmask(), esel=host_esel(), cst=host_consts()), hs


def host_sbmask():
    kk = np.arange(128)[:, None]
    u = np.arange(896)[None, :] - 384
    return np.where(u > kk, 0.0, NEG).astype(np.float32)


def host_esel():
    e = np.zeros((16, 2048), np.float32)
    for n in range(16):
        e[n, n * 128:(n + 1) * 128] = 1.0
    return e


GORDER = (0, 1, 2, 3)
BR_HEADS = ([0, 1, 2, 3], [4, 5, 6, 7, 8, 9], [10, 11, 12, 13, 14, 15])


def emit_B(P, C, T):
    S, ar = P.S, P.ar
    PS = P.ps
    stage = 2
    x, gmix, gffn = T["x"], T["gmix"], T["gffn"]
    wg, wb, wo, wgu, wd, xo = T["wg"], T["wb"], T["wo"], T["wgu"], T["wd"], T["xo"]
    slot = T["slot"]
    mB = ar.mark()
    gbx = ar.alloc(2048, F32, "gbx")

    def load_g(src):
        S.add("sp", lambda g: g.dma_start(out=gbx.ap, in_=src), writes=[gbx.buf], dma_sem=gbx.buf)
        return gbx
    tmp = norm_tmp(P, need_x=False)
    xres = ar.alloc(4 * 2048, F32, "xres")
    hT = ar.alloc(16 * 512, BF16, "hT")
    wbufs = [ar.alloc(16 * 512, BF16, f"wbuf{i}") for i in range(3)]
    macc = [ar.alloc(512, F32, f"macc{i}") for i in range(4)]
    gs = [ar.alloc(512, F32, f"gs{i}") for i in range(2)]
    tt = [ar.alloc(512, F32, f"tt{i}") for i in range(2)]
    yT = [ar.alloc(512, F32, f"yT{i}") for i in range(2)]
    wbb = ar.alloc(16 * 512, BF16, "wbb")
    gb3 = T.get("next_gmix")
    mR = ar.mark()
    wcount = [0]
    outs = []

    def wload(src3, nch, t=None):
        if t is None:
            t = wbufs[wcount[0] % 3]
            wcount[0] += 1
        dst = t.ap[:, 0:nch * 512].rearrange("p (c n) -> p c n", n=512)
        S.add("pool", lambda g: g.dma_start(out=dst, in_=src3), writes=[t.buf], dma_sem=t.buf)
        return t

    def wsrc(w, r0, nch, c0):
        return w[r0:r0 + nch * 128, c0:c0 + 512].rearrange("(c p) n -> p c n", p=128)

    def resid_add(yps, oc, cnt):
        y = yT[cnt % 2]
        S.add("act", lambda a: a.copy(out=y.ap, in_=yps.ap), reads=[yps.buf], writes=[y.buf])
        pt = PS[4 + cnt % 2]
        for t in range(4):
            S.add("pe", lambda e, t=t: e.transpose(pt.ap[:, t * 128:(t + 1) * 128], y.ap[:, t * 128:(t + 1) * 128], C["identf"]),
                  reads=[y.buf, C["cf"].buf], writes=[pt.buf])
        dst = xres.ap.rearrange("p (t f) -> p t f", t=4)[:, :, oc * 128:(oc + 1) * 128]
        src = pt.ap.rearrange("p (t f) -> p t f", t=4)
        S.add("dve", lambda v: v.tensor_tensor(out=dst, in0=dst, in1=src, op=ALU.add), reads=[pt.buf, xres.buf], writes=[xres.buf])

    for g in GORDER:
        ar.release(mR)
        oTg = ar.alloc(16 * 512, BF16, "oTg")
        mT = ar.alloc(16 * 512, BF16, "mT")
        for t in range(4):
            r0 = g * 512 + t * 128
            S.add("sp", lambda q, t=t, r0=r0: q.dma_start(out=xres.ap[:, t * 2048:(t + 1) * 2048], in_=x[r0:r0 + 128, :]),
                  writes=[xres.buf], dma_sem=xres.buf)
        T["load_oT"](S, g, oTg, mT)
        xt = [Tile(xres.ap[:, t * 2048:(t + 1) * 2048], xres.buf) for t in range(4)]
        norm_transpose(P, C, xt, load_g(gmix), hT, 4, tmp)
        cnt = 0
        for k in range(4):
            wbt = wload(wsrc(wb, 0, 16, k * 512), 16, wbb)
            for br in range(3):
                wgt = wload(wsrc(wg, 0, 16, br * 2048 + k * 512), 16)
                for c4 in range(4):
                    cc = 4 * k + c4
                    gp = PS[cnt % 2]
                    pp = PS[2 + cnt % 2]
                    g_ = gs[cnt % 2]
                    t_ = tt[cnt % 2]
                    cnt += 1
                    for fc in range(16):
                        mm(S, gp.ap, wgt.ap[:, fc * 512 + c4 * 128: fc * 512 + (c4 + 1) * 128], hT.ap[:, fc * 512:(fc + 1) * 512],
                           fc == 0, fc == 15, [wgt.buf, hT.buf], [gp.buf])
                    hl = BR_HEADS[br]
                    for i, h in enumerate(hl):
                        mm(S, pp.ap, wbt.ap[:, h * 512 + c4 * 128: h * 512 + (c4 + 1) * 128], oTg.ap[:, slot(h) * 512:(slot(h) + 1) * 512],
                           i == 0, i == len(hl) - 1, [wbt.buf, oTg.buf], [pp.buf])
                    S.add("act", lambda a, g_=g_, gp=gp: a.activation(out=g_.ap, in_=gp.ap, func=AF.Sigmoid),
                          reads=[gp.buf], writes=[g_.buf])
                    if br == 0:
                        S.add("dve", lambda v, g_=g_, pp=pp, c4=c4: v.tensor_tensor(out=macc[c4].ap, in0=pp.ap, in1=g_.ap, op=ALU.mult),
                              reads=[pp.buf, g_.buf], writes=[macc[c4].buf])
                    else:
                        S.add("dve", lambda v, g_=g_, pp=pp, t_=t_: v.tensor_tensor(out=t_.ap, in0=pp.ap, in1=g_.ap, op=ALU.mult),
                              reads=[pp.buf, g_.buf], writes=[t_.buf])
                        if br == 1:
                            S.add("dve", lambda v, t_=t_, c4=c4: v.tensor_tensor(out=macc[c4].ap, in0=macc[c4].ap, in1=t_.ap, op=ALU.add),
                                  reads=[t_.buf, macc[c4].buf], writes=[macc[c4].buf])
                        else:
                            S.add("dve", lambda v, t_=t_, c4=c4, cc=cc, mT=mT: v.tensor_tensor(out=mT.ap[:, cc * 512:(cc + 1) * 512], in0=macc[c4].ap,
                                                                                     in1=t_.ap, op=ALU.add),
                                  reads=[t_.buf, macc[c4].buf], writes=[mT.buf])
        cnt = 0
        for k in range(4):
            wot = wload(wsrc(wo, 0, 16, k * 512), 16)
            for c4 in range(4):
                oc = 4 * k + c4
                yp = PS[cnt % 2]
                for cc in range(16):
                    mm(S, yp.ap, wot.ap[:, cc * 512 + c4 * 128: cc * 512 + (c4 + 1) * 128], mT.ap[:, cc * 512:(cc + 1) * 512],
                       cc == 0, cc == 15, [wot.buf, mT.buf], [yp.buf])
                resid_add(yp, oc, cnt)
                cnt += 1
        S.barrier()
        if stage == 1:
            for t in range(4):
                r0 = g * 512 + t * 128
                outs.append(S.add("sp", lambda q, t=t, r0=r0: q.dma_start(out=xo[r0:r0 + 128, :], in_=xres.ap[:, t * 2048:(t + 1) * 2048]),
                                  reads=[xres.buf], dma_sem=xres.buf))
            S.barrier()
            continue
        ar.release(mR)
        aT = ar.alloc(44 * 512, BF16, "aT")
        norm_transpose(P, C, xt, load_g(gffn), hT, 4, tmp)
        cnt = 0
        for j in range(11):
            wgt = wload(wsrc(wgu, 0, 16, j * 512), 16)
            wut = wload(wsrc(wgu, 0, 16, D_FF + j * 512), 16)
            for c4 in range(4):
                jc = 4 * j + c4
                gp = PS[cnt % 2]
                up = PS[2 + cnt % 2]
                g_ = gs[cnt % 2]
                cnt += 1
                for fc in range(16):
                    mm(S, gp.ap, wgt.ap[:, fc * 512 + c4 * 128: fc * 512 + (c4 + 1) * 128], hT.ap[:, fc * 512:(fc + 1) * 512],
                       fc == 0, fc == 15, [wgt.buf, hT.buf], [gp.buf])
                for fc in range(16):
                    mm(S, up.ap, wut.ap[:, fc * 512 + c4 * 128: fc * 512 + (c4 + 1) * 128], hT.ap[:, fc * 512:(fc + 1) * 512],
                       fc == 0, fc == 15, [wut.buf, hT.buf], [up.buf])
                S.add("act", lambda a, g_=g_, gp=gp: a.activation(out=g_.ap, in_=gp.ap, func=AF.Silu), reads=[gp.buf], writes=[g_.buf])
                S.add("dve", lambda v, g_=g_, up=up, jc=jc, aT=aT: v.tensor_tensor(out=aT.ap[:, jc * 512:(jc + 1) * 512], in0=up.ap, in1=g_.ap, op=ALU.mult),
                      reads=[up.buf, g_.buf], writes=[aT.buf])
        cnt = 0
        for k in range(4):
            for (j0, nch) in ((0, 16), (16, 16), (32, 12)):
                wdt = wload(wsrc(wd, j0 * 128, nch, k * 512), nch)
                for c4 in range(4):
                    yp = PS[c4]
                    for jj in range(nch):
                        j = j0 + jj
                        mm(S, yp.ap, wdt.ap[:, jj * 512 + c4 * 128: jj * 512 + (c4 + 1) * 128], aT.ap[:, j * 512:(j + 1) * 512],
                           j == 0, j == 43, [wdt.buf, aT.buf], [yp.buf])
            for c4 in range(4):
                resid_add(PS[c4], 4 * k + c4, cnt)
                cnt += 1
        for t in range(4):
            r0 = g * 512 + t * 128
            outs.append(S.add("sp", lambda q, t=t, r0=r0: q.dma_start(out=xo[r0:r0 + 128, :], in_=xres.ap[:, t * 2048:(t + 1) * 2048]),
                              reads=[xres.buf], dma_sem=xres.buf))
        if gb3 is not None:
            norm_transpose(P, C, xt, load_g(gb3), hT, 4, tmp)
            outs.extend(T["store_h"](S, g, hT))
        S.barrier()
    ar.release(mB)
    return outs


HEADS_HG = ([0, 1, 4, 5, 6, 10, 11, 12], [2, 3, 7, 8, 9, 13, 14, 15])
PAIRS = [[0, 1], [2, 3], [4, 5], [6, 7]]


def head_slot(h):
    for hg in range(2):
        if h in HEADS_HG[hg]:
            return hg * 8 + HEADS_HG[hg].index(h)
    raise ValueError(h)


def build_fused(depth=DEPTH):
    P = Prog()
    S, ar, nc = P.S, P.ar, P.nc
    H2 = SEQ // 2
    x = P.din("x", [SEQ, D_MODEL])
    xh = P.din("xh", [H2, D_MODEL])
    sel = P.din("sel", [128, 2])
    gmix = P.din("gmix", [DEPTH, 128, D_MODEL])
    gffn = P.din("gffn", [DEPTH, 128, D_MODEL])
    wqkv = P.din("wqkv", [DEPTH, D_MODEL, 3072])
    gcols = P.din("gcols", [DEPTH, 128, 12])
    mstrip = P.din("mstrip", [3, 128, 4352])
    dstrip = P.din("dstrip", [3, 128, 768])
    sbmask = P.din("sbmask", [128, 896])
    esel = P.din("esel", [16, 2048])
    cst = P.din("cst", [128, 512])
    wg = P.din("wg", [DEPTH, D_MODEL, 3 * D_MODEL])
    wb = P.din("wb", [DEPTH, D_MODEL, D_MODEL])
    wo = P.din("wo", [DEPTH, D_MODEL, D_MODEL])
    wgu = P.din("wgu", [DEPTH, D_MODEL, 2 * D_FF])
    wd = P.din("wd", [DEPTH, D_FF, D_MODEL])
    xo = P.dout("xo", [H2, D_MODEL])
    qTs = P.dscratch("qTs", [8, 128, SEQ], BF16)
    kTs = P.dscratch("kTs", [8, 128, SEQ], BF16)
    vs = P.dscratch("vs", [SEQ, 1024], BF16)
    x1h = nc.dram_tensor("x1h", [H2, D_MODEL], F32)
    occ = [[[nc.dram_tensor(f"occ{l}_{hf}_{q}", [256, H2], BF16) for q in range(4)] for hf in range(2)] for l in range(depth)]
    ogc = [[[nc.dram_tensor(f"ogc{l}_{hf}_{q}", [512, H2], BF16) for q in range(4)] for hf in range(2)] for l in range(depth)]
    hcc = [nc.dram_tensor(f"hcc{j}", [256, H2], BF16) for j in range(8)]
    hgc = [nc.dram_tensor(f"hgc{j}", [512, H2], BF16) for j in range(8)]
    C = load_consts(P, cst)
    selt = ar.alloc(2, F32, "selt")
    S.add("sp", lambda g: g.dma_start(out=selt.ap, in_=sel), writes=[selt.buf], dma_sem=selt.buf)
    ccn = [0]

    def allgather(pairs):
        S.barrier()
        for (src, dst) in pairs:
            k = ccn[0]
            ccn[0] += 1
            op = S.add("pool", lambda g, src=src, dst=dst: g.collective_compute(
                "AllGather", ALU.bypass, replica_groups=PAIRS, ins=[src.ap().opt()], outs=[dst.ap().opt()]),
                cc_sem=P.cc_sems[0])
            op.sig = (P.cc_sems[0], k + 1)
        S.barrier()

    final = []
    for l in range(depth):
        def store_o(S_, hh, OT, l=l):
            ops = []
            for hf in range(2):
                dst = occ[l][hf][hh // 2].ap()[(hh % 2) * 128:(hh % 2 + 1) * 128, :]
                ops.append(S_.add("pool", lambda q, hf=hf, dst=dst: q.dma_start(out=dst, in_=OT.ap[:, hf * H2:(hf + 1) * H2]),
                                  reads=[OT.buf], dma_sem=OT.buf))
            return ops

        def load_hT(S_, g, h):
            rk, c0 = g // 4, (g % 4) * 512
            for j in range(8):
                S_.add("sp", lambda q, j=j: q.dma_start(
                    out=h.ap.rearrange("p (f t) -> p f t", f=16)[:, 2 * j:2 * j + 2, :],
                    in_=hgc[j].ap()[rk * 256:(rk + 1) * 256, c0:c0 + 512].rearrange("(f p) t -> p f t", p=128)),
                    writes=[h.buf], dma_sem=h.buf)

        TA = dict(x=x, gmix=gmix[l], wqkv=wqkv[l], gcols=gcols[l], mstrip=mstrip, dstrip=dstrip,
                  sbmask=sbmask, esel=esel, qTs=qTs, kTs=kTs, vs=vs, store_o=store_o, load_hT=(None if l == 0 else load_hT))
        emit_A(P, C, TA)
        allgather([(occ[l][hf][q], ogc[l][hf][q]) for hf in range(2) for q in range(4)])

        def load_oT(S_, g, oTg, mT, l=l):
            for hf, dstt in ((0, oTg), (1, mT)):
                d3 = dstt.ap.rearrange("p (h t) -> p h t", h=16)
                for q in range(4):
                    for rk in range(2):
                        s0 = rk * 8 + 2 * q
                        S_.add("sp", lambda e, hf=hf, q=q, rk=rk, s0=s0, d3=d3: e.dma_start(
                            out=d3[:, s0:s0 + 2, :],
                            in_=ogc[l][hf][q].ap()[rk * 256:(rk + 1) * 256, g * 512:(g + 1) * 512].rearrange("(h p) t -> p h t", p=128)),
                            writes=[dstt.buf], dma_sem=dstt.buf)
            S_.add("dve", lambda v: v.tensor_scalar(out=oTg.ap, in0=oTg.ap, scalar1=selt.ap[:, 0:1], scalar2=None, op0=ALU.mult),
                   reads=[oTg.buf, selt.buf], writes=[oTg.buf])
            S_.add("dve", lambda v: v.scalar_tensor_tensor(out=oTg.ap, in0=mT.ap, scalar=selt.ap[:, 1:2], in1=oTg.ap,
                                                           op0=ALU.mult, op1=ALU.add),
                   reads=[oTg.buf, mT.buf, selt.buf], writes=[oTg.buf])

        def store_h(S_, g, hT):
            ops = []
            for j in range(8):
                ops.append(S_.add("pool", lambda q, j=j: q.dma_start(
                    out=hcc[j].ap()[:, g * 512:(g + 1) * 512].rearrange("(f p) t -> p f t", p=128),
                    in_=hT.ap.rearrange("p (f t) -> p f t", f=16)[:, 2 * j:2 * j + 2, :]),
                    reads=[hT.buf], dma_sem=hT.buf))
            return ops

        last = (l == depth - 1)
        TB = dict(x=(xh if l == 0 else x1h.ap()), gmix=gmix[l], gffn=gffn[l], wg=wg[l], wb=wb[l], wo=wo[l], wgu=wgu[l], wd=wd[l],
                  xo=(xo if last else x1h.ap()), load_oT=load_oT, slot=head_slot,
                  next_gmix=(None if last else gmix[l + 1]), store_h=store_h)
        outs = emit_B(P, C, TB)
        if last:
            final = outs
        else:
            allgather([(hcc[j], hgc[j]) for j in range(8)])
    S.final_waits = list(final)
    return P.finish()


def kernel(x, g_mix, w_in, q_gain, k_gain, w_branch, w_out, g_ffn, w_gu, w_down, rel_bias, _depth=DEPTH):
    f = lambda a: np.ascontiguousarray(np.asarray(a, dtype=np.float32))
    x, g_mix, w_in, q_gain, k_gain = f(x), f(g_mix), f(w_in), f(q_gain), f(k_gain)
    w_branch, w_out, g_ffn, w_gu, w_down, rel_bias = f(w_branch), f(w_out), f(g_ffn), f(w_gu), f(w_down), f(rel_bias)
    rep = lambda g: np.ascontiguousarray(np.broadcast_to(g[:, None, :], (DEPTH, 128, D_MODEL)))
    gm, gf = rep(g_mix), rep(g_ffn)
    wg = np.ascontiguousarray(w_in[:, :, 3 * D_MODEL:])
    per_hg = []
    for hg in range(2):
        lay = [host_A_inputs(x[0], g_mix[l], w_in[l], q_gain[l], k_gain[l], rel_bias, hg)[0] for l in range(DEPTH)]
        per_hg.append(dict(wqkv=np.stack([a["wqkv"] for a in lay]), gcols=np.stack([a["gcols"] for a in lay]),
                           mstrip=lay[0]["mstrip"], dstrip=lay[0]["dstrip"]))
    shared = dict(gmix=gm, gffn=gf, sbmask=host_sbmask(), esel=host_esel(), cst=host_consts(),
                  wg=wg, wb=w_branch, wo=w_out, wgu=w_gu, wd=w_down)
    in_maps = []
    for b in range(BATCH):
        for r in range(2):
            selv = np.zeros((128, 2), np.float32)
            selv[:, r] = 1.0
            m = dict(shared)
            m.update(per_hg[r])
            m.update(x=x[b], xh=np.ascontiguousarray(x[b, r * 2048:(r + 1) * 2048]), sel=selv)
            in_maps.append(m)
    nc = build_fused(_depth)
    res = run_bass_kernel_spmd(nc, in_maps, core_ids=list(range(8)))
    out = np.empty_like(x)
    for b in range(BATCH):
        for r in range(2):
            out[b, r * 2048:(r + 1) * 2048] = np.asarray(res.results[b * 2 + r]["xo"])
    return out
```

```python
import contextlib
import math
import numpy as np
import concourse.bass as bass
import concourse.mybir as mybir
from concourse.bass_utils import run_bass_kernel_spmd

F32 = mybir.dt.float32
BF16 = mybir.dt.bfloat16
ALU = mybir.AluOpType
AF = mybir.ActivationFunctionType
AX = mybir.AxisListType

D_MODEL = 2048
SEQ = 4096
BATCH = 4
DEPTH = 2
HD = 128
D_FF = 5632
NEG = -30000.0
SCALE = HD ** -0.5
RMS_EPS = 1e-6
N_BUCKETS = 32
MAX_DISTANCE = 2048
DILS = ((128, 1), (512, 4), (2048, 16))

ENGS = ("pe", "act", "dve", "pool", "sp")


class Buf:
    __slots__ = ("name", "writer", "readers", "dreaders", "sem", "semcnt")

    def __init__(self, name):
        self.name = name
        self.writer = None
        self.readers = {}
        self.dreaders = []
        self.sem = None
        self.semcnt = 0


class Op:
    __slots__ = ("eng", "fn", "deps", "is_dma", "sig", "sembuf", "needs_sig", "inc")

    def __init__(self, eng, fn, is_dma):
        self.eng = eng
        self.fn = fn
        self.deps = []
        self.is_dma = is_dma
        self.sig = None
        self.sembuf = None
        self.needs_sig = False
        self.inc = 16


class Sched:
    def __init__(self, nc, sems):
        self.nc = nc
        self.sems = sems
        self.dma_pool = [[s_, 0] for s_ in sems["dma"]]
        self.dma_bufs = []
        self.streams = {e: [] for e in ENGS}
        self.final_waits = []
        self.pending_dma = []
        self.barrier_deps = {e: [] for e in ENGS}

    def buf(self, name=None):
        return Buf(name)

    def add(self, eng, fn, reads=(), writes=(), dma_sem=None, extra_deps=(), cc_sem=None):
        op = Op(eng, fn, dma_sem is not None or cc_sem is not None)
        if cc_sem is not None:
            op.sig = (cc_sem, 1)
            op.inc = 1
        if dma_sem is not None:
            b = dma_sem
            if b.sem is None:
                if not self.dma_pool:
                    raise RuntimeError("out of DMA semaphores")
                b.sem = self.dma_pool.pop()
                self.dma_bufs.append(b)
            b.sem[1] += 16
            op.sig = (b.sem[0], b.sem[1])
            op.sembuf = b
        deps = []
        same = lambda d: (not op.is_dma) and (not d.is_dma) and d.eng == eng
        for b in reads:
            w = b.writer
            if w is not None and not (same(w) and eng == "pe"):
                deps.append(w)
        for b in writes:
            w = b.writer
            if w is not None and not same(w):
                deps.append(w)
            for r in b.readers.values():
                if not same(r):
                    deps.append(r)
            deps.extend(b.dreaders)
        deps.extend(extra_deps)
        if self.barrier_deps[eng]:
            deps.extend(self.barrier_deps[eng])
            self.barrier_deps[eng] = []
        seen = set()
        for d in deps:
            if d is op or id(d) in seen:
                continue
            seen.add(id(d))
            op.deps.append(d)
            d.needs_sig = True
        for b in reads:
            if op.is_dma:
                b.dreaders.append(op)
            else:
                b.readers[eng] = op
        for b in writes:
            b.writer = op
            b.readers = {}
            b.dreaders = []
        self.streams[eng].append(op)
        if op.is_dma:
            self.pending_dma.append(op)
        return op

    def barrier(self):
        lasts = []
        for e in ENGS:
            st = [o for o in self.streams[e] if not o.is_dma]
            if st:
                lasts.append(st[-1])
        lasts.extend(self.pending_dma)
        self.pending_dma = []
        for b in self.dma_bufs:
            self.dma_pool.append(b.sem)
            b.sem = None
        self.dma_bufs = []
        for e in ENGS:
            self.barrier_deps[e] = list(lasts)

    def emit(self):
        nc = self.nc
        sems = self.sems
        cnt = {e: 0 for e in ENGS}
        for e in ENGS:
            for op in self.streams[e]:
                if (not op.is_dma) and op.needs_sig:
                    cnt[e] += 1
                    ep = cnt[e] // 30000
                    op.sig = (sems[e][ep], cnt[e] - ep * 30000 + (1 if ep else 0))
        handles = {"pe": "tensor", "act": "scalar", "dve": "vector", "pool": "gpsimd", "sp": "sync"}
        with nc.Block() as block:
            for e in ENGS:
                ops = self.streams[e]
                if not ops and not (e == "sp" and self.final_waits):
                    continue

                def body(eng, ops=ops, e=e):
                    waited = {}
                    for op in ops:
                        for d in op.deps:
                            sem, val = d.sig
                            k = id(sem)
                            if waited.get(k, 0) >= val:
                                continue
                            eng.wait_ge(sem, val)
                            waited[k] = val
                        ins = op.fn(eng)
                        if op.is_dma:
                            ins.then_inc(op.sig[0], op.inc)
                        elif op.needs_sig:
                            ins.then_inc(op.sig[0], 1)
                    if e == "sp":
                        for d in self.final_waits:
                            sem, val = d.sig
                            if waited.get(id(sem), 0) >= val:
                                continue
                            eng.wait_ge(sem, val)
                            waited[id(sem)] = val

                getattr(block, handles[e])(body)


class Tile:
    __slots__ = ("ap", "buf")

    def __init__(self, ap, buf):
        self.ap = ap
        self.buf = buf

    def __getitem__(self, k):
        return self.ap[k]


class Arena:
    def __init__(self, nc, es, S, nbytes, name="arena"):
        self.t = es.enter_context(nc.sbuf_tensor(name, [128, nbytes // 4], F32))
        self.S = S
        self.off = 0
        self.cap = nbytes

    def alloc(self, ncols, dt, name=None):
        esz = 4 if dt == F32 else 2
        nb = (ncols * esz + 31) // 32 * 32
        if self.off + nb > self.cap:
            raise RuntimeError(f"arena overflow allocating {name}: {self.off}+{nb}>{self.cap}")
        v = self.t[:, self.off // 4:(self.off + nb) // 4]
        if dt != F32:
            v = v.bitcast(dt)
        v = v[:, 0:ncols]
        self.off += nb
        return Tile(v, self.S.buf(name))

    def mark(self):
        return self.off

    def release(self, m):
        self.off = m


def t5_bucket_np(dist):
    max_exact = N_BUCKETS // 2
    d = np.maximum(dist, 0)
    df = np.maximum(d, 1).astype(np.float32)
    large = max_exact + (np.log(df / np.float32(max_exact)) / np.float32(math.log(MAX_DISTANCE / max_exact))
                         * np.float32(N_BUCKETS - max_exact)).astype(np.int32)
    large = np.minimum(large, N_BUCKETS - 1)
    return np.where(d < max_exact, d, large)


class Prog:
    def __init__(self, arena_bytes=207 * 1024):
        self.nc = bass.Bass("TRN2", target_bir_lowering=False)
        self.es = contextlib.ExitStack()
        nc, es = self.nc, self.es
        sems = {e: [es.enter_context(nc.semaphore(f"s_{e}{k}")) for k in range(4)] for e in ("pe", "act", "dve", "pool")}
        self.cc_sems = [es.enter_context(nc.semaphore(f"cc{i}")) for i in range(6)]
        sems["dma"] = [es.enter_context(nc.semaphore(f"d{i}")) for i in range(76)]
        self.S = Sched(nc, sems)
        self.ar = Arena(nc, es, self.S, arena_bytes)
        self.ps = []
        for i in range(8):
            t = es.enter_context(nc.psum_tensor(f"ps{i}", [128, 512], F32))
            self.ps.append(Tile(t[:], self.S.buf(f"ps{i}")))

    def din(self, name, shape, dt=F32):
        return self.nc.dram_tensor(name, list(shape), dt, kind="ExternalInput").ap()

    def dout(self, name, shape, dt=F32):
        return self.nc.dram_tensor(name, list(shape), dt, kind="ExternalOutput").ap()

    def dscratch(self, name, shape, dt, debug=False):
        if debug:
            return self.nc.dram_tensor(name, list(shape), dt, kind="ExternalOutput").ap()
        return self.nc.dram_tensor(name, list(shape), dt).ap()

    def finish(self):
        self.S.emit()
        self.es.close()
        return self.nc


def load_consts(P, cst_in):
    S, ar = P.S, P.ar
    cb = ar.alloc(4 * 128, BF16, "cstb")
    S.add("pool", lambda g: g.dma_start(out=cb.ap, in_=cst_in[:, 0:512]), writes=[cb.buf], dma_sem=cb.buf)
    cf = ar.alloc(128, F32, "identf")
    S.add("sp", lambda g: g.dma_start(out=cf.ap, in_=cst_in[:, 0:128]), writes=[cf.buf], dma_sem=cf.buf)
    eps = ar.alloc(1, F32, "eps")
    S.add("dve", lambda v: v.memset(eps.ap, RMS_EPS), writes=[eps.buf])
    return dict(eps=eps, cb=cb, identb=cb.ap[:, 0:128], onesb=cb.ap[:, 128:256], negones=cb.ap[:, 256:384],
                neguincl=cb.ap[:, 384:512], cf=cf, identf=cf.ap)


def host_consts():
    c = np.zeros((128, 512), np.float32)
    c[:, 0:128] = np.eye(128)
    c[:, 128:256] = 1.0
    c[:, 256:384] = -1.0
    j = np.arange(128)[:, None]
    k = np.arange(128)[None, :]
    c[:, 384:512] = -(j >= k).astype(np.float32)
    return c


def norm_transpose(P, C, x_tiles, gb, hT, ntiles, tmp):
    S = P.S
    T = ntiles * 128
    for i, xt in enumerate(x_tiles):
        xs = tmp["x"][i % 2]
        hb = tmp["hb"][i % 2]
        junk = tmp["junk"]
        ssq = tmp["ssq"][i % 2]
        rstd = tmp["rstd"][i % 2]
        if isinstance(xt, Tile):
            xs = xt
        else:
            S.add("sp", lambda g, xs=xs, xt=xt: g.dma_start(out=xs.ap, in_=xt), writes=[xs.buf], dma_sem=xs.buf)
        S.add("act", lambda a, xs=xs, ssq=ssq: a.activation(out=junk.ap, in_=xs.ap, func=AF.Square, accum_out=ssq.ap),
              reads=[xs.buf], writes=[junk.buf, ssq.buf])
        S.add("act", lambda a, ssq=ssq, rstd=rstd: a.activation(out=rstd.ap, in_=ssq.ap, func=AF.Sqrt,
                                                               scale=1.0 / D_MODEL, bias=C["eps"].ap),
              reads=[ssq.buf, C["eps"].buf], writes=[rstd.buf])
        S.add("dve", lambda v, rstd=rstd: v.reciprocal(out=rstd.ap, in_=rstd.ap), reads=[rstd.buf], writes=[rstd.buf])
        S.add("dve", lambda v, xs=xs, rstd=rstd, hb=hb: v.scalar_tensor_tensor(
            out=hb.ap, in0=xs.ap, scalar=rstd.ap, in1=gb.ap, op0=ALU.mult, op1=ALU.mult),
            reads=[xs.buf, rstd.buf, gb.buf], writes=[hb.buf])
        for half in range(2):
            pb = P.ps[6 + half]
            pv = pb.ap.bitcast(BF16)
            for k in range(8):
                fc = half * 8 + k
                S.add("pe", lambda t, pv=pv, k=k, fc=fc, hb=hb: t.transpose(pv[:, k * 128:(k + 1) * 128],
                                                                           hb.ap[:, fc * 128:(fc + 1) * 128], C["identb"]),
                      reads=[hb.buf, C["cb"].buf], writes=[pb.buf])
            dst = hT.ap.rearrange("p (f t) -> p f t", f=16)[:, half * 8:(half + 1) * 8, i * 128:(i + 1) * 128]
            src = pv.rearrange("p (f t) -> p f t", f=8)
            if half == 0:
                S.add("act", lambda a, dst=dst, src=src: a.copy(out=dst, in_=src), reads=[pb.buf], writes=[hT.buf])
            else:
                S.add("dve", lambda v, dst=dst, src=src: v.tensor_copy(out=dst, in_=src), reads=[pb.buf], writes=[hT.buf])


def norm_tmp(P, need_x=True):
    ar = P.ar
    return dict(x=[ar.alloc(2048, F32, f"xs{i}") for i in range(2)] if need_x else [None, None],
                hb=[ar.alloc(2048, BF16, f"hb{i}") for i in range(2)],
                junk=ar.alloc(2048, BF16, "junk"),
                ssq=[ar.alloc(1, F32, f"ssq{i}") for i in range(2)],
                rstd=[ar.alloc(1, F32, f"rstd{i}") for i in range(2)])


def mm(S, out, lhsT, rhs, start, stop, reads, writes):
    return S.add("pe", lambda t: t.matmul(out, lhsT, rhs, start=start, stop=stop), reads=reads, writes=writes)


def emit_A(P, C, T, phases=("a0", "sb", "moba", "dil")):
    S, ar = P.S, P.ar
    debug = False
    x, gmix, wqkv, gcols = T["x"], T["gmix"], T["wqkv"], T["gcols"]
    mstrip, dstrip, sbmask, esel = T["mstrip"], T["dstrip"], T["sbmask"], T["esel"]
    qTs, kTs, vs = T["qTs"], T["kTs"], T["vs"]
    mA = ar.mark()
    gb = ar.alloc(2048, F32, "gb")
    S.add("sp", lambda g: g.dma_start(out=gb.ap, in_=gmix), writes=[gb.buf], dma_sem=gb.buf)
    gc = ar.alloc(12, F32, "gc")
    S.add("sp", lambda g: g.dma_start(out=gc.ap, in_=gcols), writes=[gc.buf], dma_sem=gc.buf)
    S.add("act", lambda a: a.mul(out=gc.ap[:, 0:6], in_=gc.ap[:, 0:6], mul=SCALE), reads=[gc.buf], writes=[gc.buf])
    stores = []
    m0 = ar.mark()
    if "a0" in phases:
        W = ar.alloc(16 * 3072, BF16, "W")
        for fc in range(16):
            S.add("pool", lambda g, fc=fc: g.dma_start(out=W.ap[:, fc * 3072:(fc + 1) * 3072],
                                                       in_=wqkv[fc * 128:(fc + 1) * 128, :]),
                  writes=[W.buf], dma_sem=W.buf)
        tmp = norm_tmp(P)
        hT = [ar.alloc(16 * 512, BF16, f"hT{i}") for i in range(2)]
        sq = [ar.alloc(512, BF16, f"sq{i}") for i in range(2)]
        rs = [ar.alloc(512, F32, f"rs{i}") for i in range(2)]
        qst = [ar.alloc(512, BF16, f"qst{i}") for i in range(4)]
        vst = [ar.alloc(1024, BF16, f"vst{i}") for i in range(2)]
        for g in range(8):
            h = hT[g % 2]
            if T.get("load_hT") is not None:
                T["load_hT"](S, g, h)
            else:
                norm_transpose(P, C, [x[(g * 4 + i) * 128:(g * 4 + i + 1) * 128, :] for i in range(4)], gb, h, 4, tmp)

            def proj(j):
                pq = P.ps[j % 3]
                for fc in range(16):
                    mm(S, pq.ap, W.ap[:, fc * 3072 + j * 128: fc * 3072 + (j + 1) * 128], h.ap[:, fc * 512:(fc + 1) * 512],
                       fc == 0, fc == 15, [W.buf, h.buf], [pq.buf])

            def post(j):
                hh = j % 8
                isq = j < 8
                pq = P.ps[j % 3]
                st = qst[j % 4]
                if hh < 2:
                    if isq:
                        S.add("act", lambda a: a.mul(out=st.ap, in_=pq.ap, mul=SCALE), reads=[pq.buf], writes=[st.buf])
                    else:
                        S.add("act", lambda a: a.copy(out=st.ap, in_=pq.ap), reads=[pq.buf], writes=[st.buf])
                else:
                    s = hh - 2
                    col = gc.ap[:, s:s + 1] if isq else gc.ap[:, 6 + s:7 + s]
                    sqt = sq[j % 2]
                    rst = rs[j % 2]
                    pss = P.ps[3 + j % 2]
                    S.add("act", lambda a: a.activation(out=sqt.ap, in_=pq.ap, func=AF.Square), reads=[pq.buf], writes=[sqt.buf])
                    mm(S, pss.ap, C["onesb"], sqt.ap, True, True, [C["cb"].buf, sqt.buf], [pss.buf])
                    S.add("act", lambda a: a.activation(out=rst.ap, in_=pss.ap, func=AF.Sqrt, scale=1.0 / HD,
                                                        bias=C["eps"].ap), reads=[pss.buf, C["eps"].buf], writes=[rst.buf])
                    S.add("dve", lambda v: v.reciprocal(out=rst.ap, in_=rst.ap), reads=[rst.buf], writes=[rst.buf])
                    S.add("dve", lambda v: v.scalar_tensor_tensor(out=st.ap, in0=pq.ap, scalar=col, in1=rst.ap,
                                                                  op0=ALU.mult, op1=ALU.mult),
                          reads=[pq.buf, rst.buf, gc.buf], writes=[st.buf])
                dst = (qTs if isq else kTs)[hh, :, g * 512:(g + 1) * 512]
                stores.append(S.add("pool", lambda q: q.dma_start(out=dst, in_=st.ap), reads=[st.buf], dma_sem=st.buf))

            for j in range(16):
                proj(j)
                if j >= 1:
                    post(j - 1)
            post(15)
            for i in range(4):
                vt = vst[i % 2]
                for half in range(2):
                    pv = P.ps[half]
                    for fc in range(16):
                        mm(S, pv.ap, h.ap[:, fc * 512 + i * 128: fc * 512 + (i + 1) * 128],
                           W.ap[:, fc * 3072 + 2048 + half * 512: fc * 3072 + 2048 + (half + 1) * 512],
                           fc == 0, fc == 15, [W.buf, h.buf], [pv.buf])
                    if half == 0:
                        S.add("act", lambda a, vt=vt, pv=pv: a.copy(out=vt.ap[:, 0:512], in_=pv.ap), reads=[pv.buf], writes=[vt.buf])
                    else:
                        S.add("dve", lambda v, vt=vt, pv=pv: v.tensor_copy(out=vt.ap[:, 512:1024], in_=pv.ap), reads=[pv.buf], writes=[vt.buf])
                r0 = (g * 4 + i) * 128
                stores.append(S.add("pool", lambda q, vt=vt, r0=r0: q.dma_start(out=vs[r0:r0 + 128, :], in_=vt.ap),
                                    reads=[vt.buf], dma_sem=vt.buf))
        S.barrier()
    ar.release(m0)
    outs = []
    PS = P.ps

    def load_qk(hh):
        QT = ar.alloc(SEQ, BF16, "QT")
        KT = ar.alloc(SEQ, BF16, "KT")
        S.add("sp", lambda g: g.dma_start(out=QT.ap, in_=qTs[hh]), writes=[QT.buf], dma_sem=QT.buf)
        S.add("sp", lambda g: g.dma_start(out=KT.ap, in_=kTs[hh]), writes=[KT.buf], dma_sem=KT.buf)
        return QT, KT

    def load_v(hh, r):
        V = ar.alloc(SEQ, BF16, f"V{r}")
        nsub = SEQ // r // 128
        for c in range(r):
            src = vs[c::r, hh * 128:(hh + 1) * 128].rearrange("(n k) d -> k n d", k=128)
            dst = V.ap[:, c * nsub * 128:(c + 1) * nsub * 128].rearrange("p (n d) -> p n d", d=128)
            S.add("sp", lambda g, src=src, dst=dst: g.dma_start(out=dst, in_=src), writes=[V.buf], dma_sem=V.buf)
        return V

    def store_o(hh, OT):
        outs.extend(T["store_o"](S, hh, OT))

    def sb_head(hh):
        m = ar.mark()
        QT, KT = load_qk(hh)
        V = load_v(hh, 1)
        maskb = ar.alloc(896, BF16, "sbmaskb")
        S.add("pool", lambda g: g.dma_start(out=maskb.ap, in_=sbmask), writes=[maskb.buf], dma_sem=maskb.buf)
        OT = ar.alloc(SEQ, BF16, "OT")
        e1 = [ar.alloc(512, F32, f"e1{i}") for i in range(2)]
        spb = [ar.alloc(512, BF16, f"spb{i}") for i in range(2)]
        a3 = [ar.alloc(512, F32, f"a3{i}") for i in range(2)]
        PT = [ar.alloc(512, BF16, f"PT{i}") for i in range(2)]
        carry = ar.alloc(512, F32, "carry")
        zA = [PS[0], PS[1]]
        aP = [PS[2], PS[3]]
        Tp = PS[4]
        acc = PS[5]
        cb = C["cb"].buf
        for g in range(8):
            tiles = list(range(4 * g + 3, -1, -1))
            qs = QT.ap[:, g * 512:(g + 1) * 512]
            last = len(tiles) - 1

            def st1(idx):
                mt = tiles[idx]
                par = idx % 2
                ing = mt >= 4 * g
                kt = KT.ap[:, mt * 128:(mt + 1) * 128]
                mm(S, zA[par].ap, kt, qs, True, not ing, [KT.buf, QT.buf], [zA[par].buf])
                if ing:
                    off = 384 - 128 * (mt - 4 * g)
                    mm(S, zA[par].ap, C["identb"], maskb.ap[:, off:off + 512], False, True, [cb, maskb.buf], [zA[par].buf])
                S.add("act", lambda a: a.activation(out=e1[par].ap, in_=zA[par].ap, func=AF.Exp),
                      reads=[zA[par].buf], writes=[e1[par].buf])
                S.add("act", lambda a: a.activation(out=spb[par].ap, in_=e1[par].ap, func=AF.Ln, bias=1.0),
                      reads=[e1[par].buf], writes=[spb[par].buf])

            def st2(idx):
                mt = tiles[idx]
                par = idx % 2
                ing = mt >= 4 * g
                kt = KT.ap[:, mt * 128:(mt + 1) * 128]
                mm(S, aP[par].ap, kt, qs, True, False, [KT.buf, QT.buf], [aP[par].buf])
                if ing:
                    off = 384 - 128 * (mt - 4 * g)
                    mm(S, aP[par].ap, C["identb"], maskb.ap[:, off:off + 512], False, False, [cb, maskb.buf], [aP[par].buf])
                mm(S, aP[par].ap, C["neguincl"], spb[par].ap, False, True, [cb, spb[par].buf], [aP[par].buf])
                mm(S, Tp.ap, C["negones"], spb[par].ap, True, True, [cb, spb[par].buf], [Tp.buf])
                if idx == 0:
                    S.add("act", lambda a: a.activation(out=PT[par].ap, in_=aP[par].ap, func=AF.Exp),
                          reads=[aP[par].buf], writes=[PT[par].buf])
                else:
                    S.add("dve", lambda v: v.tensor_tensor(out=a3[par].ap, in0=aP[par].ap, in1=carry.ap, op=ALU.add),
                          reads=[aP[par].buf, carry.buf], writes=[a3[par].buf])
                    S.add("act", lambda a: a.activation(out=PT[par].ap, in_=a3[par].ap, func=AF.Exp),
                          reads=[a3[par].buf], writes=[PT[par].buf])
                mm(S, acc.ap, V.ap[:, mt * 128:(mt + 1) * 128], PT[par].ap, idx == 0, idx == last,
                   [V.buf, PT[par].buf], [acc.buf])
                if idx == 0:
                    S.add("dve", lambda v: v.tensor_copy(out=carry.ap, in_=Tp.ap), reads=[Tp.buf], writes=[carry.buf])
                elif idx < last:
                    S.add("dve", lambda v: v.tensor_tensor(out=carry.ap, in0=carry.ap, in1=Tp.ap, op=ALU.add),
                          reads=[Tp.buf, carry.buf], writes=[carry.buf])

            for idx in range(len(tiles) + 1):
                if idx < len(tiles):
                    st1(idx)
                if idx >= 1:
                    st2(idx - 1)
            S.add("act", lambda a, g=g: a.copy(out=OT.ap[:, g * 512:(g + 1) * 512], in_=acc.ap),
                  reads=[acc.buf], writes=[OT.buf])
        store_o(hh, OT)
        S.barrier()
        ar.release(m)

    def moba_head(hh, s):
        m = ar.mark()
        QT, KT = load_qk(hh)
        V = load_v(hh, 1)
        strip = ar.alloc(4352, BF16, "mstrip")
        S.add("pool", lambda g: g.dma_start(out=strip.ap, in_=mstrip[s]), writes=[strip.buf], dma_sem=strip.buf)
        eselb = ar.alloc(2048, BF16, "eselb")
        S.add("pool", lambda g: g.dma_start(out=eselb.ap[0:16, :], in_=esel), writes=[eselb.buf], dma_sem=eselb.buf)
        selT = ar.alloc(SEQ, BF16, "selT")
        OT = ar.alloc(SEQ, BF16, "OT")
        ksum = ar.alloc(16, F32, "ksum")
        kmb = ar.alloc(16, BF16, "kmb")
        gm = ar.alloc(16, F32, "gm")
        mx = ar.alloc(8, F32, "mx")
        sel01 = ar.alloc(16, F32, "sel01")
        rd = ar.alloc(256, F32, "rd")
        PT = [ar.alloc(256, BF16, f"PTm{i}") for i in range(2)]
        cb = C["cb"].buf
        S.add("dve", lambda v: v.reduce_sum(out=ksum.ap, in_=KT.ap.rearrange("p (n k) -> p n k", k=256), axis=AX.X),
              reads=[KT.buf], writes=[ksum.buf])
        S.add("act", lambda a: a.mul(out=kmb.ap, in_=ksum.ap, mul=1.0 / 256), reads=[ksum.buf], writes=[kmb.buf])
        S.add("dve", lambda v: v.memset(gm.ap, -1e30), writes=[gm.buf])
        pg = PS[6]
        pt = PS[7]
        for i in range(2, 32):
            ob = i // 2
            mm(S, pg.ap[:, 0:16], QT.ap[:, i * 128:(i + 1) * 128], kmb.ap, True, True, [QT.buf, kmb.buf], [pg.buf])
            S.add("dve", lambda v, ob=ob: v.tensor_copy(out=gm.ap[:, 0:ob], in_=pg.ap[:, 0:ob]), reads=[pg.buf], writes=[gm.buf])
            S.add("dve", lambda v: v.max(out=mx.ap, in_=gm.ap), reads=[gm.buf], writes=[mx.buf])
            S.add("dve", lambda v: v.tensor_scalar(out=sel01.ap, in0=gm.ap, scalar1=mx.ap[:, 2:3], scalar2=-1.0,
                                                   op0=ALU.is_ge, op1=ALU.add),
                  reads=[gm.buf, mx.buf], writes=[sel01.buf])
            S.add("pe", lambda t: t.transpose(pt.ap[0:16, 0:128], sel01.ap, C["identf"]),
                  reads=[sel01.buf, C["cf"].buf], writes=[pt.buf])
            S.add("act", lambda a, i=i: a.mul(out=selT.ap[0:16, i * 128:(i + 1) * 128], in_=pt.ap[0:16, 0:128], mul=-NEG),
                  reads=[pt.buf], writes=[selT.buf])
        for G in range(16):
            kts = list(range(0, 2 * G + 2))
            last = len(kts) - 1
            acc = PS[4 + G % 2]
            den = PS[2 + G % 2]
            qs = QT.ap[:, G * 256:(G + 1) * 256]

            def stS(idx):
                kt = kts[idx]
                n = kt // 2
                par = idx % 2
                Sp = PS[par]
                mm(S, Sp.ap[:, 0:256], KT.ap[:, kt * 128:(kt + 1) * 128], qs, True, False, [KT.buf, QT.buf], [Sp.buf])
                off = 256 * G - 128 * kt + 128
                mm(S, Sp.ap[:, 0:256], C["identb"], strip.ap[:, off:off + 256], False, n == G, [cb, strip.buf], [Sp.buf])
                if n < G:
                    mm(S, Sp.ap[:, 0:256], eselb.ap[0:16, n * 128:(n + 1) * 128], selT.ap[0:16, G * 256:(G + 1) * 256],
                       False, True, [eselb.buf, selT.buf], [Sp.buf])
                S.add("act", lambda a: a.activation(out=PT[par].ap, in_=Sp.ap[:, 0:256], func=AF.Exp),
                      reads=[Sp.buf], writes=[PT[par].buf])

            def stPV(idx):
                kt = kts[idx]
                par = idx % 2
                mm(S, acc.ap[:, 0:256], V.ap[:, kt * 128:(kt + 1) * 128], PT[par].ap, idx == 0, idx == last,
                   [V.buf, PT[par].buf], [acc.buf])
                mm(S, den.ap[:, 0:256], C["onesb"], PT[par].ap, idx == 0, idx == last, [cb, PT[par].buf], [den.buf])

            for idx in range(len(kts) + 1):
                if idx < len(kts):
                    stS(idx)
                if idx >= 1:
                    stPV(idx - 1)
            S.add("dve", lambda v, den=den: v.reciprocal(out=rd.ap, in_=den.ap[:, 0:256]), reads=[den.buf], writes=[rd.buf])
            S.add("dve", lambda v, acc=acc, G=G: v.tensor_tensor(out=OT.ap[:, G * 256:(G + 1) * 256], in0=acc.ap[:, 0:256],
                                                                in1=rd.ap, op=ALU.mult),
                  reads=[acc.buf, rd.buf], writes=[OT.buf])
        store_o(hh, OT)
        S.barrier()
        ar.release(m)

    def dil_head(hh, s):
        m = ar.mark()
        QT, KT = load_qk(hh)
        Vr = [load_v(hh, r) for (_, r) in DILS]
        dsb = ar.alloc(768, BF16, "dsb")
        S.add("pool", lambda g: g.dma_start(out=dsb.ap, in_=dstrip[s]), writes=[dsb.buf], dma_sem=dsb.buf)
        ACC = ar.alloc(2 * SEQ, F32, "ACC")
        OT = ar.alloc(SEQ, BF16, "OT")
        PT = [ar.alloc(256, BF16, f"PTd{i}") for i in range(2)]
        cb = C["cb"].buf
        ACC3 = ACC.ap.rearrange("p (a t) -> p a t", a=2)
        for pi, (w, r) in enumerate(DILS):
            nsub = SEQ // r // 128
            V = Vr[pi]
            items = [(c, n) for c in range(r) for n in range(nsub)]

            def sub(T_, c, n):
                return T_.ap[:, c + r * 128 * n: c + r * 128 * n + r * 127 + 1: r]

            def stS(idx):
                c, n = items[idx]
                par = idx % 2
                Sp = PS[par]
                qsl = sub(QT, c, n)
                mm(S, Sp.ap[:, 128:256], sub(KT, c, n), qsl, True, False, [KT.buf, QT.buf], [Sp.buf])
                mm(S, Sp.ap[:, 128:256], C["identb"], dsb.ap[:, pi * 256:pi * 256 + 128], False, True, [cb, dsb.buf], [Sp.buf])
                if n >= 1:
                    mm(S, Sp.ap[:, 0:128], sub(KT, c, n - 1), qsl, True, False, [KT.buf, QT.buf], [Sp.buf])
                    mm(S, Sp.ap[:, 0:128], C["identb"], dsb.ap[:, pi * 256 + 128:pi * 256 + 256], False, True,
                       [cb, dsb.buf], [Sp.buf])
                    S.add("act", lambda a: a.activation(out=PT[par].ap, in_=Sp.ap[:, 0:256], func=AF.Exp),
                          reads=[Sp.buf], writes=[PT[par].buf])
                else:
                    S.add("act", lambda a: a.activation(out=PT[par].ap[:, 128:256], in_=Sp.ap[:, 128:256], func=AF.Exp),
                          reads=[Sp.buf], writes=[PT[par].buf])

            def stPV(idx):
                c, n = items[idx]
                par = idx % 2
                acc = PS[2 + idx % 2]
                ti = c * nsub + n
                vc = V.ap[:, ti * 128:(ti + 1) * 128]
                rb = [V.buf, PT[par].buf]
                if n >= 1:
                    vp = V.ap[:, (ti - 1) * 128:ti * 128]
                    mm(S, acc.ap[:, 0:128], vp, PT[par].ap[:, 0:128], True, False, rb, [acc.buf])
                    mm(S, acc.ap[:, 0:128], vc, PT[par].ap[:, 128:256], False, True, rb, [acc.buf])
                    mm(S, acc.ap[:, 128:256], C["onesb"], PT[par].ap[:, 0:128], True, False, [cb, PT[par].buf], [acc.buf])
                    mm(S, acc.ap[:, 128:256], C["onesb"], PT[par].ap[:, 128:256], False, True, [cb, PT[par].buf], [acc.buf])
                else:
                    mm(S, acc.ap[:, 0:128], vc, PT[par].ap[:, 128:256], True, True, rb, [acc.buf])
                    mm(S, acc.ap[:, 128:256], C["onesb"], PT[par].ap[:, 128:256], True, True, [cb, PT[par].buf], [acc.buf])
                dst = ACC3[:, :, c + r * 128 * n: c + r * 128 * n + r * 127 + 1: r]
                src = acc.ap[:, 0:256].rearrange("p (a t) -> p a t", a=2)
                if pi == 0:
                    S.add("dve", lambda v: v.tensor_copy(out=dst, in_=src), reads=[acc.buf], writes=[ACC.buf])
                else:
                    S.add("dve", lambda v: v.tensor_tensor(out=dst, in0=dst, in1=src, op=ALU.add),
                          reads=[acc.buf, ACC.buf], writes=[ACC.buf])

            for idx in range(len(items) + 1):
                if idx < len(items):
                    stS(idx)
                if idx >= 1:
                    stPV(idx - 1)
        S.add("dve", lambda v: v.reciprocal(out=ACC.ap[:, SEQ:2 * SEQ], in_=ACC.ap[:, SEQ:2 * SEQ]), reads=[ACC.buf], writes=[ACC.buf])
        S.add("dve", lambda v: v.tensor_tensor(out=OT.ap, in0=ACC.ap[:, 0:SEQ], in1=ACC.ap[:, SEQ:2 * SEQ], op=ALU.mult),
              reads=[ACC.buf], writes=[OT.buf])
        store_o(hh, OT)
        S.barrier()
        ar.release(m)

    if "sb" in phases:
        sb_head(0)
        sb_head(1)
    if "moba" in phases:
        for s_ in range(3):
            moba_head(2 + s_, s_)
    if "dil" in phases:
        for s_ in range(3):
            dil_head(5 + s_, s_)
    ar.release(mA)
    return outs


def host_A_inputs(x_b, g_mix_l, w_in_l, q_gain_l, k_gain_l, rel_bias, hg):
    sb = [0, 1] if hg == 0 else [2, 3]
    mo = [4, 5, 6] if hg == 0 else [7, 8, 9]
    di = [10, 11, 12] if hg == 0 else [13, 14, 15]
    hs = sb + mo + di
    cols = lambda base: [w_in_l[:, base + h * 128: base + (h + 1) * 128] for h in hs]
    wqkv = np.ascontiguousarray(np.concatenate(cols(0) + cols(2048) + cols(4096), axis=1))
    soft = [h - 4 for h in hs[2:]]
    gcols = np.ascontiguousarray(np.concatenate([q_gain_l[soft].T, k_gain_l[soft].T], axis=1).astype(np.float32))
    negf = np.float32(NEG)
    kk = np.arange(128)[:, None]
    u = np.arange(4352)[None, :]
    dist = u - 128 - kk
    bidx = t5_bucket_np(dist)
    mstrip = np.stack([np.where(dist >= 0, rel_bias[bidx, h - 4], negf) for h in mo]).astype(np.float32)
    u2 = np.arange(256)[None, :]
    delta = u2 - kk
    valid = (delta >= 0) & (delta <= 128)
    ds = []
    for h in di:
        per = []
        for (w, r) in DILS:
            per.append(np.where(valid, rel_bias[t5_bucket_np(delta * r), h - 4], negf))
        ds.append(np.concatenate(per, axis=1))
    dstrip = np.stack(ds).astype(np.float32)
    return dict(x=np.ascontiguousarray(x_b), gmix=np.ascontiguousarray(np.broadcast_to(g_mix_l[None, :], (128, D_MODEL))),
                wqkv=wqkv, gcols=gcols, mstrip=np.ascontiguousarray(mstrip), dstrip=np.ascontiguousarray(dstrip),
                sbmask=host_sbmask(), esel=host_esel(), cst=host_consts()), hs


def host_sbmask():
    kk = np.arange(128)[:, None]
    u = np.arange(896)[None, :] - 384
    return np.where(u > kk, 0.0, NEG).astype(np.float32)


def host_esel():
    e = np.zeros((16, 2048), np.float32)
    for n in range(16):
        e[n, n * 128:(n + 1) * 128] = 1.0
    return e


GORDER = (0, 1, 2, 3)
BR_HEADS = ([0, 1, 2, 3], [4, 5, 6, 7, 8, 9], [10, 11, 12, 13, 14, 15])


def emit_B(P, C, T):
    S, ar = P.S, P.ar
    PS = P.ps
    stage = 2
    x, gmix, gffn = T["x"], T["gmix"], T["gffn"]
    wg, wb, wo, wgu, wd, xo = T["wg"], T["wb"], T["wo"], T["wgu"], T["wd"], T["xo"]
    slot = T["slot"]
    mB = ar.mark()
    gbx = ar.alloc(2048, F32, "gbx")

    def load_g(src):
        S.add("sp", lambda g: g.dma_start(out=gbx.ap, in_=src), writes=[gbx.buf], dma_sem=gbx.buf)
        return gbx
    tmp = norm_tmp(P, need_x=False)
    xres = ar.alloc(4 * 2048, F32, "xres")
    hT = ar.alloc(16 * 512, BF16, "hT")
    wbufs = [ar.alloc(16 * 512, BF16, f"wbuf{i}") for i in range(3)]
    macc = [ar.alloc(512, F32, f"macc{i}") for i in range(4)]
    gs = [ar.alloc(512, F32, f"gs{i}") for i in range(2)]
    tt = [ar.alloc(512, F32, f"tt{i}") for i in range(2)]
    yT = [ar.alloc(512, F32, f"yT{i}") for i in range(2)]
    wbb = ar.alloc(16 * 512, BF16, "wbb")
    gb3 = T.get("next_gmix")
    mR = ar.mark()
    wcount = [0]
    outs = []

    wcache = T.get("wcache")
    cbufs = {}
    blk = [0]

    def wload(src3, nch, t=None):
        if t is None:
            t = wbufs[wcount[0] % 3]
            wcount[0] += 1
        dst = t.ap[:, 0:nch * 512].rearrange("p (c n) -> p c n", n=512)
        if wcache is None:
            S.add("pool", lambda g: g.dma_start(out=dst, in_=src3), writes=[t.buf], dma_sem=t.buf)
            return t
        b = blk[0]
        blk[0] += 1
        flat = t.ap[:, 0:nch * 512]
        if cur_g[0] == 0:
            cbufs[b] = S.buf(f"wc{b}")
            S.add("pool", lambda g: g.dma_start(out=dst, in_=src3), writes=[t.buf], dma_sem=t.buf)
            S.add("sp", lambda g: g.dma_start(out=wcache[b, :, 0:nch * 512], in_=flat), reads=[t.buf], writes=[cbufs[b]], dma_sem=t.buf)
        else:
            S.add("sp", lambda g: g.dma_start(out=flat, in_=wcache[b, :, 0:nch * 512]), reads=[cbufs[b]], writes=[t.buf], dma_sem=t.buf)
        return t

    def wsrc(w, r0, nch, c0):
        return w[r0:r0 + nch * 128, c0:c0 + 512].rearrange("(c p) n -> p c n", p=128)

    def resid_add(yps, oc, cnt):
        y = yT[cnt % 2]
        S.add("act", lambda a: a.copy(out=y.ap, in_=yps.ap), reads=[yps.buf], writes=[y.buf])
        pt = PS[4 + cnt % 2]
        for t in range(4):
            S.add("pe", lambda e, t=t: e.transpose(pt.ap[:, t * 128:(t + 1) * 128], y.ap[:, t * 128:(t + 1) * 128], C["identf"]),
                  reads=[y.buf, C["cf"].buf], writes=[pt.buf])
        dst = xres.ap.rearrange("p (t f) -> p t f", t=4)[:, :, oc * 128:(oc + 1) * 128]
        src = pt.ap.rearrange("p (t f) -> p t f", t=4)
        S.add("dve", lambda v: v.tensor_tensor(out=dst, in0=dst, in1=src, op=ALU.add), reads=[pt.buf, xres.buf], writes=[xres.buf])

    cur_g = [0]
    for g in GORDER:
        cur_g[0] = g
        blk[0] = 0
        ar.release(mR)
        oTg = ar.alloc(16 * 512, BF16, "oTg")
        mT = ar.alloc(16 * 512, BF16, "mT")
        for t in range(4):
            r0 = g * 512 + t * 128
            S.add("sp", lambda q, t=t, r0=r0: q.dma_start(out=xres.ap[:, t * 2048:(t + 1) * 2048], in_=x[r0:r0 + 128, :]),
                  writes=[xres.buf], dma_sem=xres.buf)
        T["load_oT"](S, g, oTg, mT)
        xt = [Tile(xres.ap[:, t * 2048:(t + 1) * 2048], xres.buf) for t in range(4)]
        norm_transpose(P, C, xt, load_g(gmix), hT, 4, tmp)
        cnt = 0
        for k in range(4):
            wbt = wload(wsrc(wb, 0, 16, k * 512), 16, wbb)
            for br in range(3):
                wgt = wload(wsrc(wg, 0, 16, br * 2048 + k * 512), 16)
                for c4 in range(4):
                    cc = 4 * k + c4
                    gp = PS[cnt % 2]
                    pp = PS[2 + cnt % 2]
                    g_ = gs[cnt % 2]
                    t_ = tt[cnt % 2]
                    cnt += 1
                    for fc in range(16):
                        mm(S, gp.ap, wgt.ap[:, fc * 512 + c4 * 128: fc * 512 + (c4 + 1) * 128], hT.ap[:, fc * 512:(fc + 1) * 512],
                           fc == 0, fc == 15, [wgt.buf, hT.buf], [gp.buf])
                    hl = BR_HEADS[br]
                    for i, h in enumerate(hl):
                        mm(S, pp.ap, wbt.ap[:, h * 512 + c4 * 128: h * 512 + (c4 + 1) * 128], oTg.ap[:, slot(h) * 512:(slot(h) + 1) * 512],
                           i == 0, i == len(hl) - 1, [wbt.buf, oTg.buf], [pp.buf])
                    S.add("act", lambda a, g_=g_, gp=gp: a.activation(out=g_.ap, in_=gp.ap, func=AF.Sigmoid),
                          reads=[gp.buf], writes=[g_.buf])
                    if br == 0:
                        S.add("dve", lambda v, g_=g_, pp=pp, c4=c4: v.tensor_tensor(out=macc[c4].ap, in0=pp.ap, in1=g_.ap, op=ALU.mult),
                              reads=[pp.buf, g_.buf], writes=[macc[c4].buf])
                    else:
                        S.add("dve", lambda v, g_=g_, pp=pp, t_=t_: v.tensor_tensor(out=t_.ap, in0=pp.ap, in1=g_.ap, op=ALU.mult),
                              reads=[pp.buf, g_.buf], writes=[t_.buf])
                        if br == 1:
                            S.add("dve", lambda v, t_=t_, c4=c4: v.tensor_tensor(out=macc[c4].ap, in0=macc[c4].ap, in1=t_.ap, op=ALU.add),
                                  reads=[t_.buf, macc[c4].buf], writes=[macc[c4].buf])
                        else:
                            S.add("dve", lambda v, t_=t_, c4=c4, cc=cc, mT=mT: v.tensor_tensor(out=mT.ap[:, cc * 512:(cc + 1) * 512], in0=macc[c4].ap,
                                                                                     in1=t_.ap, op=ALU.add),
                                  reads=[t_.buf, macc[c4].buf], writes=[mT.buf])
        cnt = 0
        for k in range(4):
            wot = wload(wsrc(wo, 0, 16, k * 512), 16)
            for c4 in range(4):
                oc = 4 * k + c4
                yp = PS[cnt % 2]
                for cc in range(16):
                    mm(S, yp.ap, wot.ap[:, cc * 512 + c4 * 128: cc * 512 + (c4 + 1) * 128], mT.ap[:, cc * 512:(cc + 1) * 512],
                       cc == 0, cc == 15, [wot.buf, mT.buf], [yp.buf])
                resid_add(yp, oc, cnt)
                cnt += 1
        S.barrier()
        if stage == 1:
            for t in range(4):
                r0 = g * 512 + t * 128
                outs.append(S.add("sp", lambda q, t=t, r0=r0: q.dma_start(out=xo[r0:r0 + 128, :], in_=xres.ap[:, t * 2048:(t + 1) * 2048]),
                                  reads=[xres.buf], dma_sem=xres.buf))
            S.barrier()
            continue
        ar.release(mR)
        aT = ar.alloc(44 * 512, BF16, "aT")
        norm_transpose(P, C, xt, load_g(gffn), hT, 4, tmp)
        cnt = 0
        for j in range(11):
            wgt = wload(wsrc(wgu, 0, 16, j * 512), 16)
            wut = wload(wsrc(wgu, 0, 16, D_FF + j * 512), 16)
            for c4 in range(4):
                jc = 4 * j + c4
                gp = PS[cnt % 2]
                up = PS[2 + cnt % 2]
                g_ = gs[cnt % 2]
                cnt += 1
                for fc in range(16):
                    mm(S, gp.ap, wgt.ap[:, fc * 512 + c4 * 128: fc * 512 + (c4 + 1) * 128], hT.ap[:, fc * 512:(fc + 1) * 512],
                       fc == 0, fc == 15, [wgt.buf, hT.buf], [gp.buf])
                for fc in range(16):
                    mm(S, up.ap, wut.ap[:, fc * 512 + c4 * 128: fc * 512 + (c4 + 1) * 128], hT.ap[:, fc * 512:(fc + 1) * 512],
                       fc == 0, fc == 15, [wut.buf, hT.buf], [up.buf])
                S.add("act", lambda a, g_=g_, gp=gp: a.activation(out=g_.ap, in_=gp.ap, func=AF.Silu), reads=[gp.buf], writes=[g_.buf])
                S.add("dve", lambda v, g_=g_, up=up, jc=jc, aT=aT: v.tensor_tensor(out=aT.ap[:, jc * 512:(jc + 1) * 512], in0=up.ap, in1=g_.ap, op=ALU.mult),
                      reads=[up.buf, g_.buf], writes=[aT.buf])
        cnt = 0
        for k in range(4):
            for (j0, nch) in ((0, 16), (16, 16), (32, 12)):
                wdt = wload(wsrc(wd, j0 * 128, nch, k * 512), nch)
                for c4 in range(4):
                    yp = PS[c4]
                    for jj in range(nch):
                        j = j0 + jj
                        mm(S, yp.ap, wdt.ap[:, jj * 512 + c4 * 128: jj * 512 + (c4 + 1) * 128], aT.ap[:, j * 512:(j + 1) * 512],
                           j == 0, j == 43, [wdt.buf, aT.buf], [yp.buf])
            for c4 in range(4):
                resid_add(PS[c4], 4 * k + c4, cnt)
                cnt += 1
        for t in range(4):
            r0 = g * 512 + t * 128
            outs.append(S.add("sp", lambda q, t=t, r0=r0: q.dma_start(out=xo[r0:r0 + 128, :], in_=xres.ap[:, t * 2048:(t + 1) * 2048]),
                              reads=[xres.buf], dma_sem=xres.buf))
        if gb3 is not None:
            norm_transpose(P, C, xt, load_g(gb3), hT, 4, tmp)
            outs.extend(T["store_h"](S, g, hT))
        S.barrier()
    ar.release(mB)
    return outs


HEADS_HG = ([0, 1, 4, 5, 6, 10, 11, 12], [2, 3, 7, 8, 9, 13, 14, 15])
PAIRS = [[0, 1], [2, 3], [4, 5], [6, 7]]


def head_slot(h):
    for hg in range(2):
        if h in HEADS_HG[hg]:
            return hg * 8 + HEADS_HG[hg].index(h)
    raise ValueError(h)


def build_fused(depth=DEPTH):
    P = Prog()
    S, ar, nc = P.S, P.ar, P.nc
    H2 = SEQ // 2
    x = P.din("x", [SEQ, D_MODEL])
    xh = P.din("xh", [H2, D_MODEL])
    sel = P.din("sel", [128, 2])
    gmix = P.din("gmix", [DEPTH, 128, D_MODEL])
    gffn = P.din("gffn", [DEPTH, 128, D_MODEL])
    wqkv = P.din("wqkv", [DEPTH, D_MODEL, 3072])
    gcols = P.din("gcols", [DEPTH, 128, 12])
    mstrip = P.din("mstrip", [3, 128, 4352])
    dstrip = P.din("dstrip", [3, 128, 768])
    sbmask = P.din("sbmask", [128, 896])
    esel = P.din("esel", [16, 2048])
    cst = P.din("cst", [128, 512])
    wg = P.din("wg", [DEPTH, D_MODEL, 3 * D_MODEL])
    wb = P.din("wb", [DEPTH, D_MODEL, D_MODEL])
    wo = P.din("wo", [DEPTH, D_MODEL, D_MODEL])
    wgu = P.din("wgu", [DEPTH, D_MODEL, 2 * D_FF])
    wd = P.din("wd", [DEPTH, D_FF, D_MODEL])
    xo = P.dout("xo", [H2, D_MODEL])
    qTs = P.dscratch("qTs", [8, 128, SEQ], BF16)
    kTs = P.dscratch("kTs", [8, 128, SEQ], BF16)
    vs = P.dscratch("vs", [SEQ, 1024], BF16)
    x1h = nc.dram_tensor("x1h", [H2, D_MODEL], F32)
    occ = [[[nc.dram_tensor(f"occ{l}_{hf}_{q}", [256, H2], BF16) for q in range(4)] for hf in range(2)] for l in range(depth)]
    ogc = [[[nc.dram_tensor(f"ogc{l}_{hf}_{q}", [512, H2], BF16) for q in range(4)] for hf in range(2)] for l in range(depth)]
    wcache = nc.dram_tensor("wcache", [56, 128, 8192], BF16).ap()
    hcc = [nc.dram_tensor(f"hcc{j}", [256, H2], BF16) for j in range(8)]
    hgc = [nc.dram_tensor(f"hgc{j}", [512, H2], BF16) for j in range(8)]
    C = load_consts(P, cst)
    selt = ar.alloc(2, F32, "selt")
    S.add("sp", lambda g: g.dma_start(out=selt.ap, in_=sel), writes=[selt.buf], dma_sem=selt.buf)
    ccn = [0]

    def allgather(pairs):
        S.barrier()
        for (src, dst) in pairs:
            k = ccn[0]
            ccn[0] += 1
            op = S.add("pool", lambda g, src=src, dst=dst: g.collective_compute(
                "AllGather", ALU.bypass, replica_groups=PAIRS, ins=[src.ap().opt()], outs=[dst.ap().opt()]),
                cc_sem=P.cc_sems[0])
            op.sig = (P.cc_sems[0], k + 1)
        S.barrier()

    final = []
    for l in range(depth):
        def store_o(S_, hh, OT, l=l):
            ops = []
            for hf in range(2):
                dst = occ[l][hf][hh // 2].ap()[(hh % 2) * 128:(hh % 2 + 1) * 128, :]
                ops.append(S_.add("pool", lambda q, hf=hf, dst=dst: q.dma_start(out=dst, in_=OT.ap[:, hf * H2:(hf + 1) * H2]),
                                  reads=[OT.buf], dma_sem=OT.buf))
            return ops

        def load_hT(S_, g, h):
            rk, c0 = g // 4, (g % 4) * 512
            for j in range(8):
                S_.add("sp", lambda q, j=j: q.dma_start(
                    out=h.ap.rearrange("p (f t) -> p f t", f=16)[:, 2 * j:2 * j + 2, :],
                    in_=hgc[j].ap()[rk * 256:(rk + 1) * 256, c0:c0 + 512].rearrange("(f p) t -> p f t", p=128)),
                    writes=[h.buf], dma_sem=h.buf)

        TA = dict(x=x, gmix=gmix[l], wqkv=wqkv[l], gcols=gcols[l], mstrip=mstrip, dstrip=dstrip,
                  sbmask=sbmask, esel=esel, qTs=qTs, kTs=kTs, vs=vs, store_o=store_o, load_hT=(None if l == 0 else load_hT))
        emit_A(P, C, TA)
        allgather([(occ[l][hf][q], ogc[l][hf][q]) for hf in range(2) for q in range(4)])

        def load_oT(S_, g, oTg, mT, l=l):
            for hf, dstt in ((0, oTg), (1, mT)):
                d3 = dstt.ap.rearrange("p (h t) -> p h t", h=16)
                for q in range(4):
                    for rk in range(2):
                        s0 = rk * 8 + 2 * q
                        S_.add("sp", lambda e, hf=hf, q=q, rk=rk, s0=s0, d3=d3: e.dma_start(
                            out=d3[:, s0:s0 + 2, :],
                            in_=ogc[l][hf][q].ap()[rk * 256:(rk + 1) * 256, g * 512:(g + 1) * 512].rearrange("(h p) t -> p h t", p=128)),
                            writes=[dstt.buf], dma_sem=dstt.buf)
            S_.add("dve", lambda v: v.tensor_scalar(out=oTg.ap, in0=oTg.ap, scalar1=selt.ap[:, 0:1], scalar2=None, op0=ALU.mult),
                   reads=[oTg.buf, selt.buf], writes=[oTg.buf])
            S_.add("dve", lambda v: v.scalar_tensor_tensor(out=oTg.ap, in0=mT.ap, scalar=selt.ap[:, 1:2], in1=oTg.ap,
                                                           op0=ALU.mult, op1=ALU.add),
                   reads=[oTg.buf, mT.buf, selt.buf], writes=[oTg.buf])

        def store_h(S_, g, hT):
            ops = []
            for j in range(8):
                ops.append(S_.add("pool", lambda q, j=j: q.dma_start(
                    out=hcc[j].ap()[:, g * 512:(g + 1) * 512].rearrange("(f p) t -> p f t", p=128),
                    in_=hT.ap.rearrange("p (f t) -> p f t", f=16)[:, 2 * j:2 * j + 2, :]),
                    reads=[hT.buf], dma_sem=hT.buf))
            return ops

        last = (l == depth - 1)
        TB = dict(x=(xh if l == 0 else x1h.ap()), gmix=gmix[l], gffn=gffn[l], wg=wg[l], wb=wb[l], wo=wo[l], wgu=wgu[l], wd=wd[l],
                  xo=(xo if last else x1h.ap()), load_oT=load_oT, slot=head_slot,
                  next_gmix=(None if last else gmix[l + 1]), store_h=store_h, wcache=wcache)
        outs = emit_B(P, C, TB)
        if last:
            final = outs
        else:
            allgather([(hcc[j], hgc[j]) for j in range(8)])
    S.final_waits = list(final)
    return P.finish()


def kernel(x, g_mix, w_in, q_gain, k_gain, w_branch, w_out, g_ffn, w_gu, w_down, rel_bias, _depth=DEPTH):
    f = lambda a: np.ascontiguousarray(np.asarray(a, dtype=np.float32))
    x, g_mix, w_in, q_gain, k_gain = f(x), f(g_mix), f(w_in), f(q_gain), f(k_gain)
    w_branch, w_out, g_ffn, w_gu, w_down, rel_bias = f(w_branch), f(w_out), f(g_ffn), f(w_gu), f(w_down), f(rel_bias)
    rep = lambda g: np.ascontiguousarray(np.broadcast_to(g[:, None, :], (DEPTH, 128, D_MODEL)))
    gm, gf = rep(g_mix), rep(g_ffn)
    wg = np.ascontiguousarray(w_in[:, :, 3 * D_MODEL:])
    per_hg = []
    for hg in range(2):
        lay = [host_A_inputs(x[0], g_mix[l], w_in[l], q_gain[l], k_gain[l], rel_bias, hg)[0] for l in range(DEPTH)]
        per_hg.append(dict(wqkv=np.stack([a["wqkv"] for a in lay]), gcols=np.stack([a["gcols"] for a in lay]),
                           mstrip=lay[0]["mstrip"], dstrip=lay[0]["dstrip"]))
    shared = dict(gmix=gm, gffn=gf, sbmask=host_sbmask(), esel=host_esel(), cst=host_consts(),
                  wg=wg, wb=w_branch, wo=w_out, wgu=w_gu, wd=w_down)
    in_maps = []
    for b in range(BATCH):
        for r in range(2):
            selv = np.zeros((128, 2), np.float32)
            selv[:, r] = 1.0
            m = dict(shared)
            m.update(per_hg[r])
            m.update(x=x[b], xh=np.ascontiguousarray(x[b, r * 2048:(r + 1) * 2048]), sel=selv)
            in_maps.append(m)
    nc = build_fused(_depth)
    res = run_bass_kernel_spmd(nc, in_maps, core_ids=list(range(8)))
    out = np.empty_like(x)
    for b in range(BATCH):
        for r in range(2):
            out[b, r * 2048:(r + 1) * 2048] = np.asarray(res.results[b * 2 + r]["xo"])
    return out
```

```python
import contextlib
import math
import numpy as np
import concourse.bass as bass
import concourse.mybir as mybir
from concourse.bass_utils import run_bass_kernel_spmd

F32 = mybir.dt.float32
BF16 = mybir.dt.bfloat16
ALU = mybir.AluOpType
AF = mybir.ActivationFunctionType
AX = mybir.AxisListType

D_MODEL = 2048
SEQ = 4096
BATCH = 4
DEPTH = 2
HD = 128
D_FF = 5632
NEG = -30000.0
SCALE = HD ** -0.5
RMS_EPS = 1e-6
N_BUCKETS = 32
MAX_DISTANCE = 2048
DILS = ((128, 1), (512, 4), (2048, 16))

ENGS = ("pe", "act", "dve", "pool", "sp")


class Buf:
    __slots__ = ("name", "writer", "readers", "dreaders", "sem", "semcnt")

    def __init__(self, name):
        self.name = name
        self.writer = None
        self.readers = {}
        self.dreaders = []
        self.sem = None
        self.semcnt = 0


class Op:
    __slots__ = ("eng", "fn", "deps", "is_dma", "sig", "sembuf", "needs_sig", "inc", "batch")

    def __init__(self, eng, fn, is_dma):
        self.eng = eng
        self.fn = fn
        self.deps = []
        self.is_dma = is_dma
        self.sig = None
        self.sembuf = None
        self.needs_sig = False
        self.inc = 16
        self.batch = None


class Sched:
    def __init__(self, nc, sems):
        self.nc = nc
        self.sems = sems
        self.dma_pool = [[s_, 0] for s_ in sems["dma"]]
        self.dma_bufs = []
        self.streams = {e: [] for e in ENGS}
        self.final_waits = []
        self.pending_dma = []
        self.barrier_deps = {e: [] for e in ENGS}

    def buf(self, name=None):
        return Buf(name)

    def add(self, eng, fn, reads=(), writes=(), dma_sem=None, extra_deps=(), cc_sem=None, batch=None):
        op = Op(eng, fn, dma_sem is not None or cc_sem is not None)
        op.batch = batch
        if cc_sem is not None:
            op.sig = (cc_sem, 1)
            op.inc = 1
        if dma_sem is not None:
            b = dma_sem
            if b.sem is None:
                if not self.dma_pool:
                    raise RuntimeError("out of DMA semaphores")
                b.sem = self.dma_pool.pop()
                self.dma_bufs.append(b)
            b.sem[1] += 16
            op.sig = (b.sem[0], b.sem[1])
            op.sembuf = b
        deps = []
        same = lambda d: (not op.is_dma) and (not d.is_dma) and d.eng == eng
        for b in reads:
            w = b.writer
            if w is not None and not (same(w) and eng == "pe"):
                deps.append(w)
        for b in writes:
            w = b.writer
            if w is not None and not same(w) and not (batch is not None and w.batch == batch):
                deps.append(w)
            for r in b.readers.values():
                if not same(r):
                    deps.append(r)
            deps.extend(b.dreaders)
        deps.extend(extra_deps)
        if self.barrier_deps[eng]:
            deps.extend(self.barrier_deps[eng])
            self.barrier_deps[eng] = []
        seen = set()
        for d in deps:
            if d is op or id(d) in seen:
                continue
            seen.add(id(d))
            op.deps.append(d)
            d.needs_sig = True
        for b in reads:
            if op.is_dma:
                b.dreaders.append(op)
            else:
                b.readers[eng] = op
        for b in writes:
            b.writer = op
            b.readers = {}
            b.dreaders = []
        self.streams[eng].append(op)
        if op.is_dma:
            self.pending_dma.append(op)
        return op

    def barrier(self):
        lasts = []
        for e in ENGS:
            st = [o for o in self.streams[e] if not o.is_dma]
            if st:
                lasts.append(st[-1])
        lasts.extend(self.pending_dma)
        self.pending_dma = []
        for b in self.dma_bufs:
            self.dma_pool.append(b.sem)
            b.sem = None
        self.dma_bufs = []
        for e in ENGS:
            self.barrier_deps[e] = list(lasts)

    def emit(self):
        nc = self.nc
        sems = self.sems
        cnt = {e: 0 for e in ENGS}
        for e in ENGS:
            for op in self.streams[e]:
                if (not op.is_dma) and op.needs_sig:
                    cnt[e] += 1
                    ep = cnt[e] // 30000
                    op.sig = (sems[e][ep], cnt[e] - ep * 30000 + (1 if ep else 0))
        handles = {"pe": "tensor", "act": "scalar", "dve": "vector", "pool": "gpsimd", "sp": "sync"}
        with nc.Block() as block:
            for e in ENGS:
                ops = self.streams[e]
                if not ops and not (e == "sp" and self.final_waits):
                    continue

                def body(eng, ops=ops, e=e):
                    waited = {}
                    for op in ops:
                        for d in op.deps:
                            sem, val = d.sig
                            k = id(sem)
                            if waited.get(k, 0) >= val:
                                continue
                            eng.wait_ge(sem, val)
                            waited[k] = val
                        ins = op.fn(eng)
                        if op.is_dma:
                            ins.then_inc(op.sig[0], op.inc)
                        elif op.needs_sig:
                            ins.then_inc(op.sig[0], 1)
                    if e == "sp":
                        for d in self.final_waits:
                            sem, val = d.sig
                            if waited.get(id(sem), 0) >= val:
                                continue
                            eng.wait_ge(sem, val)
                            waited[id(sem)] = val

                getattr(block, handles[e])(body)


class Tile:
    __slots__ = ("ap", "buf")

    def __init__(self, ap, buf):
        self.ap = ap
        self.buf = buf

    def __getitem__(self, k):
        return self.ap[k]


class Arena:
    def __init__(self, nc, es, S, nbytes, name="arena"):
        self.t = es.enter_context(nc.sbuf_tensor(name, [128, nbytes // 4], F32))
        self.S = S
        self.off = 0
        self.cap = nbytes

    def alloc(self, ncols, dt, name=None):
        esz = 4 if dt == F32 else 2
        nb = (ncols * esz + 31) // 32 * 32
        if self.off + nb > self.cap:
            raise RuntimeError(f"arena overflow allocating {name}: {self.off}+{nb}>{self.cap}")
        v = self.t[:, self.off // 4:(self.off + nb) // 4]
        if dt != F32:
            v = v.bitcast(dt)
        v = v[:, 0:ncols]
        self.off += nb
        return Tile(v, self.S.buf(name))

    def mark(self):
        return self.off

    def release(self, m):
        self.off = m


def t5_bucket_np(dist):
    max_exact = N_BUCKETS // 2
    d = np.maximum(dist, 0)
    df = np.maximum(d, 1).astype(np.float32)
    large = max_exact + (np.log(df / np.float32(max_exact)) / np.float32(math.log(MAX_DISTANCE / max_exact))
                         * np.float32(N_BUCKETS - max_exact)).astype(np.int32)
    large = np.minimum(large, N_BUCKETS - 1)
    return np.where(d < max_exact, d, large)


class Prog:
    def __init__(self, arena_bytes=207 * 1024):
        self.nc = bass.Bass("TRN2", target_bir_lowering=False)
        self.es = contextlib.ExitStack()
        nc, es = self.nc, self.es
        sems = {e: [es.enter_context(nc.semaphore(f"s_{e}{k}")) for k in range(4)] for e in ("pe", "act", "dve", "pool")}
        self.cc_sems = [es.enter_context(nc.semaphore(f"cc{i}")) for i in range(6)]
        sems["dma"] = [es.enter_context(nc.semaphore(f"d{i}")) for i in range(76)]
        self.S = Sched(nc, sems)
        self.ar = Arena(nc, es, self.S, arena_bytes)
        self.ps = []
        for i in range(8):
            t = es.enter_context(nc.psum_tensor(f"ps{i}", [128, 512], F32))
            self.ps.append(Tile(t[:], self.S.buf(f"ps{i}")))

    def din(self, name, shape, dt=F32):
        return self.nc.dram_tensor(name, list(shape), dt, kind="ExternalInput").ap()

    def dout(self, name, shape, dt=F32):
        return self.nc.dram_tensor(name, list(shape), dt, kind="ExternalOutput").ap()

    def dscratch(self, name, shape, dt, debug=False):
        if debug:
            return self.nc.dram_tensor(name, list(shape), dt, kind="ExternalOutput").ap()
        return self.nc.dram_tensor(name, list(shape), dt).ap()

    def finish(self):
        self.S.emit()
        self.es.close()
        return self.nc


def load_consts(P, cst_in):
    S, ar = P.S, P.ar
    cb = ar.alloc(4 * 128, BF16, "cstb")
    S.add("pool", lambda g: g.dma_start(out=cb.ap, in_=cst_in[:, 0:512]), writes=[cb.buf], dma_sem=cb.buf)
    cf = ar.alloc(128, F32, "identf")
    S.add("sp", lambda g: g.dma_start(out=cf.ap, in_=cst_in[:, 0:128]), writes=[cf.buf], dma_sem=cf.buf)
    eps = ar.alloc(1, F32, "eps")
    S.add("dve", lambda v: v.memset(eps.ap, RMS_EPS), writes=[eps.buf])
    return dict(eps=eps, cb=cb, identb=cb.ap[:, 0:128], onesb=cb.ap[:, 128:256], negones=cb.ap[:, 256:384],
                neguincl=cb.ap[:, 384:512], cf=cf, identf=cf.ap)


def host_consts():
    c = np.zeros((128, 512), np.float32)
    c[:, 0:128] = np.eye(128)
    c[:, 128:256] = 1.0
    c[:, 256:384] = -1.0
    j = np.arange(128)[:, None]
    k = np.arange(128)[None, :]
    c[:, 384:512] = -(j >= k).astype(np.float32)
    return c


def norm_transpose(P, C, x_tiles, gb, hT, ntiles, tmp, dq="sp", t0=0):
    S = P.S
    T = ntiles * 128
    for i, xt in enumerate(x_tiles):
        xs = tmp["x"][i % 2]
        hb = tmp["hb"][i % 2]
        junk = tmp["junk"]
        ssq = tmp["ssq"][i % 2]
        rstd = tmp["rstd"][i % 2]
        if isinstance(xt, Tile):
            xs = xt
        else:
            S.add(dq, lambda g, xs=xs, xt=xt: g.dma_start(out=xs.ap, in_=xt), writes=[xs.buf], dma_sem=xs.buf)
        S.add("act", lambda a, xs=xs, ssq=ssq: a.activation(out=junk.ap, in_=xs.ap, func=AF.Square, accum_out=ssq.ap),
              reads=[xs.buf], writes=[junk.buf, ssq.buf])
        S.add("act", lambda a, ssq=ssq, rstd=rstd: a.activation(out=rstd.ap, in_=ssq.ap, func=AF.Sqrt,
                                                               scale=1.0 / D_MODEL, bias=C["eps"].ap),
              reads=[ssq.buf, C["eps"].buf], writes=[rstd.buf])
        S.add("dve", lambda v, rstd=rstd: v.reciprocal(out=rstd.ap, in_=rstd.ap), reads=[rstd.buf], writes=[rstd.buf])
        S.add("dve", lambda v, xs=xs, rstd=rstd, hb=hb: v.scalar_tensor_tensor(
            out=hb.ap, in0=xs.ap, scalar=rstd.ap, in1=gb.ap, op0=ALU.mult, op1=ALU.mult),
            reads=[xs.buf, rstd.buf, gb.buf], writes=[hb.buf])
        for half in range(2):
            pb = P.ps[6 + half]
            pv = pb.ap.bitcast(BF16)
            for k in range(8):
                fc = half * 8 + k
                S.add("pe", lambda t, pv=pv, k=k, fc=fc, hb=hb: t.transpose(pv[:, k * 128:(k + 1) * 128],
                                                                           hb.ap[:, fc * 128:(fc + 1) * 128], C["identb"]),
                      reads=[hb.buf, C["cb"].buf], writes=[pb.buf])
            dst = hT.ap.rearrange("p (f t) -> p f t", f=16)[:, half * 8:(half + 1) * 8, (t0 + i) * 128:(t0 + i + 1) * 128]
            src = pv.rearrange("p (f t) -> p f t", f=8)
            if half == 0:
                S.add("act", lambda a, dst=dst, src=src: a.copy(out=dst, in_=src), reads=[pb.buf], writes=[hT.buf])
            else:
                S.add("dve", lambda v, dst=dst, src=src: v.tensor_copy(out=dst, in_=src), reads=[pb.buf], writes=[hT.buf])


def norm_tmp(P, need_x=True):
    ar = P.ar
    return dict(x=[ar.alloc(2048, F32, f"xs{i}") for i in range(2)] if need_x else [None, None],
                hb=[ar.alloc(2048, BF16, f"hb{i}") for i in range(2)],
                junk=ar.alloc(2048, BF16, "junk"),
                ssq=[ar.alloc(1, F32, f"ssq{i}") for i in range(2)],
                rstd=[ar.alloc(1, F32, f"rstd{i}") for i in range(2)])


def mm(S, out, lhsT, rhs, start, stop, reads, writes):
    return S.add("pe", lambda t: t.matmul(out, lhsT, rhs, start=start, stop=stop), reads=reads, writes=writes)


def emit_A(P, C, T, phases=("a0", "sb", "moba", "dil")):
    S, ar = P.S, P.ar
    debug = False
    x, gmix, wqkv, gcols = T["x"], T["gmix"], T["wqkv"], T["gcols"]
    mstrip, dstrip, sbmask, esel = T["mstrip"], T["dstrip"], T["sbmask"], T["esel"]
    qTs, kTs, vs = T["qTs"], T["kTs"], T["vs"]
    mA = ar.mark()
    gb = ar.alloc(2048, F32, "gb")
    S.add("sp", lambda g: g.dma_start(out=gb.ap, in_=gmix), writes=[gb.buf], dma_sem=gb.buf)
    gc = ar.alloc(12, F32, "gc")
    S.add("sp", lambda g: g.dma_start(out=gc.ap, in_=gcols), writes=[gc.buf], dma_sem=gc.buf)
    S.add("act", lambda a: a.mul(out=gc.ap[:, 0:6], in_=gc.ap[:, 0:6], mul=SCALE), reads=[gc.buf], writes=[gc.buf])
    stores = []
    m0 = ar.mark()
    if "a0" in phases:
        W = ar.alloc(16 * 3072, BF16, "W")
        for fc in range(16):
            S.add("pool", lambda g, fc=fc: g.dma_start(out=W.ap[:, fc * 3072:(fc + 1) * 3072],
                                                       in_=wqkv[fc * 128:(fc + 1) * 128, :]),
                  writes=[W.buf], dma_sem=W.buf, batch="W")
        tmp = norm_tmp(P)
        hT = [ar.alloc(16 * 512, BF16, f"hT{i}") for i in range(2)]
        sq = [ar.alloc(512, BF16, f"sq{i}") for i in range(2)]
        rs = [ar.alloc(512, F32, f"rs{i}") for i in range(2)]
        qst = [ar.alloc(512, BF16, f"qst{i}") for i in range(4)]
        vst = [ar.alloc(1024, BF16, f"vst{i}") for i in range(2)]
        for g in range(8):
            h = hT[g % 2]
            def prep(gg):
                hh_ = hT[gg % 2]
                if T.get("load_hT") is not None:
                    T["load_hT"](S, gg, hh_)
                else:
                    norm_transpose(P, C, [x[(gg * 4 + i) * 128:(gg * 4 + i + 1) * 128, :] for i in range(4)], gb, hh_, 4, tmp)

            if g == 0:
                prep(0)

            def proj(j):
                pq = P.ps[j % 3]
                for fc in range(16):
                    mm(S, pq.ap, W.ap[:, fc * 3072 + j * 128: fc * 3072 + (j + 1) * 128], h.ap[:, fc * 512:(fc + 1) * 512],
                       fc == 0, fc == 15, [W.buf, h.buf], [pq.buf])

            def post(j):
                hh = j % 8
                isq = j < 8
                pq = P.ps[j % 3]
                st = qst[j % 4]
                if hh < 2:
                    if isq:
                        S.add("act", lambda a: a.mul(out=st.ap, in_=pq.ap, mul=SCALE), reads=[pq.buf], writes=[st.buf])
                    else:
                        S.add("act", lambda a: a.copy(out=st.ap, in_=pq.ap), reads=[pq.buf], writes=[st.buf])
                else:
                    s = hh - 2
                    col = gc.ap[:, s:s + 1] if isq else gc.ap[:, 6 + s:7 + s]
                    sqt = sq[j % 2]
                    rst = rs[j % 2]
                    pss = P.ps[3 + j % 2]
                    S.add("act", lambda a: a.activation(out=sqt.ap, in_=pq.ap, func=AF.Square), reads=[pq.buf], writes=[sqt.buf])
                    mm(S, pss.ap, C["onesb"], sqt.ap, True, True, [C["cb"].buf, sqt.buf], [pss.buf])
                    S.add("act", lambda a: a.activation(out=rst.ap, in_=pss.ap, func=AF.Sqrt, scale=1.0 / HD,
                                                        bias=C["eps"].ap), reads=[pss.buf, C["eps"].buf], writes=[rst.buf])
                    S.add("dve", lambda v: v.reciprocal(out=rst.ap, in_=rst.ap), reads=[rst.buf], writes=[rst.buf])
                    S.add("dve", lambda v: v.scalar_tensor_tensor(out=st.ap, in0=pq.ap, scalar=col, in1=rst.ap,
                                                                  op0=ALU.mult, op1=ALU.mult),
                          reads=[pq.buf, rst.buf, gc.buf], writes=[st.buf])
                dst = (qTs if isq else kTs)[hh, :, g * 512:(g + 1) * 512]
                stores.append(S.add("pool", lambda q: q.dma_start(out=dst, in_=st.ap), reads=[st.buf], dma_sem=st.buf))

            for j in range(16):
                proj(j)
                if j >= 1:
                    post(j - 1)
                if j == 8 and g + 1 < 8:
                    prep(g + 1)
            post(15)
            for i in range(4):
                vt = vst[i % 2]
                for half in range(2):
                    pv = P.ps[half]
                    for fc in range(16):
                        mm(S, pv.ap, h.ap[:, fc * 512 + i * 128: fc * 512 + (i + 1) * 128],
                           W.ap[:, fc * 3072 + 2048 + half * 512: fc * 3072 + 2048 + (half + 1) * 512],
                           fc == 0, fc == 15, [W.buf, h.buf], [pv.buf])
                    if half == 0:
                        S.add("act", lambda a, vt=vt, pv=pv: a.copy(out=vt.ap[:, 0:512], in_=pv.ap), reads=[pv.buf], writes=[vt.buf])
                    else:
                        S.add("dve", lambda v, vt=vt, pv=pv: v.tensor_copy(out=vt.ap[:, 512:1024], in_=pv.ap), reads=[pv.buf], writes=[vt.buf])
                r0 = (g * 4 + i) * 128
                stores.append(S.add("pool", lambda q, vt=vt, r0=r0: q.dma_start(out=vs[r0:r0 + 128, :], in_=vt.ap),
                                    reads=[vt.buf], dma_sem=vt.buf))
        S.barrier()
    ar.release(m0)
    outs = []
    PS = P.ps

    def load_qk(hh):
        QT = ar.alloc(SEQ, BF16, "QT")
        KT = ar.alloc(SEQ, BF16, "KT")
        S.add("sp", lambda g: g.dma_start(out=QT.ap, in_=qTs[hh]), writes=[QT.buf], dma_sem=QT.buf)
        S.add("sp", lambda g: g.dma_start(out=KT.ap, in_=kTs[hh]), writes=[KT.buf], dma_sem=KT.buf)
        return QT, KT

    def load_v(hh, r):
        V = ar.alloc(SEQ, BF16, f"V{r}")
        nsub = SEQ // r // 128
        for c in range(r):
            src = vs[c::r, hh * 128:(hh + 1) * 128].rearrange("(n k) d -> k n d", k=128)
            dst = V.ap[:, c * nsub * 128:(c + 1) * nsub * 128].rearrange("p (n d) -> p n d", d=128)
            S.add("sp", lambda g, src=src, dst=dst: g.dma_start(out=dst, in_=src), writes=[V.buf], dma_sem=V.buf, batch=("V", id(V)))
        return V

    def store_o(hh, OT):
        outs.extend(T["store_o"](S, hh, OT))

    def sb_head(hh):
        m = ar.mark()
        QT, KT = load_qk(hh)
        V = load_v(hh, 1)
        maskb = ar.alloc(896, BF16, "sbmaskb")
        S.add("pool", lambda g: g.dma_start(out=maskb.ap, in_=sbmask), writes=[maskb.buf], dma_sem=maskb.buf)
        OT = ar.alloc(SEQ, BF16, "OT")
        e1 = [ar.alloc(512, F32, f"e1{i}") for i in range(2)]
        spb = [ar.alloc(512, BF16, f"spb{i}") for i in range(2)]
        a3 = [ar.alloc(512, F32, f"a3{i}") for i in range(2)]
        PT = [ar.alloc(512, BF16, f"PT{i}") for i in range(2)]
        carry = ar.alloc(512, F32, "carry")
        zA = [PS[0], PS[1]]
        aP = [PS[2], PS[3]]
        Tp = PS[4]
        acc = PS[5]
        cb = C["cb"].buf
        for g in range(8):
            tiles = list(range(4 * g + 3, -1, -1))
            qs = QT.ap[:, g * 512:(g + 1) * 512]
            last = len(tiles) - 1

            def st1(idx):
                mt = tiles[idx]
                par = idx % 2
                ing = mt >= 4 * g
                kt = KT.ap[:, mt * 128:(mt + 1) * 128]
                mm(S, zA[par].ap, kt, qs, True, not ing, [KT.buf, QT.buf], [zA[par].buf])
                if ing:
                    off = 384 - 128 * (mt - 4 * g)
                    mm(S, zA[par].ap, C["identb"], maskb.ap[:, off:off + 512], False, True, [cb, maskb.buf], [zA[par].buf])
                S.add("act", lambda a: a.activation(out=e1[par].ap, in_=zA[par].ap, func=AF.Exp),
                      reads=[zA[par].buf], writes=[e1[par].buf])
                S.add("act", lambda a: a.activation(out=spb[par].ap, in_=e1[par].ap, func=AF.Ln, bias=1.0),
                      reads=[e1[par].buf], writes=[spb[par].buf])

            def st2(idx):
                mt = tiles[idx]
                par = idx % 2
                ing = mt >= 4 * g
                kt = KT.ap[:, mt * 128:(mt + 1) * 128]
                mm(S, aP[par].ap, kt, qs, True, False, [KT.buf, QT.buf], [aP[par].buf])
                if ing:
                    off = 384 - 128 * (mt - 4 * g)
                    mm(S, aP[par].ap, C["identb"], maskb.ap[:, off:off + 512], False, False, [cb, maskb.buf], [aP[par].buf])
                mm(S, aP[par].ap, C["neguincl"], spb[par].ap, False, True, [cb, spb[par].buf], [aP[par].buf])
                mm(S, Tp.ap, C["negones"], spb[par].ap, True, True, [cb, spb[par].buf], [Tp.buf])
                if idx == 0:
                    S.add("act", lambda a: a.activation(out=PT[par].ap, in_=aP[par].ap, func=AF.Exp),
                          reads=[aP[par].buf], writes=[PT[par].buf])
                else:
                    S.add("dve", lambda v: v.tensor_tensor(out=a3[par].ap, in0=aP[par].ap, in1=carry.ap, op=ALU.add),
                          reads=[aP[par].buf, carry.buf], writes=[a3[par].buf])
                    S.add("act", lambda a: a.activation(out=PT[par].ap, in_=a3[par].ap, func=AF.Exp),
                          reads=[a3[par].buf], writes=[PT[par].buf])
                mm(S, acc.ap, V.ap[:, mt * 128:(mt + 1) * 128], PT[par].ap, idx == 0, idx == last,
                   [V.buf, PT[par].buf], [acc.buf])
                if idx == 0:
                    S.add("dve", lambda v: v.tensor_copy(out=carry.ap, in_=Tp.ap), reads=[Tp.buf], writes=[carry.buf])
                elif idx < last:
                    S.add("dve", lambda v: v.tensor_tensor(out=carry.ap, in0=carry.ap, in1=Tp.ap, op=ALU.add),
                          reads=[Tp.buf, carry.buf], writes=[carry.buf])

            for idx in range(len(tiles) + 1):
                if idx < len(tiles):
                    st1(idx)
                if idx >= 1:
                    st2(idx - 1)
            S.add("act", lambda a, g=g: a.copy(out=OT.ap[:, g * 512:(g + 1) * 512], in_=acc.ap),
                  reads=[acc.buf], writes=[OT.buf])
        store_o(hh, OT)
        S.barrier()
        ar.release(m)

    def sb_pair(hhs):
        m = ar.mark()
        maskb = ar.alloc(896, BF16, "sbmaskb")
        S.add("pool", lambda g: g.dma_start(out=maskb.ap, in_=sbmask), writes=[maskb.buf], dma_sem=maskb.buf)
        cb = C["cb"].buf
        H = []
        for k, hh in enumerate(hhs):
            QT, KT = load_qk(hh)
            V = load_v(hh, 1)
            H.append(dict(hh=hh, QT=QT, KT=KT, V=V, OT=ar.alloc(SEQ, BF16, f"OT{k}"),
                          e1=[ar.alloc(512, F32, f"e1{k}{i}") for i in range(2)],
                          spb=[ar.alloc(512, BF16, f"spb{k}{i}") for i in range(2)],
                          a3=[ar.alloc(512, F32, f"a3{k}{i}") for i in range(2)],
                          PT=[ar.alloc(512, BF16, f"PT{k}{i}") for i in range(2)],
                          carry=ar.alloc(512, F32, f"carry{k}"),
                          zA=PS[4 * k + 0], aP=PS[4 * k + 1], Tp=PS[4 * k + 2], acc=PS[4 * k + 3]))
        for g in range(8):
            tiles = list(range(4 * g + 3, -1, -1))
            last = len(tiles) - 1

            def st1(h, idx):
                mt = tiles[idx]
                par = idx % 2
                ing = mt >= 4 * g
                zA, e1, spb = h["zA"], h["e1"][par], h["spb"][par]
                kt = h["KT"].ap[:, mt * 128:(mt + 1) * 128]
                qs = h["QT"].ap[:, g * 512:(g + 1) * 512]
                mm(S, zA.ap, kt, qs, True, not ing, [h["KT"].buf, h["QT"].buf], [zA.buf])
                if ing:
                    off = 384 - 128 * (mt - 4 * g)
                    mm(S, zA.ap, C["identb"], maskb.ap[:, off:off + 512], False, True, [cb, maskb.buf], [zA.buf])
                S.add("act", lambda a: a.activation(out=e1.ap, in_=zA.ap, func=AF.Exp), reads=[zA.buf], writes=[e1.buf])
                S.add("act", lambda a: a.activation(out=spb.ap, in_=e1.ap, func=AF.Ln, bias=1.0), reads=[e1.buf], writes=[spb.buf])

            def st2a(h, idx):
                mt = tiles[idx]
                par = idx % 2
                ing = mt >= 4 * g
                aP, Tp, spb, a3, PT, carry = h["aP"], h["Tp"], h["spb"][par], h["a3"][par], h["PT"][par], h["carry"]
                kt = h["KT"].ap[:, mt * 128:(mt + 1) * 128]
                qs = h["QT"].ap[:, g * 512:(g + 1) * 512]
                mm(S, aP.ap, kt, qs, True, False, [h["KT"].buf, h["QT"].buf], [aP.buf])
                if ing:
                    off = 384 - 128 * (mt - 4 * g)
                    mm(S, aP.ap, C["identb"], maskb.ap[:, off:off + 512], False, False, [cb, maskb.buf], [aP.buf])
                mm(S, aP.ap, C["neguincl"], spb.ap, False, True, [cb, spb.buf], [aP.buf])
                if idx < last:
                    mm(S, Tp.ap, C["negones"], spb.ap, True, True, [cb, spb.buf], [Tp.buf])
                if idx == 0:
                    S.add("act", lambda a: a.activation(out=PT.ap, in_=aP.ap, func=AF.Exp), reads=[aP.buf], writes=[PT.buf])
                else:
                    S.add("dve", lambda v: v.tensor_tensor(out=a3.ap, in0=aP.ap, in1=carry.ap, op=ALU.add),
                          reads=[aP.buf, carry.buf], writes=[a3.buf])
                    S.add("act", lambda a: a.activation(out=PT.ap, in_=a3.ap, func=AF.Exp), reads=[a3.buf], writes=[PT.buf])
                if idx == 0:
                    S.add("dve", lambda v: v.tensor_copy(out=carry.ap, in_=Tp.ap), reads=[Tp.buf], writes=[carry.buf])
                elif idx < last:
                    S.add("dve", lambda v: v.tensor_tensor(out=carry.ap, in0=carry.ap, in1=Tp.ap, op=ALU.add),
                          reads=[Tp.buf, carry.buf], writes=[carry.buf])

            def st2b(h, idx):
                mt = tiles[idx]
                PT = h["PT"][idx % 2]
                mm(S, h["acc"].ap, h["V"].ap[:, mt * 128:(mt + 1) * 128], PT.ap, idx == 0, idx == last,
                   [h["V"].buf, PT.buf], [h["acc"].buf])

            for idx in range(len(tiles) + 1):
                if idx < len(tiles):
                    for h in H:
                        st1(h, idx)
                if idx >= 1:
                    for h in H:
                        st2a(h, idx - 1)
                    for h in H:
                        st2b(h, idx - 1)
            for k, h in enumerate(H):
                eng = "act" if k == 0 else "dve"
                if k == 0:
                    S.add("act", lambda a, h=h, g=g: a.copy(out=h["OT"].ap[:, g * 512:(g + 1) * 512], in_=h["acc"].ap),
                          reads=[h["acc"].buf], writes=[h["OT"].buf])
                else:
                    S.add("dve", lambda v, h=h, g=g: v.tensor_copy(out=h["OT"].ap[:, g * 512:(g + 1) * 512], in_=h["acc"].ap),
                          reads=[h["acc"].buf], writes=[h["OT"].buf])
        for h in H:
            store_o(h["hh"], h["OT"])
        S.barrier()
        ar.release(m)

    def moba_head(hh, s):
        m = ar.mark()
        QT, KT = load_qk(hh)
        V = load_v(hh, 1)
        strip = ar.alloc(4352, BF16, "mstrip")
        S.add("pool", lambda g: g.dma_start(out=strip.ap, in_=mstrip[s]), writes=[strip.buf], dma_sem=strip.buf)
        eselb = ar.alloc(2048, BF16, "eselb")
        S.add("pool", lambda g: g.dma_start(out=eselb.ap[0:16, :], in_=esel), writes=[eselb.buf], dma_sem=eselb.buf)
        selT = ar.alloc(SEQ, BF16, "selT")
        OT = ar.alloc(SEQ, BF16, "OT")
        ksum = ar.alloc(16, F32, "ksum")
        kmb = ar.alloc(16, BF16, "kmb")
        NBUF = 4
        gm = [ar.alloc(16, F32, f"gm{i}") for i in range(NBUF)]
        mx = [ar.alloc(8, F32, f"mx{i}") for i in range(NBUF)]
        sel01 = [ar.alloc(16, F32, f"sel01{i}") for i in range(NBUF)]
        rd = ar.alloc(256, F32, "rd")
        PT = [ar.alloc(256, BF16, f"PTm{i}") for i in range(2)]
        cb = C["cb"].buf
        S.add("dve", lambda v: v.reduce_sum(out=ksum.ap, in_=KT.ap.rearrange("p (n k) -> p n k", k=256), axis=AX.X),
              reads=[KT.buf], writes=[ksum.buf])
        S.add("act", lambda a: a.mul(out=kmb.ap, in_=ksum.ap, mul=1.0 / 256), reads=[ksum.buf], writes=[kmb.buf])
        for i in range(NBUF):
            S.add("dve", lambda v, i=i: v.memset(gm[i].ap, -1e30), writes=[gm[i].buf])
        pgv = [Tile(PS[6].ap[:, k * 16:(k + 1) * 16], S.buf(f"pg{k}")) for k in range(NBUF)]
        ptv = [Tile(PS[7].ap[0:16, k * 128:(k + 1) * 128], S.buf(f"pt{k}")) for k in range(NBUF)]
        for i in range(2, 32):
            ob = i // 2
            k = i % NBUF
            pg, pt, gmk, mxk, slk = pgv[k], ptv[k], gm[k], mx[k], sel01[k]
            mm(S, pg.ap, QT.ap[:, i * 128:(i + 1) * 128], kmb.ap, True, True, [QT.buf, kmb.buf], [pg.buf])
            S.add("dve", lambda v, ob=ob, pg=pg, gmk=gmk: v.tensor_copy(out=gmk.ap[:, 0:ob], in_=pg.ap[:, 0:ob]),
                  reads=[pg.buf], writes=[gmk.buf])
            S.add("dve", lambda v, gmk=gmk, mxk=mxk: v.max(out=mxk.ap, in_=gmk.ap), reads=[gmk.buf], writes=[mxk.buf])
            S.add("dve", lambda v, gmk=gmk, mxk=mxk, slk=slk: v.tensor_scalar(out=slk.ap, in0=gmk.ap, scalar1=mxk.ap[:, 2:3], scalar2=-1.0,
                                                                             op0=ALU.is_ge, op1=ALU.add),
                  reads=[gmk.buf, mxk.buf], writes=[slk.buf])
            S.add("pe", lambda t, pt=pt, slk=slk: t.transpose(pt.ap, slk.ap, C["identf"]),
                  reads=[slk.buf, C["cf"].buf], writes=[pt.buf])
            S.add("act", lambda a, i=i, pt=pt: a.mul(out=selT.ap[0:16, i * 128:(i + 1) * 128], in_=pt.ap, mul=-NEG),
                  reads=[pt.buf], writes=[selT.buf])
        for G in range(16):
            kts = list(range(0, 2 * G + 2))
            last = len(kts) - 1
            acc = PS[4 + G % 2]
            den = PS[2 + G % 2]
            qs = QT.ap[:, G * 256:(G + 1) * 256]

            def stS(idx):
                kt = kts[idx]
                n = kt // 2
                par = idx % 2
                Sp = PS[par]
                mm(S, Sp.ap[:, 0:256], KT.ap[:, kt * 128:(kt + 1) * 128], qs, True, False, [KT.buf, QT.buf], [Sp.buf])
                off = 256 * G - 128 * kt + 128
                mm(S, Sp.ap[:, 0:256], C["identb"], strip.ap[:, off:off + 256], False, n == G, [cb, strip.buf], [Sp.buf])
                if n < G:
                    mm(S, Sp.ap[:, 0:256], eselb.ap[0:16, n * 128:(n + 1) * 128], selT.ap[0:16, G * 256:(G + 1) * 256],
                       False, True, [eselb.buf, selT.buf], [Sp.buf])
                S.add("act", lambda a: a.activation(out=PT[par].ap, in_=Sp.ap[:, 0:256], func=AF.Exp),
                      reads=[Sp.buf], writes=[PT[par].buf])

            def stPV(idx):
                kt = kts[idx]
                par = idx % 2
                mm(S, acc.ap[:, 0:256], V.ap[:, kt * 128:(kt + 1) * 128], PT[par].ap, idx == 0, idx == last,
                   [V.buf, PT[par].buf], [acc.buf])
                mm(S, den.ap[:, 0:256], C["onesb"], PT[par].ap, idx == 0, idx == last, [cb, PT[par].buf], [den.buf])

            for idx in range(len(kts) + 1):
                if idx < len(kts):
                    stS(idx)
                if idx >= 1:
                    stPV(idx - 1)
            S.add("dve", lambda v, den=den: v.reciprocal(out=rd.ap, in_=den.ap[:, 0:256]), reads=[den.buf], writes=[rd.buf])
            S.add("dve", lambda v, acc=acc, G=G: v.tensor_tensor(out=OT.ap[:, G * 256:(G + 1) * 256], in0=acc.ap[:, 0:256],
                                                                in1=rd.ap, op=ALU.mult),
                  reads=[acc.buf, rd.buf], writes=[OT.buf])
        store_o(hh, OT)
        S.barrier()
        ar.release(m)

    def dil_head(hh, s):
        m = ar.mark()
        QT, KT = load_qk(hh)
        Vr = [load_v(hh, r) for (_, r) in DILS]
        dsb = ar.alloc(768, BF16, "dsb")
        S.add("pool", lambda g: g.dma_start(out=dsb.ap, in_=dstrip[s]), writes=[dsb.buf], dma_sem=dsb.buf)
        ACC = ar.alloc(2 * SEQ, F32, "ACC")
        OT = ar.alloc(SEQ, BF16, "OT")
        PT = [ar.alloc(256, BF16, f"PTd{i}") for i in range(2)]
        cb = C["cb"].buf
        ACC3 = ACC.ap.rearrange("p (a t) -> p a t", a=2)
        for pi, (w, r) in enumerate(DILS):
            nsub = SEQ // r // 128
            V = Vr[pi]
            items = [(c, n) for c in range(r) for n in range(nsub)]

            def sub(T_, c, n):
                return T_.ap[:, c + r * 128 * n: c + r * 128 * n + r * 127 + 1: r]

            def stS(idx):
                c, n = items[idx]
                par = idx % 2
                Sp = PS[par]
                qsl = sub(QT, c, n)
                mm(S, Sp.ap[:, 128:256], sub(KT, c, n), qsl, True, False, [KT.buf, QT.buf], [Sp.buf])
                mm(S, Sp.ap[:, 128:256], C["identb"], dsb.ap[:, pi * 256:pi * 256 + 128], False, True, [cb, dsb.buf], [Sp.buf])
                if n >= 1:
                    mm(S, Sp.ap[:, 0:128], sub(KT, c, n - 1), qsl, True, False, [KT.buf, QT.buf], [Sp.buf])
                    mm(S, Sp.ap[:, 0:128], C["identb"], dsb.ap[:, pi * 256 + 128:pi * 256 + 256], False, True,
                       [cb, dsb.buf], [Sp.buf])
                    S.add("act", lambda a: a.activation(out=PT[par].ap, in_=Sp.ap[:, 0:256], func=AF.Exp),
                          reads=[Sp.buf], writes=[PT[par].buf])
                else:
                    S.add("act", lambda a: a.activation(out=PT[par].ap[:, 128:256], in_=Sp.ap[:, 128:256], func=AF.Exp),
                          reads=[Sp.buf], writes=[PT[par].buf])

            def stPV(idx):
                c, n = items[idx]
                par = idx % 2
                acc = PS[2 + idx % 2]
                ti = c * nsub + n
                vc = V.ap[:, ti * 128:(ti + 1) * 128]
                rb = [V.buf, PT[par].buf]
                if n >= 1:
                    vp = V.ap[:, (ti - 1) * 128:ti * 128]
                    mm(S, acc.ap[:, 0:128], vp, PT[par].ap[:, 0:128], True, False, rb, [acc.buf])
                    mm(S, acc.ap[:, 0:128], vc, PT[par].ap[:, 128:256], False, True, rb, [acc.buf])
                    mm(S, acc.ap[:, 128:256], C["onesb"], PT[par].ap[:, 0:128], True, False, [cb, PT[par].buf], [acc.buf])
                    mm(S, acc.ap[:, 128:256], C["onesb"], PT[par].ap[:, 128:256], False, True, [cb, PT[par].buf], [acc.buf])
                else:
                    mm(S, acc.ap[:, 0:128], vc, PT[par].ap[:, 128:256], True, True, rb, [acc.buf])
                    mm(S, acc.ap[:, 128:256], C["onesb"], PT[par].ap[:, 128:256], True, True, [cb, PT[par].buf], [acc.buf])
                dst = ACC3[:, :, c + r * 128 * n: c + r * 128 * n + r * 127 + 1: r]
                src = acc.ap[:, 0:256].rearrange("p (a t) -> p a t", a=2)
                if pi == 0:
                    S.add("dve", lambda v: v.tensor_copy(out=dst, in_=src), reads=[acc.buf], writes=[ACC.buf])
                else:
                    S.add("dve", lambda v: v.tensor_tensor(out=dst, in0=dst, in1=src, op=ALU.add),
                          reads=[acc.buf, ACC.buf], writes=[ACC.buf])

            for idx in range(len(items) + 1):
                if idx < len(items):
                    stS(idx)
                if idx >= 1:
                    stPV(idx - 1)
        S.add("dve", lambda v: v.reciprocal(out=ACC.ap[:, SEQ:2 * SEQ], in_=ACC.ap[:, SEQ:2 * SEQ]), reads=[ACC.buf], writes=[ACC.buf])
        S.add("dve", lambda v: v.tensor_tensor(out=OT.ap, in0=ACC.ap[:, 0:SEQ], in1=ACC.ap[:, SEQ:2 * SEQ], op=ALU.mult),
              reads=[ACC.buf], writes=[OT.buf])
        store_o(hh, OT)
        S.barrier()
        ar.release(m)

    if "sb" in phases:
        sb_pair((0, 1))
    if "moba" in phases:
        for s_ in range(3):
            moba_head(2 + s_, s_)
    if "dil" in phases:
        for s_ in range(3):
            dil_head(5 + s_, s_)
    ar.release(mA)
    return outs


def host_A_inputs(x_b, g_mix_l, w_in_l, q_gain_l, k_gain_l, rel_bias, hg):
    sb = [0, 1] if hg == 0 else [2, 3]
    mo = [4, 5, 6] if hg == 0 else [7, 8, 9]
    di = [10, 11, 12] if hg == 0 else [13, 14, 15]
    hs = sb + mo + di
    cols = lambda base: [w_in_l[:, base + h * 128: base + (h + 1) * 128] for h in hs]
    wqkv = np.ascontiguousarray(np.concatenate(cols(0) + cols(2048) + cols(4096), axis=1))
    soft = [h - 4 for h in hs[2:]]
    gcols = np.ascontiguousarray(np.concatenate([q_gain_l[soft].T, k_gain_l[soft].T], axis=1).astype(np.float32))
    negf = np.float32(NEG)
    kk = np.arange(128)[:, None]
    u = np.arange(4352)[None, :]
    dist = u - 128 - kk
    bidx = t5_bucket_np(dist)
    mstrip = np.stack([np.where(dist >= 0, rel_bias[bidx, h - 4], negf) for h in mo]).astype(np.float32)
    u2 = np.arange(256)[None, :]
    delta = u2 - kk
    valid = (delta >= 0) & (delta <= 128)
    ds = []
    for h in di:
        per = []
        for (w, r) in DILS:
            per.append(np.where(valid, rel_bias[t5_bucket_np(delta * r), h - 4], negf))
        ds.append(np.concatenate(per, axis=1))
    dstrip = np.stack(ds).astype(np.float32)
    return dict(x=np.ascontiguousarray(x_b), gmix=np.ascontiguousarray(np.broadcast_to(g_mix_l[None, :], (128, D_MODEL))),
                wqkv=wqkv, gcols=gcols, mstrip=np.ascontiguousarray(mstrip), dstrip=np.ascontiguousarray(dstrip),
                sbmask=host_sbmask(), esel=host_esel(), cst=host_consts()), hs


def host_sbmask():
    kk = np.arange(128)[:, None]
    u = np.arange(896)[None, :] - 384
    return np.where(u > kk, 0.0, NEG).astype(np.float32)


def host_esel():
    e = np.zeros((16, 2048), np.float32)
    for n in range(16):
        e[n, n * 128:(n + 1) * 128] = 1.0
    return e


GORDER = (0, 1, 2, 3)
BR_HEADS = ([0, 1, 2, 3], [4, 5, 6, 7, 8, 9], [10, 11, 12, 13, 14, 15])


def emit_B(P, C, T):
    S, ar = P.S, P.ar
    PS = P.ps
    stage = 2
    x, gmix, gffn = T["x"], T["gmix"], T["gffn"]
    wg, wb, wo, wgu, wd, xo = T["wg"], T["wb"], T["wo"], T["wgu"], T["wd"], T["xo"]
    slot = T["slot"]
    mB = ar.mark()
    gbx = ar.alloc(2048, F32, "gbx")

    def load_g(src):
        S.add("pool", lambda g: g.dma_start(out=gbx.ap, in_=src), writes=[gbx.buf], dma_sem=gbx.buf)
        return gbx
    tmp = norm_tmp(P, need_x=False)
    xres = ar.alloc(4 * 2048, F32, "xres")
    hT = ar.alloc(16 * 512, BF16, "hT")
    wbufs = [ar.alloc(16 * 512, BF16, f"wbuf{i}") for i in range(3)]
    macc = [ar.alloc(512, F32, f"macc{i}") for i in range(4)]
    gs = [ar.alloc(512, F32, f"gs{i}") for i in range(2)]
    tt = [ar.alloc(512, F32, f"tt{i}") for i in range(2)]
    yT = [ar.alloc(512, F32, f"yT{i}") for i in range(2)]
    wbb = ar.alloc(16 * 512, BF16, "wbb")
    gb3 = T.get("next_gmix")
    wcount = [0]
    outs = []

    wcache = T.get("wcache")
    cbufs = {}
    blk = [0]

    def wload(src3, nch, t=None):
        if t is None:
            t = wbufs[wcount[0] % 3]
            wcount[0] += 1
        dst = t.ap[:, 0:nch * 512].rearrange("p (c n) -> p c n", n=512)
        if wcache is None:
            S.add("pool", lambda g: g.dma_start(out=dst, in_=src3), writes=[t.buf], dma_sem=t.buf)
            return t
        b = blk[0]
        blk[0] += 1
        flat = t.ap[:, 0:nch * 512]
        if cur_g[0] == 0:
            cbufs[b] = S.buf(f"wc{b}")
            S.add("pool", lambda g: g.dma_start(out=dst, in_=src3), writes=[t.buf], dma_sem=t.buf)
            S.add("sp", lambda g: g.dma_start(out=wcache[b, :, 0:nch * 512], in_=flat), reads=[t.buf], writes=[cbufs[b]], dma_sem=t.buf)
        else:
            S.add("sp", lambda g: g.dma_start(out=flat, in_=wcache[b, :, 0:nch * 512]), reads=[cbufs[b]], writes=[t.buf], dma_sem=t.buf)
        return t

    def wsrc(w, r0, nch, c0):
        return w[r0:r0 + nch * 128, c0:c0 + 512].rearrange("(c p) n -> p c n", p=128)

    def resid_add(yps, oc, cnt):
        y = yT[cnt % 2]
        S.add("act", lambda a: a.copy(out=y.ap, in_=yps.ap), reads=[yps.buf], writes=[y.buf])
        pt = PS[4 + cnt % 2]
        for t in range(4):
            S.add("pe", lambda e, t=t: e.transpose(pt.ap[:, t * 128:(t + 1) * 128], y.ap[:, t * 128:(t + 1) * 128], C["identf"]),
                  reads=[y.buf, C["cf"].buf], writes=[pt.buf])
        dst = xres.ap.rearrange("p (t f) -> p t f", t=4)[:, :, oc * 128:(oc + 1) * 128]
        src = pt.ap.rearrange("p (t f) -> p t f", t=4)
        S.add("dve", lambda v: v.tensor_tensor(out=dst, in0=dst, in1=src, op=ALU.add), reads=[pt.buf, xres.buf], writes=[xres.buf])

    oTg = ar.alloc(16 * 512, BF16, "oTg")
    mT = ar.alloc(16 * 512, BF16, "mT")
    aTc = ar.alloc(12 * 512, BF16, "aTc")
    xstage = ar.alloc(2048, F32, "xstage")
    tmp["x"] = [xstage, xstage]

    def aT_chunk(j):
        if j < 16:
            return oTg.ap[:, j * 512:(j + 1) * 512], oTg.buf
        if j < 32:
            return mT.ap[:, (j - 16) * 512:(j - 15) * 512], mT.buf
        return aTc.ap[:, (j - 32) * 512:(j - 31) * 512], aTc.buf

    def first_norm(gg, tiles=(0, 1, 2, 3), gbt=None):
        gbt = gbt if gbt is not None else load_g(gmix)
        norm_transpose(P, C, [x[gg * 512 + t * 128: gg * 512 + (t + 1) * 128, :] for t in tiles], gbt, hT, 4, tmp, dq="pool",
                       t0=tiles[0])
        return gbt

    hoist = gb3 is None
    cur_g = [0]
    for g in GORDER:
        cur_g[0] = g
        blk[0] = 0
        T["load_oT"](S, g, oTg, mT)
        for t in range(4):
            r0 = g * 512 + t * 128
            S.add("pool", lambda q, t=t, r0=r0: q.dma_start(out=xres.ap[:, t * 2048:(t + 1) * 2048], in_=x[r0:r0 + 128, :]),
                  writes=[xres.buf], dma_sem=xres.buf, batch=("xres", g))
        xt = [Tile(xres.ap[:, t * 2048:(t + 1) * 2048], xres.buf) for t in range(4)]
        if g == 0 or not hoist:
            first_norm(g)
        cnt = 0
        for k in range(4):
            wbt = wload(wsrc(wb, 0, 16, k * 512), 16, wbb)
            for br in range(3):
                wgt = wload(wsrc(wg, 0, 16, br * 2048 + k * 512), 16)
                for c4 in range(4):
                    cc = 4 * k + c4
                    gp = PS[cnt % 2]
                    pp = PS[2 + cnt % 2]
                    g_ = gs[cnt % 2]
                    t_ = tt[cnt % 2]
                    cnt += 1
                    for fc in range(16):
                        mm(S, gp.ap, wgt.ap[:, fc * 512 + c4 * 128: fc * 512 + (c4 + 1) * 128], hT.ap[:, fc * 512:(fc + 1) * 512],
                           fc == 0, fc == 15, [wgt.buf, hT.buf], [gp.buf])
                    hl = BR_HEADS[br]
                    for i, h in enumerate(hl):
                        mm(S, pp.ap, wbt.ap[:, h * 512 + c4 * 128: h * 512 + (c4 + 1) * 128], oTg.ap[:, slot(h) * 512:(slot(h) + 1) * 512],
                           i == 0, i == len(hl) - 1, [wbt.buf, oTg.buf], [pp.buf])
                    S.add("act", lambda a, g_=g_, gp=gp: a.activation(out=g_.ap, in_=gp.ap, func=AF.Sigmoid),
                          reads=[gp.buf], writes=[g_.buf])
                    if br == 0:
                        S.add("dve", lambda v, g_=g_, pp=pp, c4=c4: v.tensor_tensor(out=macc[c4].ap, in0=pp.ap, in1=g_.ap, op=ALU.mult),
                              reads=[pp.buf, g_.buf], writes=[macc[c4].buf])
                    else:
                        S.add("dve", lambda v, g_=g_, pp=pp, t_=t_: v.tensor_tensor(out=t_.ap, in0=pp.ap, in1=g_.ap, op=ALU.mult),
                              reads=[pp.buf, g_.buf], writes=[t_.buf])
                        if br == 1:
                            S.add("dve", lambda v, t_=t_, c4=c4: v.tensor_tensor(out=macc[c4].ap, in0=macc[c4].ap, in1=t_.ap, op=ALU.add),
                                  reads=[t_.buf, macc[c4].buf], writes=[macc[c4].buf])
                        else:
                            S.add("dve", lambda v, t_=t_, c4=c4, cc=cc, mT=mT: v.tensor_tensor(out=mT.ap[:, cc * 512:(cc + 1) * 512], in0=macc[c4].ap,
                                                                                     in1=t_.ap, op=ALU.add),
                                  reads=[t_.buf, macc[c4].buf], writes=[mT.buf])
        cnt = 0
        for k in range(4):
            wot = wload(wsrc(wo, 0, 16, k * 512), 16)
            for c4 in range(4):
                oc = 4 * k + c4
                yp = PS[cnt % 2]
                for cc in range(16):
                    mm(S, yp.ap, wot.ap[:, cc * 512 + c4 * 128: cc * 512 + (c4 + 1) * 128], mT.ap[:, cc * 512:(cc + 1) * 512],
                       cc == 0, cc == 15, [wot.buf, mT.buf], [yp.buf])
                resid_add(yp, oc, cnt)
                cnt += 1
        norm_transpose(P, C, xt, load_g(gffn), hT, 4, tmp)
        cnt = 0
        for j in range(11):
            wgt = wload(wsrc(wgu, 0, 16, j * 512), 16)
            wut = wload(wsrc(wgu, 0, 16, D_FF + j * 512), 16)
            for c4 in range(4):
                jc = 4 * j + c4
                gp = PS[cnt % 2]
                up = PS[2 + cnt % 2]
                g_ = gs[cnt % 2]
                cnt += 1
                for fc in range(16):
                    mm(S, gp.ap, wgt.ap[:, fc * 512 + c4 * 128: fc * 512 + (c4 + 1) * 128], hT.ap[:, fc * 512:(fc + 1) * 512],
                       fc == 0, fc == 15, [wgt.buf, hT.buf], [gp.buf])
                for fc in range(16):
                    mm(S, up.ap, wut.ap[:, fc * 512 + c4 * 128: fc * 512 + (c4 + 1) * 128], hT.ap[:, fc * 512:(fc + 1) * 512],
                       fc == 0, fc == 15, [wut.buf, hT.buf], [up.buf])
                S.add("act", lambda a, g_=g_, gp=gp: a.activation(out=g_.ap, in_=gp.ap, func=AF.Silu), reads=[gp.buf], writes=[g_.buf])
                aap, abuf = aT_chunk(jc)
                S.add("dve", lambda v, g_=g_, up=up, aap=aap: v.tensor_tensor(out=aap, in0=up.ap, in1=g_.ap, op=ALU.mult),
                      reads=[up.buf, g_.buf], writes=[abuf])
        cnt = 0
        for k in range(4):
            for (j0, nch) in ((0, 16), (16, 16), (32, 12)):
                wdt = wload(wsrc(wd, j0 * 128, nch, k * 512), nch)
                for c4 in range(4):
                    yp = PS[c4]
                    for jj in range(nch):
                        j = j0 + jj
                        aap, abuf = aT_chunk(j)
                        mm(S, yp.ap, wdt.ap[:, jj * 512 + c4 * 128: jj * 512 + (c4 + 1) * 128], aap,
                           j == 0, j == 43, [wdt.buf, abuf], [yp.buf])
            for c4 in range(4):
                resid_add(PS[c4], 4 * k + c4, cnt)
                cnt += 1
            if hoist and g + 1 < len(GORDER):
                gbt_h = first_norm(g + 1, (k,), None if k == 0 else gbt_h)
        for t in range(4):
            r0 = g * 512 + t * 128
            outs.append(S.add("pool", lambda q, t=t, r0=r0: q.dma_start(out=xo[r0:r0 + 128, :], in_=xres.ap[:, t * 2048:(t + 1) * 2048]),
                              reads=[xres.buf], dma_sem=xres.buf))
        if gb3 is not None:
            norm_transpose(P, C, xt, load_g(gb3), hT, 4, tmp)
            outs.extend(T["store_h"](S, g, hT))
    ar.release(mB)
    return outs


HEADS_HG = ([0, 1, 4, 5, 6, 10, 11, 12], [2, 3, 7, 8, 9, 13, 14, 15])
PAIRS = [[0, 1], [2, 3], [4, 5], [6, 7]]


def head_slot(h):
    for hg in range(2):
        if h in HEADS_HG[hg]:
            return hg * 8 + HEADS_HG[hg].index(h)
    raise ValueError(h)


def build_fused(depth=DEPTH):
    P = Prog()
    S, ar, nc = P.S, P.ar, P.nc
    H2 = SEQ // 2
    x = P.din("x", [SEQ, D_MODEL])
    xh = P.din("xh", [H2, D_MODEL])
    sel = P.din("sel", [128, 2])
    gmix = P.din("gmix", [DEPTH, 128, D_MODEL])
    gffn = P.din("gffn", [DEPTH, 128, D_MODEL])
    wqkv = P.din("wqkv", [DEPTH, D_MODEL, 3072])
    gcols = P.din("gcols", [DEPTH, 128, 12])
    mstrip = P.din("mstrip", [3, 128, 4352])
    dstrip = P.din("dstrip", [3, 128, 768])
    sbmask = P.din("sbmask", [128, 896])
    esel = P.din("esel", [16, 2048])
    cst = P.din("cst", [128, 512])
    wg = P.din("wg", [DEPTH, D_MODEL, 3 * D_MODEL])
    wb = P.din("wb", [DEPTH, D_MODEL, D_MODEL])
    wo = P.din("wo", [DEPTH, D_MODEL, D_MODEL])
    wgu = P.din("wgu", [DEPTH, D_MODEL, 2 * D_FF])
    wd = P.din("wd", [DEPTH, D_FF, D_MODEL])
    xo = P.dout("xo", [H2, D_MODEL])
    qTs = P.dscratch("qTs", [8, 128, SEQ], BF16)
    kTs = P.dscratch("kTs", [8, 128, SEQ], BF16)
    vs = P.dscratch("vs", [SEQ, 1024], BF16)
    x1h = nc.dram_tensor("x1h", [H2, D_MODEL], F32)
    occ = [[[nc.dram_tensor(f"occ{l}_{hf}_{q}", [256, H2], BF16) for q in range(4)] for hf in range(2)] for l in range(depth)]
    ogc = [[[nc.dram_tensor(f"ogc{l}_{hf}_{q}", [512, H2], BF16) for q in range(4)] for hf in range(2)] for l in range(depth)]
    wcache = nc.dram_tensor("wcache", [56, 128, 8192], BF16).ap()
    hcc = [nc.dram_tensor(f"hcc{j}", [256, H2], BF16) for j in range(8)]
    hgc = [nc.dram_tensor(f"hgc{j}", [512, H2], BF16) for j in range(8)]
    C = load_consts(P, cst)
    selt = ar.alloc(2, F32, "selt")
    S.add("sp", lambda g: g.dma_start(out=selt.ap, in_=sel), writes=[selt.buf], dma_sem=selt.buf)
    ccn = [0]

    def allgather(pairs):
        S.barrier()
        for (src, dst) in pairs:
            k = ccn[0]
            ccn[0] += 1
            op = S.add("pool", lambda g, src=src, dst=dst: g.collective_compute(
                "AllGather", ALU.bypass, replica_groups=PAIRS, ins=[src.ap().opt()], outs=[dst.ap().opt()]),
                cc_sem=P.cc_sems[0])
            op.sig = (P.cc_sems[0], k + 1)
        S.barrier()

    final = []
    for l in range(depth):
        def store_o(S_, hh, OT, l=l):
            ops = []
            for hf in range(2):
                dst = occ[l][hf][hh // 2].ap()[(hh % 2) * 128:(hh % 2 + 1) * 128, :]
                ops.append(S_.add("pool", lambda q, hf=hf, dst=dst: q.dma_start(out=dst, in_=OT.ap[:, hf * H2:(hf + 1) * H2]),
                                  reads=[OT.buf], dma_sem=OT.buf))
            return ops

        def load_hT(S_, g, h):
            rk, c0 = g // 4, (g % 4) * 512
            for j in range(8):
                S_.add("sp", lambda q, j=j: q.dma_start(
                    out=h.ap.rearrange("p (f t) -> p f t", f=16)[:, 2 * j:2 * j + 2, :],
                    in_=hgc[j].ap()[rk * 256:(rk + 1) * 256, c0:c0 + 512].rearrange("(f p) t -> p f t", p=128)),
                    writes=[h.buf], dma_sem=h.buf, batch=("hT", g))

        TA = dict(x=x, gmix=gmix[l], wqkv=wqkv[l], gcols=gcols[l], mstrip=mstrip, dstrip=dstrip,
                  sbmask=sbmask, esel=esel, qTs=qTs, kTs=kTs, vs=vs, store_o=store_o, load_hT=(None if l == 0 else load_hT))
        emit_A(P, C, TA)
        allgather([(occ[l][hf][q], ogc[l][hf][q]) for hf in range(2) for q in range(4)])

        def load_oT(S_, g, oTg, mT, l=l):
            for hf, dstt in ((0, oTg), (1, mT)):
                d3 = dstt.ap.rearrange("p (h t) -> p h t", h=16)
                for q in range(4):
                    for rk in range(2):
                        s0 = rk * 8 + 2 * q
                        S_.add("pool", lambda e, hf=hf, q=q, rk=rk, s0=s0, d3=d3: e.dma_start(
                            out=d3[:, s0:s0 + 2, :],
                            in_=ogc[l][hf][q].ap()[rk * 256:(rk + 1) * 256, g * 512:(g + 1) * 512].rearrange("(h p) t -> p h t", p=128)),
                            writes=[dstt.buf], dma_sem=dstt.buf, batch=("oT", l, g, hf))
            S_.add("dve", lambda v: v.tensor_scalar(out=oTg.ap, in0=oTg.ap, scalar1=selt.ap[:, 0:1], scalar2=None, op0=ALU.mult),
                   reads=[oTg.buf, selt.buf], writes=[oTg.buf])
            S_.add("dve", lambda v: v.scalar_tensor_tensor(out=oTg.ap, in0=mT.ap, scalar=selt.ap[:, 1:2], in1=oTg.ap,
                                                           op0=ALU.mult, op1=ALU.add),
                   reads=[oTg.buf, mT.buf, selt.buf], writes=[oTg.buf])

        def store_h(S_, g, hT):
            ops = []
            for j in range(8):
                ops.append(S_.add("pool", lambda q, j=j: q.dma_start(
                    out=hcc[j].ap()[:, g * 512:(g + 1) * 512].rearrange("(f p) t -> p f t", p=128),
                    in_=hT.ap.rearrange("p (f t) -> p f t", f=16)[:, 2 * j:2 * j + 2, :]),
                    reads=[hT.buf], dma_sem=hT.buf))
            return ops

        last = (l == depth - 1)
        TB = dict(x=(xh if l == 0 else x1h.ap()), gmix=gmix[l], gffn=gffn[l], wg=wg[l], wb=wb[l], wo=wo[l], wgu=wgu[l], wd=wd[l],
                  xo=(xo if last else x1h.ap()), load_oT=load_oT, slot=head_slot,
                  next_gmix=(None if last else gmix[l + 1]), store_h=store_h, wcache=wcache)
        outs = emit_B(P, C, TB)
        if last:
            final = outs
        else:
            allgather([(hcc[j], hgc[j]) for j in range(8)])
    S.final_waits = list(final)
    return P.finish()


def kernel(x, g_mix, w_in, q_gain, k_gain, w_branch, w_out, g_ffn, w_gu, w_down, rel_bias, _depth=DEPTH):
    f = lambda a: np.ascontiguousarray(np.asarray(a, dtype=np.float32))
    x, g_mix, w_in, q_gain, k_gain = f(x), f(g_mix), f(w_in), f(q_gain), f(k_gain)
    w_branch, w_out, g_ffn, w_gu, w_down, rel_bias = f(w_branch), f(w_out), f(g_ffn), f(w_gu), f(w_down), f(rel_bias)
    rep = lambda g: np.ascontiguousarray(np.broadcast_to(g[:, None, :], (DEPTH, 128, D_MODEL)))
    gm, gf = rep(g_mix), rep(g_ffn)
    wg = np.ascontiguousarray(w_in[:, :, 3 * D_MODEL:])
    per_hg = []
    for hg in range(2):
        lay = [host_A_inputs(x[0], g_mix[l], w_in[l], q_gain[l], k_gain[l], rel_bias, hg)[0] for l in range(DEPTH)]
        per_hg.append(dict(wqkv=np.stack([a["wqkv"] for a in lay]), gcols=np.stack([a["gcols"] for a in lay]),
                           mstrip=lay[0]["mstrip"], dstrip=lay[0]["dstrip"]))
    shared = dict(gmix=gm, gffn=gf, sbmask=host_sbmask(), esel=host_esel(), cst=host_consts(),
                  wg=wg, wb=w_branch, wo=w_out, wgu=w_gu, wd=w_down)
    in_maps = []
    for b in range(BATCH):
        for r in range(2):
            selv = np.zeros((128, 2), np.float32)
            selv[:, r] = 1.0
            m = dict(shared)
            m.update(per_hg[r])
            m.update(x=x[b], xh=np.ascontiguousarray(x[b, r * 2048:(r + 1) * 2048]), sel=selv)
            in_maps.append(m)
    nc = build_fused(_depth)
    res = run_bass_kernel_spmd(nc, in_maps, core_ids=list(range(8)))
    out = np.empty_like(x)
    for b in range(BATCH):
        for r in range(2):
            out[b, r * 2048:(r + 1) * 2048] = np.asarray(res.results[b * 2 + r]["xo"])
    return out
```

```python
import contextlib
import math
import numpy as np
import concourse.bass as bass
import concourse.mybir as mybir
from concourse.bass_utils import run_bass_kernel_spmd

F32 = mybir.dt.float32
BF16 = mybir.dt.bfloat16
ALU = mybir.AluOpType
AF = mybir.ActivationFunctionType
AX = mybir.AxisListType

D_MODEL = 2048
SEQ = 4096
BATCH = 4
DEPTH = 2
HD = 128
D_FF = 5632
NEG = -30000.0
SCALE = HD ** -0.5
RMS_EPS = 1e-6
N_BUCKETS = 32
MAX_DISTANCE = 2048
DILS = ((128, 1), (512, 4), (2048, 16))

ENGS = ("pe", "act", "dve", "pool", "sp")


class Buf:
    __slots__ = ("name", "writer", "readers", "dreaders", "sem", "semcnt")

    def __init__(self, name):
        self.name = name
        self.writer = None
        self.readers = {}
        self.dreaders = []
        self.sem = None
        self.semcnt = 0


class Op:
    __slots__ = ("eng", "fn", "deps", "is_dma", "sig", "sembuf", "needs_sig", "inc", "batch")

    def __init__(self, eng, fn, is_dma):
        self.eng = eng
        self.fn = fn
        self.deps = []
        self.is_dma = is_dma
        self.sig = None
        self.sembuf = None
        self.needs_sig = False
        self.inc = 16
        self.batch = None


class Sched:
    def __init__(self, nc, sems):
        self.nc = nc
        self.sems = sems
        self.dma_pool = [[s_, 0] for s_ in sems["dma"]]
        self.dma_bufs = []
        self.streams = {e: [] for e in ENGS}
        self.final_waits = []
        self.pending_dma = []
        self.barrier_deps = {e: [] for e in ENGS}

    def buf(self, name=None):
        return Buf(name)

    def add(self, eng, fn, reads=(), writes=(), dma_sem=None, extra_deps=(), cc_sem=None, batch=None):
        op = Op(eng, fn, dma_sem is not None or cc_sem is not None)
        op.batch = batch
        if cc_sem is not None:
            op.sig = (cc_sem, 1)
            op.inc = 1
        if dma_sem is not None:
            b = dma_sem
            if b.sem is None:
                if not self.dma_pool:
                    raise RuntimeError("out of DMA semaphores")
                b.sem = self.dma_pool.pop()
                self.dma_bufs.append(b)
            b.sem[1] += 16
            op.sig = (b.sem[0], b.sem[1])
            op.sembuf = b
        deps = []
        same = lambda d: (not op.is_dma) and (not d.is_dma) and d.eng == eng
        for b in reads:
            w = b.writer
            if w is not None and not (same(w) and eng == "pe"):
                deps.append(w)
        for b in writes:
            w = b.writer
            if w is not None and not same(w) and not (batch is not None and w.batch == batch):
                deps.append(w)
            for r in b.readers.values():
                if not same(r):
                    deps.append(r)
            deps.extend(b.dreaders)
        deps.extend(extra_deps)
        if self.barrier_deps[eng]:
            deps.extend(self.barrier_deps[eng])
            self.barrier_deps[eng] = []
        seen = set()
        for d in deps:
            if d is op or id(d) in seen:
                continue
            seen.add(id(d))
            op.deps.append(d)
            d.needs_sig = True
        for b in reads:
            if op.is_dma:
                b.dreaders.append(op)
            else:
                b.readers[eng] = op
        for b in writes:
            b.writer = op
            b.readers = {}
            b.dreaders = []
        self.streams[eng].append(op)
        if op.is_dma:
            self.pending_dma.append(op)
        return op

    def barrier(self):
        lasts = []
        for e in ENGS:
            st = [o for o in self.streams[e] if not o.is_dma]
            if st:
                lasts.append(st[-1])
        lasts.extend(self.pending_dma)
        self.pending_dma = []
        for b in self.dma_bufs:
            self.dma_pool.append(b.sem)
            b.sem = None
        self.dma_bufs = []
        for e in ENGS:
            self.barrier_deps[e] = list(lasts)

    def emit(self):
        nc = self.nc
        sems = self.sems
        cnt = {e: 0 for e in ENGS}
        for e in ENGS:
            for op in self.streams[e]:
                if (not op.is_dma) and op.needs_sig:
                    cnt[e] += 1
                    ep = cnt[e] // 30000
                    op.sig = (sems[e][ep], cnt[e] - ep * 30000 + (1 if ep else 0))
        handles = {"pe": "tensor", "act": "scalar", "dve": "vector", "pool": "gpsimd", "sp": "sync"}
        with nc.Block() as block:
            for e in ENGS:
                ops = self.streams[e]
                if not ops and not (e == "sp" and self.final_waits):
                    continue

                def body(eng, ops=ops, e=e):
                    waited = {}
                    for op in ops:
                        need = {}
                        for d in op.deps:
                            sem, val = d.sig
                            k = id(sem)
                            if waited.get(k, 0) >= val:
                                continue
                            if k not in need or need[k][1] < val:
                                need[k] = (sem, val)
                        for k, (sem, val) in need.items():
                            eng.wait_ge(sem, val)
                            waited[k] = val
                        ins = op.fn(eng)
                        if op.is_dma:
                            ins.then_inc(op.sig[0], op.inc)
                        elif op.needs_sig:
                            ins.then_inc(op.sig[0], 1)
                    if e == "sp":
                        need = {}
                        for d in self.final_waits:
                            sem, val = d.sig
                            k = id(sem)
                            if waited.get(k, 0) >= val:
                                continue
                            if k not in need or need[k][1] < val:
                                need[k] = (sem, val)
                        for k, (sem, val) in need.items():
                            eng.wait_ge(sem, val)

                getattr(block, handles[e])(body)


class Tile:
    __slots__ = ("ap", "buf")

    def __init__(self, ap, buf):
        self.ap = ap
        self.buf = buf

    def __getitem__(self, k):
        return self.ap[k]


class Arena:
    def __init__(self, nc, es, S, nbytes, name="arena"):
        self.t = es.enter_context(nc.sbuf_tensor(name, [128, nbytes // 4], F32))
        self.S = S
        self.off = 0
        self.cap = nbytes

    def alloc(self, ncols, dt, name=None):
        esz = 4 if dt == F32 else 2
        nb = (ncols * esz + 31) // 32 * 32
        if self.off + nb > self.cap:
            raise RuntimeError(f"arena overflow allocating {name}: {self.off}+{nb}>{self.cap}")
        v = self.t[:, self.off // 4:(self.off + nb) // 4]
        if dt != F32:
            v = v.bitcast(dt)
        v = v[:, 0:ncols]
        self.off += nb
        return Tile(v, self.S.buf(name))

    def mark(self):
        return self.off

    def release(self, m):
        self.off = m


def t5_bucket_np(dist):
    max_exact = N_BUCKETS // 2
    d = np.maximum(dist, 0)
    df = np.maximum(d, 1).astype(np.float32)
    large = max_exact + (np.log(df / np.float32(max_exact)) / np.float32(math.log(MAX_DISTANCE / max_exact))
                         * np.float32(N_BUCKETS - max_exact)).astype(np.int32)
    large = np.minimum(large, N_BUCKETS - 1)
    return np.where(d < max_exact, d, large)


class Prog:
    def __init__(self, arena_bytes=207 * 1024):
        self.nc = bass.Bass("TRN2", target_bir_lowering=False)
        self.es = contextlib.ExitStack()
        nc, es = self.nc, self.es
        sems = {e: [es.enter_context(nc.semaphore(f"s_{e}{k}")) for k in range(4)] for e in ("pe", "act", "dve", "pool")}
        self.cc_sems = [es.enter_context(nc.semaphore(f"cc{i}")) for i in range(6)]
        sems["dma"] = [es.enter_context(nc.semaphore(f"d{i}")) for i in range(76)]
        self.S = Sched(nc, sems)
        self.ar = Arena(nc, es, self.S, arena_bytes)
        self.ps = []
        for i in range(8):
            t = es.enter_context(nc.psum_tensor(f"ps{i}", [128, 512], F32))
            self.ps.append(Tile(t[:], self.S.buf(f"ps{i}")))

    def din(self, name, shape, dt=F32):
        return self.nc.dram_tensor(name, list(shape), dt, kind="ExternalInput").ap()

    def dout(self, name, shape, dt=F32):
        return self.nc.dram_tensor(name, list(shape), dt, kind="ExternalOutput").ap()

    def dscratch(self, name, shape, dt, debug=False):
        if debug:
            return self.nc.dram_tensor(name, list(shape), dt, kind="ExternalOutput").ap()
        return self.nc.dram_tensor(name, list(shape), dt).ap()

    def finish(self):
        self.S.emit()
        self.es.close()
        return self.nc


def load_consts(P, cst_in):
    S, ar = P.S, P.ar
    cb = ar.alloc(4 * 128, BF16, "cstb")
    S.add("pool", lambda g: g.dma_start(out=cb.ap, in_=cst_in[:, 0:512]), writes=[cb.buf], dma_sem=cb.buf)
    cf = ar.alloc(128, F32, "identf")
    S.add("sp", lambda g: g.dma_start(out=cf.ap, in_=cst_in[:, 0:128]), writes=[cf.buf], dma_sem=cf.buf)
    eps = ar.alloc(1, F32, "eps")
    S.add("dve", lambda v: v.memset(eps.ap, RMS_EPS), writes=[eps.buf])
    return dict(eps=eps, cb=cb, identb=cb.ap[:, 0:128], onesb=cb.ap[:, 128:256], negones=cb.ap[:, 256:384],
                neguincl=cb.ap[:, 384:512], cf=cf, identf=cf.ap)


def host_consts():
    c = np.zeros((128, 512), np.float32)
    c[:, 0:128] = np.eye(128)
    c[:, 128:256] = 1.0
    c[:, 256:384] = -1.0
    j = np.arange(128)[:, None]
    k = np.arange(128)[None, :]
    c[:, 384:512] = -(j >= k).astype(np.float32)
    return c


def norm_transpose(P, C, x_tiles, gb, hT, ntiles, tmp, dq="sp", t0=0):
    S = P.S
    T = ntiles * 128
    for i, xt in enumerate(x_tiles):
        xs = tmp["x"][i % 2]
        hb = tmp["hb"][i % 2]
        junk = tmp["junk"]
        ssq = tmp["ssq"][i % 2]
        rstd = tmp["rstd"][i % 2]
        if isinstance(xt, Tile):
            xs = xt
        else:
            S.add(dq, lambda g, xs=xs, xt=xt: g.dma_start(out=xs.ap, in_=xt), writes=[xs.buf], dma_sem=xs.buf)
        S.add("act", lambda a, xs=xs, ssq=ssq: a.activation(out=junk.ap, in_=xs.ap, func=AF.Square, accum_out=ssq.ap),
              reads=[xs.buf], writes=[junk.buf, ssq.buf])
        S.add("act", lambda a, ssq=ssq, rstd=rstd: a.activation(out=rstd.ap, in_=ssq.ap, func=AF.Ln,
                                                               scale=1.0 / D_MODEL, bias=C["eps"].ap),
              reads=[ssq.buf, C["eps"].buf], writes=[rstd.buf])
        S.add("act", lambda a, rstd=rstd: a.activation(out=rstd.ap, in_=rstd.ap, func=AF.Exp, scale=-0.5),
              reads=[rstd.buf], writes=[rstd.buf])
        S.add("dve", lambda v, xs=xs, rstd=rstd, hb=hb: v.scalar_tensor_tensor(
            out=hb.ap, in0=xs.ap, scalar=rstd.ap, in1=gb.ap, op0=ALU.mult, op1=ALU.mult),
            reads=[xs.buf, rstd.buf, gb.buf], writes=[hb.buf])
        for half in range(2):
            pb = P.ps[6 + half]
            pv = pb.ap.bitcast(BF16)
            for k in range(8):
                fc = half * 8 + k
                S.add("pe", lambda t, pv=pv, k=k, fc=fc, hb=hb: t.transpose(pv[:, k * 128:(k + 1) * 128],
                                                                           hb.ap[:, fc * 128:(fc + 1) * 128], C["identb"]),
                      reads=[hb.buf, C["cb"].buf], writes=[pb.buf])
            dst = hT.ap.rearrange("p (f t) -> p f t", f=16)[:, half * 8:(half + 1) * 8, (t0 + i) * 128:(t0 + i + 1) * 128]
            src = pv.rearrange("p (f t) -> p f t", f=8)
            if half == 0:
                S.add("act", lambda a, dst=dst, src=src: a.copy(out=dst, in_=src), reads=[pb.buf], writes=[hT.buf])
            else:
                S.add("dve", lambda v, dst=dst, src=src: v.tensor_copy(out=dst, in_=src), reads=[pb.buf], writes=[hT.buf])


def norm_tmp(P, need_x=True):
    ar = P.ar
    return dict(x=[ar.alloc(2048, F32, f"xs{i}") for i in range(2)] if need_x else [None, None],
                hb=[ar.alloc(2048, BF16, f"hb{i}") for i in range(2)],
                junk=ar.alloc(2048, BF16, "junk"),
                ssq=[ar.alloc(1, F32, f"ssq{i}") for i in range(2)],
                rstd=[ar.alloc(1, F32, f"rstd{i}") for i in range(2)])


def mm(S, out, lhsT, rhs, start, stop, reads, writes):
    return S.add("pe", lambda t: t.matmul(out, lhsT, rhs, start=start, stop=stop), reads=reads, writes=writes)


def emit_A(P, C, T, phases=("a0", "sb", "moba", "dil")):
    S, ar = P.S, P.ar
    debug = False
    x, gmix, wqkv, gcols = T["x"], T["gmix"], T["wqkv"], T["gcols"]
    mstrip, dstrip, sbmask, esel = T["mstrip"], T["dstrip"], T["sbmask"], T["esel"]
    qTs, kTs, vs = T["qTs"], T["kTs"], T["vs"]
    mA = ar.mark()
    gb = ar.alloc(2048, F32, "gb")
    S.add("sp", lambda g: g.dma_start(out=gb.ap, in_=gmix), writes=[gb.buf], dma_sem=gb.buf)
    gc = ar.alloc(12, F32, "gc")
    S.add("sp", lambda g: g.dma_start(out=gc.ap, in_=gcols), writes=[gc.buf], dma_sem=gc.buf)
    S.add("act", lambda a: a.mul(out=gc.ap[:, 0:6], in_=gc.ap[:, 0:6], mul=SCALE), reads=[gc.buf], writes=[gc.buf])
    stores = []
    m0 = ar.mark()
    if "a0" in phases:
        W = ar.alloc(16 * 3072, BF16, "W")
        Wb = [S.buf(f"W{fc}") for fc in range(16)]
        for fc in range(16):
            S.add("pool", lambda g, fc=fc: g.dma_start(out=W.ap[:, fc * 3072:(fc + 1) * 3072],
                                                       in_=wqkv[fc * 128:(fc + 1) * 128, :]),
                  writes=[Wb[fc]], dma_sem=Wb[fc])
        tmp = norm_tmp(P)
        hT = [ar.alloc(16 * 512, BF16, f"hT{i}") for i in range(2)]
        sq = [ar.alloc(512, BF16, f"sq{i}") for i in range(2)]
        rs = [ar.alloc(512, F32, f"rs{i}") for i in range(2)]
        qst = [ar.alloc(512, BF16, f"qst{i}") for i in range(4)]
        vst = [ar.alloc(1024, BF16, f"vst{i}") for i in range(2)]
        for g in range(8):
            h = hT[g % 2]
            def prep(gg):
                hh_ = hT[gg % 2]
                if T.get("load_hT") is not None:
                    T["load_hT"](S, gg, hh_)
                else:
                    norm_transpose(P, C, [x[(gg * 4 + i) * 128:(gg * 4 + i + 1) * 128, :] for i in range(4)], gb, hh_, 4, tmp)

            if g == 0:
                prep(0)

            def proj(j):
                pq = P.ps[j % 3]
                for fc in range(16):
                    mm(S, pq.ap, W.ap[:, fc * 3072 + j * 128: fc * 3072 + (j + 1) * 128], h.ap[:, fc * 512:(fc + 1) * 512],
                       fc == 0, fc == 15, [Wb[fc], h.buf], [pq.buf])

            def post(j):
                hh = j % 8
                isq = j < 8
                pq = P.ps[j % 3]
                st = qst[j % 4]
                if hh < 2:
                    if isq:
                        S.add("act", lambda a: a.mul(out=st.ap, in_=pq.ap, mul=SCALE), reads=[pq.buf], writes=[st.buf])
                    else:
                        S.add("act", lambda a: a.copy(out=st.ap, in_=pq.ap), reads=[pq.buf], writes=[st.buf])
                else:
                    s = hh - 2
                    col = gc.ap[:, s:s + 1] if isq else gc.ap[:, 6 + s:7 + s]
                    sqt = sq[j % 2]
                    rst = rs[j % 2]
                    pss = P.ps[3 + j % 2]
                    S.add("act", lambda a: a.activation(out=sqt.ap, in_=pq.ap, func=AF.Square), reads=[pq.buf], writes=[sqt.buf])
                    mm(S, pss.ap, C["onesb"], sqt.ap, True, True, [C["cb"].buf, sqt.buf], [pss.buf])
                    S.add("act", lambda a: a.activation(out=rst.ap, in_=pss.ap, func=AF.Ln, scale=1.0 / HD,
                                                        bias=C["eps"].ap), reads=[pss.buf, C["eps"].buf], writes=[rst.buf])
                    S.add("act", lambda a: a.activation(out=rst.ap, in_=rst.ap, func=AF.Exp, scale=-0.5),
                          reads=[rst.buf], writes=[rst.buf])
                    S.add("dve", lambda v: v.scalar_tensor_tensor(out=st.ap, in0=pq.ap, scalar=col, in1=rst.ap,
                                                                  op0=ALU.mult, op1=ALU.mult),
                          reads=[pq.buf, rst.buf, gc.buf], writes=[st.buf])
                dst = (qTs if isq else kTs)[hh, :, g * 512:(g + 1) * 512]
                stores.append(S.add("pool", lambda q: q.dma_start(out=dst, in_=st.ap), reads=[st.buf], dma_sem=st.buf))

            for j in range(16):
                proj(j)
                if j >= 1:
                    post(j - 1)
                if j == 8 and g + 1 < 8:
                    prep(g + 1)
            post(15)
            for i in range(4):
                vt = vst[i % 2]
                for half in range(2):
                    pv = P.ps[half]
                    for fc in range(16):
                        mm(S, pv.ap, h.ap[:, fc * 512 + i * 128: fc * 512 + (i + 1) * 128],
                           W.ap[:, fc * 3072 + 2048 + half * 512: fc * 3072 + 2048 + (half + 1) * 512],
                           fc == 0, fc == 15, [Wb[fc], h.buf], [pv.buf])
                    if half == 0:
                        S.add("act", lambda a, vt=vt, pv=pv: a.copy(out=vt.ap[:, 0:512], in_=pv.ap), reads=[pv.buf], writes=[vt.buf])
                    else:
                        S.add("dve", lambda v, vt=vt, pv=pv: v.tensor_copy(out=vt.ap[:, 512:1024], in_=pv.ap), reads=[pv.buf], writes=[vt.buf])
                r0 = (g * 4 + i) * 128
                stores.append(S.add("pool", lambda q, vt=vt, r0=r0: q.dma_start(out=vs[r0:r0 + 128, :], in_=vt.ap),
                                    reads=[vt.buf], dma_sem=vt.buf))
        S.barrier()
    ar.release(m0)
    outs = []
    PS = P.ps

    def load_qk(hh):
        QT = ar.alloc(SEQ, BF16, "QT")
        KT = ar.alloc(SEQ, BF16, "KT")
        S.add("sp", lambda g: g.dma_start(out=QT.ap, in_=qTs[hh]), writes=[QT.buf], dma_sem=QT.buf)
        S.add("sp", lambda g: g.dma_start(out=KT.ap, in_=kTs[hh]), writes=[KT.buf], dma_sem=KT.buf)
        return QT, KT

    def load_v(hh, r):
        V = ar.alloc(SEQ, BF16, f"V{r}")
        nsub = SEQ // r // 128
        for c in range(r):
            src = vs[c::r, hh * 128:(hh + 1) * 128].rearrange("(n k) d -> k n d", k=128)
            dst = V.ap[:, c * nsub * 128:(c + 1) * nsub * 128].rearrange("p (n d) -> p n d", d=128)
            S.add("sp", lambda g, src=src, dst=dst: g.dma_start(out=dst, in_=src), writes=[V.buf], dma_sem=V.buf, batch=("V", id(V)))
        return V

    def store_o(hh, OT):
        outs.extend(T["store_o"](S, hh, OT))

    def sb_head(hh):
        m = ar.mark()
        QT, KT = load_qk(hh)
        V = load_v(hh, 1)
        maskb = ar.alloc(896, BF16, "sbmaskb")
        S.add("pool", lambda g: g.dma_start(out=maskb.ap, in_=sbmask), writes=[maskb.buf], dma_sem=maskb.buf)
        OT = ar.alloc(SEQ, BF16, "OT")
        e1 = [ar.alloc(512, F32, f"e1{i}") for i in range(2)]
        spb = [ar.alloc(512, BF16, f"spb{i}") for i in range(2)]
        a3 = [ar.alloc(512, F32, f"a3{i}") for i in range(2)]
        PT = [ar.alloc(512, BF16, f"PT{i}") for i in range(2)]
        carry = ar.alloc(512, F32, "carry")
        zA = [PS[0], PS[1]]
        aP = [PS[2], PS[3]]
        Tp = PS[4]
        acc = PS[5]
        cb = C["cb"].buf
        for g in range(8):
            tiles = list(range(4 * g + 3, -1, -1))
            qs = QT.ap[:, g * 512:(g + 1) * 512]
            last = len(tiles) - 1

            def st1(idx):
                mt = tiles[idx]
                par = idx % 2
                ing = mt >= 4 * g
                kt = KT.ap[:, mt * 128:(mt + 1) * 128]
                mm(S, zA[par].ap, kt, qs, True, not ing, [KT.buf, QT.buf], [zA[par].buf])
                if ing:
                    off = 384 - 128 * (mt - 4 * g)
                    mm(S, zA[par].ap, C["identb"], maskb.ap[:, off:off + 512], False, True, [cb, maskb.buf], [zA[par].buf])
                S.add("act", lambda a: a.activation(out=e1[par].ap, in_=zA[par].ap, func=AF.Exp),
                      reads=[zA[par].buf], writes=[e1[par].buf])
                S.add("act", lambda a: a.activation(out=spb[par].ap, in_=e1[par].ap, func=AF.Ln, bias=1.0),
                      reads=[e1[par].buf], writes=[spb[par].buf])

            def st2(idx):
                mt = tiles[idx]
                par = idx % 2
                ing = mt >= 4 * g
                kt = KT.ap[:, mt * 128:(mt + 1) * 128]
                mm(S, aP[par].ap, kt, qs, True, False, [KT.buf, QT.buf], [aP[par].buf])
                if ing:
                    off = 384 - 128 * (mt - 4 * g)
                    mm(S, aP[par].ap, C["identb"], maskb.ap[:, off:off + 512], False, False, [cb, maskb.buf], [aP[par].buf])
                mm(S, aP[par].ap, C["neguincl"], spb[par].ap, False, True, [cb, spb[par].buf], [aP[par].buf])
                mm(S, Tp.ap, C["negones"], spb[par].ap, True, True, [cb, spb[par].buf], [Tp.buf])
                if idx == 0:
                    S.add("act", lambda a: a.activation(out=PT[par].ap, in_=aP[par].ap, func=AF.Exp),
                          reads=[aP[par].buf], writes=[PT[par].buf])
                else:
                    S.add("dve", lambda v: v.tensor_tensor(out=a3[par].ap, in0=aP[par].ap, in1=carry.ap, op=ALU.add),
                          reads=[aP[par].buf, carry.buf], writes=[a3[par].buf])
                    S.add("act", lambda a: a.activation(out=PT[par].ap, in_=a3[par].ap, func=AF.Exp),
                          reads=[a3[par].buf], writes=[PT[par].buf])
                mm(S, acc.ap, V.ap[:, mt * 128:(mt + 1) * 128], PT[par].ap, idx == 0, idx == last,
                   [V.buf, PT[par].buf], [acc.buf])
                if idx == 0:
                    S.add("dve", lambda v: v.tensor_copy(out=carry.ap, in_=Tp.ap), reads=[Tp.buf], writes=[carry.buf])
                elif idx < last:
                    S.add("dve", lambda v: v.tensor_tensor(out=carry.ap, in0=carry.ap, in1=Tp.ap, op=ALU.add),
                          reads=[Tp.buf, carry.buf], writes=[carry.buf])

            for idx in range(len(tiles) + 1):
                if idx < len(tiles):
                    st1(idx)
                if idx >= 1:
                    st2(idx - 1)
            S.add("act", lambda a, g=g: a.copy(out=OT.ap[:, g * 512:(g + 1) * 512], in_=acc.ap),
                  reads=[acc.buf], writes=[OT.buf])
        store_o(hh, OT)
        S.barrier()
        ar.release(m)

    def sb_pair(hhs):
        m = ar.mark()
        maskb = ar.alloc(896, BF16, "sbmaskb")
        S.add("pool", lambda g: g.dma_start(out=maskb.ap, in_=sbmask), writes=[maskb.buf], dma_sem=maskb.buf)
        cb = C["cb"].buf
        H = []
        for k, hh in enumerate(hhs):
            QT, KT = load_qk(hh)
            V = load_v(hh, 1)
            H.append(dict(hh=hh, QT=QT, KT=KT, V=V, OT=ar.alloc(SEQ, BF16, f"OT{k}"),
                          e1=[ar.alloc(512, F32, f"e1{k}{i}") for i in range(2)],
                          spb=[ar.alloc(512, BF16, f"spb{k}{i}") for i in range(2)],
                          a3=[ar.alloc(512, F32, f"a3{k}{i}") for i in range(2)],
                          PT=[ar.alloc(512, BF16, f"PT{k}{i}") for i in range(2)],
                          carry=ar.alloc(512, F32, f"carry{k}"),
                          zA=PS[4 * k + 0], aP=PS[4 * k + 1], Tp=PS[4 * k + 2], acc=PS[4 * k + 3]))
        for g in range(8):
            tiles = list(range(4 * g + 3, -1, -1))
            last = len(tiles) - 1

            def st1(h, idx):
                mt = tiles[idx]
                par = idx % 2
                ing = mt >= 4 * g
                zA, e1, spb = h["zA"], h["e1"][par], h["spb"][par]
                kt = h["KT"].ap[:, mt * 128:(mt + 1) * 128]
                qs = h["QT"].ap[:, g * 512:(g + 1) * 512]
                mm(S, zA.ap, kt, qs, True, not ing, [h["KT"].buf, h["QT"].buf], [zA.buf])
                if ing:
                    off = 384 - 128 * (mt - 4 * g)
                    mm(S, zA.ap, C["identb"], maskb.ap[:, off:off + 512], False, True, [cb, maskb.buf], [zA.buf])
                S.add("act", lambda a: a.activation(out=e1.ap, in_=zA.ap, func=AF.Exp), reads=[zA.buf], writes=[e1.buf])
                S.add("act", lambda a: a.activation(out=spb.ap, in_=e1.ap, func=AF.Ln, bias=1.0), reads=[e1.buf], writes=[spb.buf])

            def st2a(h, idx):
                mt = tiles[idx]
                par = idx % 2
                ing = mt >= 4 * g
                aP, Tp, spb, a3, PT, carry = h["aP"], h["Tp"], h["spb"][par], h["a3"][par], h["PT"][par], h["carry"]
                kt = h["KT"].ap[:, mt * 128:(mt + 1) * 128]
                qs = h["QT"].ap[:, g * 512:(g + 1) * 512]
                mm(S, aP.ap, kt, qs, True, False, [h["KT"].buf, h["QT"].buf], [aP.buf])
                if ing:
                    off = 384 - 128 * (mt - 4 * g)
                    mm(S, aP.ap, C["identb"], maskb.ap[:, off:off + 512], False, False, [cb, maskb.buf], [aP.buf])
                mm(S, aP.ap, C["neguincl"], spb.ap, False, True, [cb, spb.buf], [aP.buf])
                if idx < last:
                    mm(S, Tp.ap, C["negones"], spb.ap, True, True, [cb, spb.buf], [Tp.buf])
                if idx == 0:
                    S.add("act", lambda a: a.activation(out=PT.ap, in_=aP.ap, func=AF.Exp), reads=[aP.buf], writes=[PT.buf])
                else:
                    S.add("dve", lambda v: v.tensor_tensor(out=a3.ap, in0=aP.ap, in1=carry.ap, op=ALU.add),
                          reads=[aP.buf, carry.buf], writes=[a3.buf])
                    S.add("act", lambda a: a.activation(out=PT.ap, in_=a3.ap, func=AF.Exp), reads=[a3.buf], writes=[PT.buf])
                if idx == 0:
                    S.add("dve", lambda v: v.tensor_copy(out=carry.ap, in_=Tp.ap), reads=[Tp.buf], writes=[carry.buf])
                elif idx < last:
                    S.add("dve", lambda v: v.tensor_tensor(out=carry.ap, in0=carry.ap, in1=Tp.ap, op=ALU.add),
                          reads=[Tp.buf, carry.buf], writes=[carry.buf])

            def st2b(h, idx):
                mt = tiles[idx]
                PT = h["PT"][idx % 2]
                mm(S, h["acc"].ap, h["V"].ap[:, mt * 128:(mt + 1) * 128], PT.ap, idx == 0, idx == last,
                   [h["V"].buf, PT.buf], [h["acc"].buf])

            for idx in range(len(tiles) + 1):
                if idx < len(tiles):
                    for h in H:
                        st1(h, idx)
                if idx >= 1:
                    for h in H:
                        st2a(h, idx - 1)
                    for h in H:
                        st2b(h, idx - 1)
            for k, h in enumerate(H):
                eng = "act" if k == 0 else "dve"
                if k == 0:
                    S.add("act", lambda a, h=h, g=g: a.copy(out=h["OT"].ap[:, g * 512:(g + 1) * 512], in_=h["acc"].ap),
                          reads=[h["acc"].buf], writes=[h["OT"].buf])
                else:
                    S.add("dve", lambda v, h=h, g=g: v.tensor_copy(out=h["OT"].ap[:, g * 512:(g + 1) * 512], in_=h["acc"].ap),
                          reads=[h["acc"].buf], writes=[h["OT"].buf])
        for h in H:
            store_o(h["hh"], h["OT"])
        S.barrier()
        ar.release(m)

    def moba_head(hh, s):
        m = ar.mark()
        QT, KT = load_qk(hh)
        V = load_v(hh, 1)
        strip = ar.alloc(4352, BF16, "mstrip")
        S.add("pool", lambda g: g.dma_start(out=strip.ap, in_=mstrip[s]), writes=[strip.buf], dma_sem=strip.buf)
        eselb = ar.alloc(2048, BF16, "eselb")
        S.add("pool", lambda g: g.dma_start(out=eselb.ap[0:16, :], in_=esel), writes=[eselb.buf], dma_sem=eselb.buf)
        selT = ar.alloc(SEQ, BF16, "selT")
        OT = ar.alloc(SEQ, BF16, "OT")
        ksum = ar.alloc(16, F32, "ksum")
        kmb = ar.alloc(16, BF16, "kmb")
        NBUF = 4
        gm = [ar.alloc(16, F32, f"gm{i}") for i in range(NBUF)]
        mx = [ar.alloc(8, F32, f"mx{i}") for i in range(NBUF)]
        sel01 = [ar.alloc(16, F32, f"sel01{i}") for i in range(NBUF)]
        rd = ar.alloc(256, F32, "rd")
        PT = [ar.alloc(256, BF16, f"PTm{i}") for i in range(2)]
        cb = C["cb"].buf
        S.add("dve", lambda v: v.reduce_sum(out=ksum.ap, in_=KT.ap.rearrange("p (n k) -> p n k", k=256), axis=AX.X),
              reads=[KT.buf], writes=[ksum.buf])
        S.add("act", lambda a: a.mul(out=kmb.ap, in_=ksum.ap, mul=1.0 / 256), reads=[ksum.buf], writes=[kmb.buf])
        for i in range(NBUF):
            S.add("dve", lambda v, i=i: v.memset(gm[i].ap, -1e30), writes=[gm[i].buf])
        pgv = [Tile(PS[6].ap[:, k * 16:(k + 1) * 16], S.buf(f"pg{k}")) for k in range(NBUF)]
        ptv = [Tile(PS[7].ap[0:16, k * 128:(k + 1) * 128], S.buf(f"pt{k}")) for k in range(NBUF)]
        for i in range(2, 32):
            ob = i // 2
            k = i % NBUF
            pg, pt, gmk, mxk, slk = pgv[k], ptv[k], gm[k], mx[k], sel01[k]
            mm(S, pg.ap, QT.ap[:, i * 128:(i + 1) * 128], kmb.ap, True, True, [QT.buf, kmb.buf], [pg.buf])
            S.add("dve", lambda v, ob=ob, pg=pg, gmk=gmk: v.tensor_copy(out=gmk.ap[:, 0:ob], in_=pg.ap[:, 0:ob]),
                  reads=[pg.buf], writes=[gmk.buf])
            S.add("dve", lambda v, gmk=gmk, mxk=mxk: v.max(out=mxk.ap, in_=gmk.ap), reads=[gmk.buf], writes=[mxk.buf])
            S.add("dve", lambda v, gmk=gmk, mxk=mxk, slk=slk: v.tensor_scalar(out=slk.ap, in0=gmk.ap, scalar1=mxk.ap[:, 2:3], scalar2=-1.0,
                                                                             op0=ALU.is_ge, op1=ALU.add),
                  reads=[gmk.buf, mxk.buf], writes=[slk.buf])
            S.add("pe", lambda t, pt=pt, slk=slk: t.transpose(pt.ap, slk.ap, C["identf"]),
                  reads=[slk.buf, C["cf"].buf], writes=[pt.buf])
            S.add("act", lambda a, i=i, pt=pt: a.mul(out=selT.ap[0:16, i * 128:(i + 1) * 128], in_=pt.ap, mul=-NEG),
                  reads=[pt.buf], writes=[selT.buf])
        for G in range(16):
            kts = list(range(0, 2 * G + 2))
            last = len(kts) - 1
            acc = PS[4 + G % 2]
            den = PS[2 + G % 2]
            qs = QT.ap[:, G * 256:(G + 1) * 256]

            def stS(idx):
                kt = kts[idx]
                n = kt // 2
                par = idx % 2
                Sp = PS[par]
                mm(S, Sp.ap[:, 0:256], KT.ap[:, kt * 128:(kt + 1) * 128], qs, True, False, [KT.buf, QT.buf], [Sp.buf])
                off = 256 * G - 128 * kt + 128
                mm(S, Sp.ap[:, 0:256], C["identb"], strip.ap[:, off:off + 256], False, n == G, [cb, strip.buf], [Sp.buf])
                if n < G:
                    mm(S, Sp.ap[:, 0:256], eselb.ap[0:16, n * 128:(n + 1) * 128], selT.ap[0:16, G * 256:(G + 1) * 256],
                       False, True, [eselb.buf, selT.buf], [Sp.buf])
                S.add("act", lambda a: a.activation(out=PT[par].ap, in_=Sp.ap[:, 0:256], func=AF.Exp),
                      reads=[Sp.buf], writes=[PT[par].buf])

            def stPV(idx):
                kt = kts[idx]
                par = idx % 2
                mm(S, acc.ap[:, 0:256], V.ap[:, kt * 128:(kt + 1) * 128], PT[par].ap, idx == 0, idx == last,
                   [V.buf, PT[par].buf], [acc.buf])
                mm(S, den.ap[:, 0:256], C["onesb"], PT[par].ap, idx == 0, idx == last, [cb, PT[par].buf], [den.buf])

            for idx in range(len(kts) + 1):
                if idx < len(kts):
                    stS(idx)
                if idx >= 1:
                    stPV(idx - 1)
            S.add("dve", lambda v, den=den: v.reciprocal(out=rd.ap, in_=den.ap[:, 0:256]), reads=[den.buf], writes=[rd.buf])
            S.add("dve", lambda v, acc=acc, G=G: v.tensor_tensor(out=OT.ap[:, G * 256:(G + 1) * 256], in0=acc.ap[:, 0:256],
                                                                in1=rd.ap, op=ALU.mult),
                  reads=[acc.buf, rd.buf], writes=[OT.buf])
        store_o(hh, OT)
        S.barrier()
        ar.release(m)

    def dil_head(hh, s):
        m = ar.mark()
        QT, KT = load_qk(hh)
        Vr = [load_v(hh, r) for (_, r) in DILS]
        dsb = ar.alloc(768, BF16, "dsb")
        S.add("pool", lambda g: g.dma_start(out=dsb.ap, in_=dstrip[s]), writes=[dsb.buf], dma_sem=dsb.buf)
        ACC = ar.alloc(2 * SEQ, F32, "ACC")
        OT = ar.alloc(SEQ, BF16, "OT")
        PT = [ar.alloc(256, BF16, f"PTd{i}") for i in range(2)]
        cb = C["cb"].buf
        ACC3 = ACC.ap.rearrange("p (a t) -> p a t", a=2)
        for pi, (w, r) in enumerate(DILS):
            nsub = SEQ // r // 128
            V = Vr[pi]
            items = [(c, n) for c in range(r) for n in range(nsub)]

            def sub(T_, c, n):
                return T_.ap[:, c + r * 128 * n: c + r * 128 * n + r * 127 + 1: r]

            def stS(idx):
                c, n = items[idx]
                par = idx % 2
                Sp = PS[par]
                qsl = sub(QT, c, n)
                mm(S, Sp.ap[:, 128:256], sub(KT, c, n), qsl, True, False, [KT.buf, QT.buf], [Sp.buf])
                mm(S, Sp.ap[:, 128:256], C["identb"], dsb.ap[:, pi * 256:pi * 256 + 128], False, True, [cb, dsb.buf], [Sp.buf])
                if n >= 1:
                    mm(S, Sp.ap[:, 0:128], sub(KT, c, n - 1), qsl, True, False, [KT.buf, QT.buf], [Sp.buf])
                    mm(S, Sp.ap[:, 0:128], C["identb"], dsb.ap[:, pi * 256 + 128:pi * 256 + 256], False, True,
                       [cb, dsb.buf], [Sp.buf])
                    S.add("act", lambda a: a.activation(out=PT[par].ap, in_=Sp.ap[:, 0:256], func=AF.Exp),
                          reads=[Sp.buf], writes=[PT[par].buf])
                else:
                    S.add("act", lambda a: a.activation(out=PT[par].ap[:, 128:256], in_=Sp.ap[:, 128:256], func=AF.Exp),
                          reads=[Sp.buf], writes=[PT[par].buf])

            def stPV(idx):
                c, n = items[idx]
                par = idx % 2
                acc = PS[2 + idx % 2]
                ti = c * nsub + n
                vc = V.ap[:, ti * 128:(ti + 1) * 128]
                rb = [V.buf, PT[par].buf]
                if n >= 1:
                    vp = V.ap[:, (ti - 1) * 128:ti * 128]
                    mm(S, acc.ap[:, 0:128], vp, PT[par].ap[:, 0:128], True, False, rb, [acc.buf])
                    mm(S, acc.ap[:, 0:128], vc, PT[par].ap[:, 128:256], False, True, rb, [acc.buf])
                    mm(S, acc.ap[:, 128:256], C["onesb"], PT[par].ap[:, 0:128], True, False, [cb, PT[par].buf], [acc.buf])
                    mm(S, acc.ap[:, 128:256], C["onesb"], PT[par].ap[:, 128:256], False, True, [cb, PT[par].buf], [acc.buf])
                else:
                    mm(S, acc.ap[:, 0:128], vc, PT[par].ap[:, 128:256], True, True, rb, [acc.buf])
                    mm(S, acc.ap[:, 128:256], C["onesb"], PT[par].ap[:, 128:256], True, True, [cb, PT[par].buf], [acc.buf])
                dst = ACC3[:, :, c + r * 128 * n: c + r * 128 * n + r * 127 + 1: r]
                src = acc.ap[:, 0:256].rearrange("p (a t) -> p a t", a=2)
                if pi == 0:
                    S.add("dve", lambda v: v.tensor_copy(out=dst, in_=src), reads=[acc.buf], writes=[ACC.buf])
                else:
                    S.add("dve", lambda v: v.tensor_tensor(out=dst, in0=dst, in1=src, op=ALU.add),
                          reads=[acc.buf, ACC.buf], writes=[ACC.buf])

            for idx in range(len(items) + 1):
                if idx < len(items):
                    stS(idx)
                if idx >= 1:
                    stPV(idx - 1)
        S.add("act", lambda a: a.activation(out=ACC.ap[:, SEQ:2 * SEQ], in_=ACC.ap[:, SEQ:2 * SEQ], func=AF.Ln),
              reads=[ACC.buf], writes=[ACC.buf])
        S.add("act", lambda a: a.activation(out=ACC.ap[:, SEQ:2 * SEQ], in_=ACC.ap[:, SEQ:2 * SEQ], func=AF.Exp, scale=-1.0),
              reads=[ACC.buf], writes=[ACC.buf])
        S.add("dve", lambda v: v.tensor_tensor(out=OT.ap, in0=ACC.ap[:, 0:SEQ], in1=ACC.ap[:, SEQ:2 * SEQ], op=ALU.mult),
              reads=[ACC.buf], writes=[OT.buf])
        store_o(hh, OT)
        S.barrier()
        ar.release(m)

    if "sb" in phases:
        sb_pair((0, 1))
    if "moba" in phases:
        for s_ in range(3):
            moba_head(2 + s_, s_)
    if "dil" in phases:
        for s_ in range(3):
            dil_head(5 + s_, s_)
    ar.release(mA)
    return outs


def host_A_inputs(x_b, g_mix_l, w_in_l, q_gain_l, k_gain_l, rel_bias, hg):
    sb = [0, 1] if hg == 0 else [2, 3]
    mo = [4, 5, 6] if hg == 0 else [7, 8, 9]
    di = [10, 11, 12] if hg == 0 else [13, 14, 15]
    hs = sb + mo + di
    cols = lambda base: [w_in_l[:, base + h * 128: base + (h + 1) * 128] for h in hs]
    wqkv = np.ascontiguousarray(np.concatenate(cols(0) + cols(2048) + cols(4096), axis=1))
    soft = [h - 4 for h in hs[2:]]
    gcols = np.ascontiguousarray(np.concatenate([q_gain_l[soft].T, k_gain_l[soft].T], axis=1).astype(np.float32))
    negf = np.float32(NEG)
    kk = np.arange(128)[:, None]
    u = np.arange(4352)[None, :]
    dist = u - 128 - kk
    bidx = t5_bucket_np(dist)
    mstrip = np.stack([np.where(dist >= 0, rel_bias[bidx, h - 4], negf) for h in mo]).astype(np.float32)
    u2 = np.arange(256)[None, :]
    delta = u2 - kk
    valid = (delta >= 0) & (delta <= 128)
    ds = []
    for h in di:
        per = []
        for (w, r) in DILS:
            per.append(np.where(valid, rel_bias[t5_bucket_np(delta * r), h - 4], negf))
        ds.append(np.concatenate(per, axis=1))
    dstrip = np.stack(ds).astype(np.float32)
    return dict(x=np.ascontiguousarray(x_b), gmix=np.ascontiguousarray(np.broadcast_to(g_mix_l[None, :], (128, D_MODEL))),
                wqkv=wqkv, gcols=gcols, mstrip=np.ascontiguousarray(mstrip), dstrip=np.ascontiguousarray(dstrip),
                sbmask=host_sbmask(), esel=host_esel(), cst=host_consts()), hs


def host_sbmask():
    kk = np.arange(128)[:, None]
    u = np.arange(896)[None, :] - 384
    return np.where(u > kk, 0.0, NEG).astype(np.float32)


def host_esel():
    e = np.zeros((16, 2048), np.float32)
    for n in range(16):
        e[n, n * 128:(n + 1) * 128] = 1.0
    return e


GORDER = (0, 1, 2, 3)
BR_HEADS = ([0, 1, 2, 3], [4, 5, 6, 7, 8, 9], [10, 11, 12, 13, 14, 15])


def emit_B(P, C, T):
    S, ar = P.S, P.ar
    PS = P.ps
    stage = 2
    x, gmix, gffn = T["x"], T["gmix"], T["gffn"]
    wg, wb, wo, wgu, wd, xo = T["wg"], T["wb"], T["wo"], T["wgu"], T["wd"], T["xo"]
    slot = T["slot"]
    mB = ar.mark()
    gbx = ar.alloc(2048, F32, "gbx")

    def load_g(src):
        S.add("pool", lambda g: g.dma_start(out=gbx.ap, in_=src), writes=[gbx.buf], dma_sem=gbx.buf)
        return gbx
    tmp = norm_tmp(P, need_x=False)
    xres = ar.alloc(4 * 2048, F32, "xres")
    hT = ar.alloc(16 * 512, BF16, "hT")
    wbufs = [ar.alloc(16 * 512, BF16, f"wbuf{i}") for i in range(3)]
    macc = [ar.alloc(512, F32, f"macc{i}") for i in range(4)]
    gs = [ar.alloc(512, F32, f"gs{i}") for i in range(2)]
    tt = [ar.alloc(512, F32, f"tt{i}") for i in range(2)]
    yT = [ar.alloc(512, F32, f"yT{i}") for i in range(2)]
    wbb = ar.alloc(16 * 512, BF16, "wbb")
    gb3 = T.get("next_gmix")
    wcount = [0]
    outs = []

    wcache = T.get("wcache")
    cbufs = {}
    blk = [0]

    def wload(src3, nch, t=None):
        if t is None:
            t = wbufs[wcount[0] % 3]
            wcount[0] += 1
        dst = t.ap[:, 0:nch * 512].rearrange("p (c n) -> p c n", n=512)
        if wcache is None:
            S.add("pool", lambda g: g.dma_start(out=dst, in_=src3), writes=[t.buf], dma_sem=t.buf)
            return t
        b = blk[0]
        blk[0] += 1
        flat = t.ap[:, 0:nch * 512]
        if cur_g[0] == 0:
            cbufs[b] = S.buf(f"wc{b}")
            S.add("pool", lambda g: g.dma_start(out=dst, in_=src3), writes=[t.buf], dma_sem=t.buf)
            S.add("sp", lambda g: g.dma_start(out=wcache[b, :, 0:nch * 512], in_=flat), reads=[t.buf], writes=[cbufs[b]], dma_sem=t.buf)
        else:
            S.add("sp", lambda g: g.dma_start(out=flat, in_=wcache[b, :, 0:nch * 512]), reads=[cbufs[b]], writes=[t.buf], dma_sem=t.buf)
        return t

    def wsrc(w, r0, nch, c0):
        return w[r0:r0 + nch * 128, c0:c0 + 512].rearrange("(c p) n -> p c n", p=128)

    def resid_add(yps, oc, cnt):
        y = yT[cnt % 2]
        S.add("act", lambda a: a.copy(out=y.ap, in_=yps.ap), reads=[yps.buf], writes=[y.buf])
        pt = PS[4 + cnt % 2]
        for t in range(4):
            S.add("pe", lambda e, t=t: e.transpose(pt.ap[:, t * 128:(t + 1) * 128], y.ap[:, t * 128:(t + 1) * 128], C["identf"]),
                  reads=[y.buf, C["cf"].buf], writes=[pt.buf])
        dst = xres.ap.rearrange("p (t f) -> p t f", t=4)[:, :, oc * 128:(oc + 1) * 128]
        src = pt.ap.rearrange("p (t f) -> p t f", t=4)
        S.add("dve", lambda v: v.tensor_tensor(out=dst, in0=dst, in1=src, op=ALU.add), reads=[pt.buf, xres.buf], writes=[xres.buf])

    oTg = ar.alloc(16 * 512, BF16, "oTg")
    mT = ar.alloc(16 * 512, BF16, "mT")
    aTc = ar.alloc(12 * 512, BF16, "aTc")
    xstage = ar.alloc(2048, F32, "xstage")
    tmp["x"] = [xstage, xstage]

    def aT_chunk(j):
        if j < 16:
            return oTg.ap[:, j * 512:(j + 1) * 512], oTg.buf
        if j < 32:
            return mT.ap[:, (j - 16) * 512:(j - 15) * 512], mT.buf
        return aTc.ap[:, (j - 32) * 512:(j - 31) * 512], aTc.buf

    def first_norm(gg, tiles=(0, 1, 2, 3), gbt=None):
        gbt = gbt if gbt is not None else load_g(gmix)
        norm_transpose(P, C, [x[gg * 512 + t * 128: gg * 512 + (t + 1) * 128, :] for t in tiles], gbt, hT, 4, tmp, dq="pool",
                       t0=tiles[0])
        return gbt

    hoist = gb3 is None
    cur_g = [0]
    for g in GORDER:
        cur_g[0] = g
        blk[0] = 0
        T["load_oT"](S, g, oTg, mT)
        for t in range(4):
            r0 = g * 512 + t * 128
            S.add("pool", lambda q, t=t, r0=r0: q.dma_start(out=xres.ap[:, t * 2048:(t + 1) * 2048], in_=x[r0:r0 + 128, :]),
                  writes=[xres.buf], dma_sem=xres.buf, batch=("xres", g))
        xt = [Tile(xres.ap[:, t * 2048:(t + 1) * 2048], xres.buf) for t in range(4)]
        if g == 0 or not hoist:
            first_norm(g)
        cnt = 0
        for k in range(4):
            wbt = wload(wsrc(wb, 0, 16, k * 512), 16, wbb)
            for br in range(3):
                wgt = wload(wsrc(wg, 0, 16, br * 2048 + k * 512), 16)
                for c4 in range(4):
                    cc = 4 * k + c4
                    gp = PS[cnt % 2]
                    pp = PS[2 + cnt % 2]
                    g_ = gs[cnt % 2]
                    t_ = tt[cnt % 2]
                    cnt += 1
                    for fc in range(16):
                        mm(S, gp.ap, wgt.ap[:, fc * 512 + c4 * 128: fc * 512 + (c4 + 1) * 128], hT.ap[:, fc * 512:(fc + 1) * 512],
                           fc == 0, fc == 15, [wgt.buf, hT.buf], [gp.buf])
                    hl = BR_HEADS[br]
                    for i, h in enumerate(hl):
                        mm(S, pp.ap, wbt.ap[:, h * 512 + c4 * 128: h * 512 + (c4 + 1) * 128], oTg.ap[:, slot(h) * 512:(slot(h) + 1) * 512],
                           i == 0, i == len(hl) - 1, [wbt.buf, oTg.buf], [pp.buf])
                    S.add("act", lambda a, g_=g_, gp=gp: a.activation(out=g_.ap, in_=gp.ap, func=AF.Sigmoid),
                          reads=[gp.buf], writes=[g_.buf])
                    if br == 0:
                        S.add("dve", lambda v, g_=g_, pp=pp, c4=c4: v.tensor_tensor(out=macc[c4].ap, in0=pp.ap, in1=g_.ap, op=ALU.mult),
                              reads=[pp.buf, g_.buf], writes=[macc[c4].buf])
                    else:
                        S.add("dve", lambda v, g_=g_, pp=pp, t_=t_: v.tensor_tensor(out=t_.ap, in0=pp.ap, in1=g_.ap, op=ALU.mult),
                              reads=[pp.buf, g_.buf], writes=[t_.buf])
                        if br == 1:
                            S.add("dve", lambda v, t_=t_, c4=c4: v.tensor_tensor(out=macc[c4].ap, in0=macc[c4].ap, in1=t_.ap, op=ALU.add),
                                  reads=[t_.buf, macc[c4].buf], writes=[macc[c4].buf])
                        else:
                            S.add("dve", lambda v, t_=t_, c4=c4, cc=cc, mT=mT: v.tensor_tensor(out=mT.ap[:, cc * 512:(cc + 1) * 512], in0=macc[c4].ap,
                                                                                     in1=t_.ap, op=ALU.add),
                                  reads=[t_.buf, macc[c4].buf], writes=[mT.buf])
        cnt = 0
        for k in range(4):
            wot = wload(wsrc(wo, 0, 16, k * 512), 16)
            for c4 in range(4):
                oc = 4 * k + c4
                yp = PS[cnt % 2]
                for cc in range(16):
                    mm(S, yp.ap, wot.ap[:, cc * 512 + c4 * 128: cc * 512 + (c4 + 1) * 128], mT.ap[:, cc * 512:(cc + 1) * 512],
                       cc == 0, cc == 15, [wot.buf, mT.buf], [yp.buf])
                resid_add(yp, oc, cnt)
                cnt += 1
        norm_transpose(P, C, xt, load_g(gffn), hT, 4, tmp)
        cnt = 0
        for j in range(11):
            wgt = wload(wsrc(wgu, 0, 16, j * 512), 16)
            wut = wload(wsrc(wgu, 0, 16, D_FF + j * 512), 16)
            for c4 in range(4):
                jc = 4 * j + c4
                gp = PS[cnt % 2]
                up = PS[2 + cnt % 2]
                g_ = gs[cnt % 2]
                cnt += 1
                for fc in range(16):
                    mm(S, gp.ap, wgt.ap[:, fc * 512 + c4 * 128: fc * 512 + (c4 + 1) * 128], hT.ap[:, fc * 512:(fc + 1) * 512],
                       fc == 0, fc == 15, [wgt.buf, hT.buf], [gp.buf])
                for fc in range(16):
                    mm(S, up.ap, wut.ap[:, fc * 512 + c4 * 128: fc * 512 + (c4 + 1) * 128], hT.ap[:, fc * 512:(fc + 1) * 512],
                       fc == 0, fc == 15, [wut.buf, hT.buf], [up.buf])
                S.add("act", lambda a, g_=g_, gp=gp: a.activation(out=g_.ap, in_=gp.ap, func=AF.Silu), reads=[gp.buf], writes=[g_.buf])
                aap, abuf = aT_chunk(jc)
                S.add("dve", lambda v, g_=g_, up=up, aap=aap: v.tensor_tensor(out=aap, in0=up.ap, in1=g_.ap, op=ALU.mult),
                      reads=[up.buf, g_.buf], writes=[abuf])
        cnt = 0
        for k in range(4):
            for (j0, nch) in ((0, 16), (16, 16), (32, 12)):
                wdt = wload(wsrc(wd, j0 * 128, nch, k * 512), nch)
                for c4 in range(4):
                    yp = PS[c4]
                    for jj in range(nch):
                        j = j0 + jj
                        aap, abuf = aT_chunk(j)
                        mm(S, yp.ap, wdt.ap[:, jj * 512 + c4 * 128: jj * 512 + (c4 + 1) * 128], aap,
                           j == 0, j == 43, [wdt.buf, abuf], [yp.buf])
            for c4 in range(4):
                resid_add(PS[c4], 4 * k + c4, cnt)
                cnt += 1
            if hoist and g + 1 < len(GORDER):
                gbt_h = first_norm(g + 1, (k,), None if k == 0 else gbt_h)
        for t in range(4):
            r0 = g * 512 + t * 128
            outs.append(S.add("pool", lambda q, t=t, r0=r0: q.dma_start(out=xo[r0:r0 + 128, :], in_=xres.ap[:, t * 2048:(t + 1) * 2048]),
                              reads=[xres.buf], dma_sem=xres.buf))
        if gb3 is not None:
            norm_transpose(P, C, xt, load_g(gb3), hT, 4, tmp)
            outs.extend(T["store_h"](S, g, hT))
    ar.release(mB)
    return outs


HEADS_HG = ([0, 1, 4, 5, 6, 10, 11, 12], [2, 3, 7, 8, 9, 13, 14, 15])
PAIRS = [[0, 1], [2, 3], [4, 5], [6, 7]]


def head_slot(h):
    for hg in range(2):
        if h in HEADS_HG[hg]:
            return hg * 8 + HEADS_HG[hg].index(h)
    raise ValueError(h)


def build_fused(depth=DEPTH):
    P = Prog()
    S, ar, nc = P.S, P.ar, P.nc
    H2 = SEQ // 2
    x = P.din("x", [SEQ, D_MODEL])
    xh = P.din("xh", [H2, D_MODEL])
    sel = P.din("sel", [128, 2])
    gmix = P.din("gmix", [DEPTH, 128, D_MODEL])
    gffn = P.din("gffn", [DEPTH, 128, D_MODEL])
    wqkv = P.din("wqkv", [DEPTH, D_MODEL, 3072])
    gcols = P.din("gcols", [DEPTH, 128, 12])
    mstrip = P.din("mstrip", [3, 128, 4352])
    dstrip = P.din("dstrip", [3, 128, 768])
    sbmask = P.din("sbmask", [128, 896])
    esel = P.din("esel", [16, 2048])
    cst = P.din("cst", [128, 512])
    wg = P.din("wg", [DEPTH, D_MODEL, 3 * D_MODEL])
    wb = P.din("wb", [DEPTH, D_MODEL, D_MODEL])
    wo = P.din("wo", [DEPTH, D_MODEL, D_MODEL])
    wgu = P.din("wgu", [DEPTH, D_MODEL, 2 * D_FF])
    wd = P.din("wd", [DEPTH, D_FF, D_MODEL])
    xo = P.dout("xo", [H2, D_MODEL])
    qTs = P.dscratch("qTs", [8, 128, SEQ], BF16)
    kTs = P.dscratch("kTs", [8, 128, SEQ], BF16)
    vs = P.dscratch("vs", [SEQ, 1024], BF16)
    x1h = nc.dram_tensor("x1h", [H2, D_MODEL], F32)
    occ = [[[nc.dram_tensor(f"occ{l}_{hf}_{q}", [256, H2], BF16) for q in range(4)] for hf in range(2)] for l in range(depth)]
    ogc = [[[nc.dram_tensor(f"ogc{l}_{hf}_{q}", [512, H2], BF16) for q in range(4)] for hf in range(2)] for l in range(depth)]
    wcache = nc.dram_tensor("wcache", [56, 128, 8192], BF16).ap()
    hcc = [nc.dram_tensor(f"hcc{j}", [256, H2], BF16) for j in range(8)]
    hgc = [nc.dram_tensor(f"hgc{j}", [512, H2], BF16) for j in range(8)]
    C = load_consts(P, cst)
    selt = ar.alloc(2, F32, "selt")
    S.add("sp", lambda g: g.dma_start(out=selt.ap, in_=sel), writes=[selt.buf], dma_sem=selt.buf)
    ccn = [0]

    def allgather(pairs):
        S.barrier()
        for (src, dst) in pairs:
            k = ccn[0]
            ccn[0] += 1
            op = S.add("pool", lambda g, src=src, dst=dst: g.collective_compute(
                "AllGather", ALU.bypass, replica_groups=PAIRS, ins=[src.ap().opt()], outs=[dst.ap().opt()]),
                cc_sem=P.cc_sems[0])
            op.sig = (P.cc_sems[0], k + 1)
        S.barrier()

    final = []
    for l in range(depth):
        def store_o(S_, hh, OT, l=l):
            ops = []
            for hf in range(2):
                dst = occ[l][hf][hh // 2].ap()[(hh % 2) * 128:(hh % 2 + 1) * 128, :]
                ops.append(S_.add("pool", lambda q, hf=hf, dst=dst: q.dma_start(out=dst, in_=OT.ap[:, hf * H2:(hf + 1) * H2]),
                                  reads=[OT.buf], dma_sem=OT.buf))
            return ops

        def load_hT(S_, g, h):
            rk, c0 = g // 4, (g % 4) * 512
            for j in range(8):
                S_.add("sp", lambda q, j=j: q.dma_start(
                    out=h.ap.rearrange("p (f t) -> p f t", f=16)[:, 2 * j:2 * j + 2, :],
                    in_=hgc[j].ap()[rk * 256:(rk + 1) * 256, c0:c0 + 512].rearrange("(f p) t -> p f t", p=128)),
                    writes=[h.buf], dma_sem=h.buf, batch=("hT", g))

        TA = dict(x=x, gmix=gmix[l], wqkv=wqkv[l], gcols=gcols[l], mstrip=mstrip, dstrip=dstrip,
                  sbmask=sbmask, esel=esel, qTs=qTs, kTs=kTs, vs=vs, store_o=store_o, load_hT=(None if l == 0 else load_hT))
        emit_A(P, C, TA)
        allgather([(occ[l][hf][q], ogc[l][hf][q]) for hf in range(2) for q in range(4)])

        def load_oT(S_, g, oTg, mT, l=l):
            for hf, dstt in ((0, oTg), (1, mT)):
                d3 = dstt.ap.rearrange("p (h t) -> p h t", h=16)
                for q in range(4):
                    for rk in range(2):
                        s0 = rk * 8 + 2 * q
                        S_.add("pool", lambda e, hf=hf, q=q, rk=rk, s0=s0, d3=d3: e.dma_start(
                            out=d3[:, s0:s0 + 2, :],
                            in_=ogc[l][hf][q].ap()[rk * 256:(rk + 1) * 256, g * 512:(g + 1) * 512].rearrange("(h p) t -> p h t", p=128)),
                            writes=[dstt.buf], dma_sem=dstt.buf, batch=("oT", l, g, hf))
            S_.add("dve", lambda v: v.tensor_scalar(out=oTg.ap, in0=oTg.ap, scalar1=selt.ap[:, 0:1], scalar2=None, op0=ALU.mult),
                   reads=[oTg.buf, selt.buf], writes=[oTg.buf])
            S_.add("dve", lambda v: v.scalar_tensor_tensor(out=oTg.ap, in0=mT.ap, scalar=selt.ap[:, 1:2], in1=oTg.ap,
                                                           op0=ALU.mult, op1=ALU.add),
                   reads=[oTg.buf, mT.buf, selt.buf], writes=[oTg.buf])

        def store_h(S_, g, hT):
            ops = []
            for j in range(8):
                ops.append(S_.add("pool", lambda q, j=j: q.dma_start(
                    out=hcc[j].ap()[:, g * 512:(g + 1) * 512].rearrange("(f p) t -> p f t", p=128),
                    in_=hT.ap.rearrange("p (f t) -> p f t", f=16)[:, 2 * j:2 * j + 2, :]),
                    reads=[hT.buf], dma_sem=hT.buf))
            return ops

        last = (l == depth - 1)
        TB = dict(x=(xh if l == 0 else x1h.ap()), gmix=gmix[l], gffn=gffn[l], wg=wg[l], wb=wb[l], wo=wo[l], wgu=wgu[l], wd=wd[l],
                  xo=(xo if last else x1h.ap()), load_oT=load_oT, slot=head_slot,
                  next_gmix=(None if last else gmix[l + 1]), store_h=store_h, wcache=wcache)
        outs = emit_B(P, C, TB)
        if last:
            final = outs
        else:
            allgather([(hcc[j], hgc[j]) for j in range(8)])
    S.final_waits = list(final)
    return P.finish()


def kernel(x, g_mix, w_in, q_gain, k_gain, w_branch, w_out, g_ffn, w_gu, w_down, rel_bias, _depth=DEPTH):
    f = lambda a: np.ascontiguousarray(np.asarray(a, dtype=np.float32))
    x, g_mix, w_in, q_gain, k_gain = f(x), f(g_mix), f(w_in), f(q_gain), f(k_gain)
    w_branch, w_out, g_ffn, w_gu, w_down, rel_bias = f(w_branch), f(w_out), f(g_ffn), f(w_gu), f(w_down), f(rel_bias)
    rep = lambda g: np.ascontiguousarray(np.broadcast_to(g[:, None, :], (DEPTH, 128, D_MODEL)))
    gm, gf = rep(g_mix), rep(g_ffn)
    wg = np.ascontiguousarray(w_in[:, :, 3 * D_MODEL:])
    per_hg = []
    for hg in range(2):
        lay = [host_A_inputs(x[0], g_mix[l], w_in[l], q_gain[l], k_gain[l], rel_bias, hg)[0] for l in range(DEPTH)]
        per_hg.append(dict(wqkv=np.stack([a["wqkv"] for a in lay]), gcols=np.stack([a["gcols"] for a in lay]),
                           mstrip=lay[0]["mstrip"], dstrip=lay[0]["dstrip"]))
    shared = dict(gmix=gm, gffn=gf, sbmask=host_sbmask(), esel=host_esel(), cst=host_consts(),
                  wg=wg, wb=w_branch, wo=w_out, wgu=w_gu, wd=w_down)
    in_maps = []
    for b in range(BATCH):
        for r in range(2):
            selv = np.zeros((128, 2), np.float32)
            selv[:, r] = 1.0
            m = dict(shared)
            m.update(per_hg[r])
            m.update(x=x[b], xh=np.ascontiguousarray(x[b, r * 2048:(r + 1) * 2048]), sel=selv)
            in_maps.append(m)
    nc = build_fused(_depth)
    res = run_bass_kernel_spmd(nc, in_maps, core_ids=list(range(8)))
    out = np.empty_like(x)
    for b in range(BATCH):
        for r in range(2):
            out[b, r * 2048:(r + 1) * 2048] = np.asarray(res.results[b * 2 + r]["xo"])
    return out
```
